# Optimizing a Trainium2 kernel written in Bass

```python
import jax, jax.numpy as jnp
from jax import lax
import numpy as np

D_MODEL = 2048
BATCH = 16
SEQ = 2048
DEPTH = 1
DEC_BATCH = 8
DEC_SEQ = 4096
PAST_LEN = 128

GRID_W = 64
POOL_WINDOWS = (2, 4, 8, 16)
POOL_GROUPS = len(POOL_WINDOWS)
POOL_WIDTH = D_MODEL // 2
POOL_GROUP_W = POOL_WIDTH // POOL_GROUPS
HEAD_DIM = 64
N_HEADS = (D_MODEL // 2) // HEAD_DIM
ATTN_WIDTH = N_HEADS * HEAD_DIM
WIN_R = 8
WIN_C = 16
IN_SPLITS = (POOL_WIDTH, POOL_WIDTH, ATTN_WIDTH, ATTN_WIDTH, ATTN_WIDTH, ATTN_WIDTH, D_MODEL, D_MODEL)
IN_WIDTH = sum(IN_SPLITS)
NORM_EPS = 1e-6

kernel_name = "gated_pool_natten_hybrid_encoder"


def _rms(x, g):
    xf = x.astype(jnp.float32)
    y = xf * lax.rsqrt(jnp.mean(xf * xf, axis=-1, keepdims=True) + NORM_EPS) * g.astype(jnp.float32)
    return y.astype(x.dtype)


def _pool_mixer(u, pool_w, pool_scale):
    B, S, _ = u.shape
    uf = u.astype(jnp.float32)
    cs = jnp.concatenate([jnp.zeros((B, 1, POOL_WIDTH), jnp.float32), jnp.cumsum(uf, axis=1)], axis=1)
    t = jnp.arange(S)
    outs = []
    for gi, w in enumerate(POOL_WINDOWS):
        lo = jnp.clip(t - w // 2, 0, S - 1)
        hi = jnp.clip(t + w // 2 - 1, 0, S - 1)
        sl = slice(gi * POOL_GROUP_W, (gi + 1) * POOL_GROUP_W)
        csg = cs[:, :, sl]
        cnt = (hi - lo + 1).astype(jnp.float32)[None, :, None]
        mean = (jnp.take(csg, hi + 1, axis=1) - jnp.take(csg, lo, axis=1)) / cnt
        outs.append(mean - uf[:, :, sl])
    d = jnp.stack(outs, axis=2).astype(u.dtype)
    y = jnp.einsum('bsgc,gcd->bsgd', d, pool_w).reshape(B, S, POOL_WIDTH)
    return y * pool_scale


def _neighbourhood_attention(q, k, v, rpb):
    B, S, H, hd = q.shape
    rows = S // GRID_W
    kr = min(WIN_R, rows)
    kc = WIN_C
    n_keys = kr * kc
    r = jnp.arange(rows)
    c = jnp.arange(GRID_W)
    r_start = jnp.clip(r - kr // 2, 0, rows - kr)
    c_start = jnp.clip(c - kc // 2, 0, GRID_W - kc)
    key_rows = r_start[:, None] + jnp.arange(kr)[None, :]
    key_cols = c_start[:, None] + jnp.arange(kc)[None, :]
    idx = (key_rows[:, None, :, None] * GRID_W + key_cols[None, :, None, :]).reshape(rows, GRID_W, n_keys)
    dr = key_rows - r[:, None] + (WIN_R - 1)
    dc = key_cols - c[:, None] + (WIN_C - 1)
    bias = rpb[:, dr[:, None, :, None], dc[None, :, None, :]]
    bias = bias.reshape(H, rows, GRID_W, n_keys).transpose(1, 0, 2, 3)
    q_rows = q.reshape(B, rows, GRID_W, H, hd).transpose(1, 0, 2, 3, 4)
    scale = HEAD_DIM ** -0.5

    def row_block(args):
        q_r, idx_r, bias_r = args
        k_win = jnp.take(k, idx_r, axis=1)
        v_win = jnp.take(v, idx_r, axis=1)
        s = jnp.einsum('bwhd,bwnhd->bhwn', q_r, k_win, preferred_element_type=jnp.float32)
        s = s * scale + bias_r[None].astype(jnp.float32)
        p = jax.nn.softmax(s, axis=-1).astype(v.dtype)
        return jnp.einsum('bhwn,bwnhd->bwhd', p, v_win)

    o = lax.map(row_block, (q_rows, idx, bias))
    return o.transpose(1, 0, 2, 3, 4).reshape(B, S, H * hd)


def _layer(x, c, w_ada, b_ada, norm_g, w_in, pool_w, pool_scale, q_norm_g, k_norm_g,
           rpb, w_pool_up, w_attn_up, w_o):
    B, S, D = x.shape
    ada = jax.nn.silu(c) @ w_ada + b_ada
    shift, scl, gate = jnp.split(ada, 3, axis=-1)
    h = _rms(x, norm_g) * (1 + scl[:, None, :]) + shift[:, None, :]
    proj = h @ w_in
    cuts = [int(v) for v in np.cumsum(IN_SPLITS)[:-1]]
    pool_u, pool_z, q, k, v, attn_z, g_pool, g_attn = jnp.split(proj, cuts, axis=-1)
    pool_y = _pool_mixer(pool_u, pool_w, pool_scale) * jax.nn.silu(pool_z)
    q = _rms(q.reshape(B, S, N_HEADS, HEAD_DIM), q_norm_g)
    k = _rms(k.reshape(B, S, N_HEADS, HEAD_DIM), k_norm_g)
    v = v.reshape(B, S, N_HEADS, HEAD_DIM)
    attn_y = _neighbourhood_attention(q, k, v, rpb) * jax.nn.silu(attn_z)
    merged = jax.nn.sigmoid(g_pool) * (pool_y @ w_pool_up) + jax.nn.sigmoid(g_attn) * (attn_y @ w_attn_up)
    return x + gate[:, None, :] * (merged @ w_o)


def setup_inputs(seed: int = 0) -> dict:
    key = jax.random.key(seed)
    ks = jax.random.split(key, 16)
    f32 = jnp.float32
    nrm = lambda k, shp, s: jax.random.normal(k, shp, f32) * s
    return {
        "x_prompt": nrm(ks[0], (BATCH, SEQ, D_MODEL), 1.0),
        "x_sample": nrm(ks[1], (DEC_BATCH, DEC_SEQ, D_MODEL), 1.0),
        "c_prompt": nrm(ks[2], (BATCH, D_MODEL), 1.0),
        "c_sample": nrm(ks[3], (DEC_BATCH, D_MODEL), 1.0),
        "w_ada": nrm(ks[4], (DEPTH, D_MODEL, 3 * D_MODEL), 0.5 * D_MODEL ** -0.5),
        "b_ada": nrm(ks[5], (DEPTH, 3 * D_MODEL), 0.01),
        "norm_g": 1.0 + nrm(ks[6], (DEPTH, D_MODEL), 0.02),
        "w_in": nrm(ks[7], (DEPTH, D_MODEL, IN_WIDTH), D_MODEL ** -0.5),
        "pool_w": nrm(ks[8], (DEPTH, POOL_GROUPS, POOL_GROUP_W, POOL_GROUP_W), POOL_GROUP_W ** -0.5),
        "pool_scale": 1.0 + nrm(ks[9], (DEPTH, POOL_WIDTH), 0.02),
        "q_norm_g": 1.0 + nrm(ks[10], (DEPTH, HEAD_DIM), 0.02),
        "k_norm_g": 1.0 + nrm(ks[11], (DEPTH, HEAD_DIM), 0.02),
        "rpb": nrm(ks[12], (DEPTH, N_HEADS, 2 * WIN_R - 1, 2 * WIN_C - 1), 0.02),
        "w_pool_up": nrm(ks[13], (DEPTH, POOL_WIDTH, D_MODEL), POOL_WIDTH ** -0.5),
        "w_attn_up": nrm(ks[14], (DEPTH, ATTN_WIDTH, D_MODEL), ATTN_WIDTH ** -0.5),
        "w_o": nrm(ks[15], (DEPTH, D_MODEL, D_MODEL), D_MODEL ** -0.5),
    }


def reference(x_prompt, x_sample, c_prompt, c_sample, w_ada, b_ada, norm_g, w_in, pool_w,
              pool_scale, q_norm_g, k_norm_g, rpb, w_pool_up, w_attn_up, w_o):
    def trunk(x, c):
        for l in range(DEPTH):
            x = _layer(x, c, w_ada[l], b_ada[l], norm_g[l], w_in[l], pool_w[l], pool_scale[l],
                       q_norm_g[l], k_norm_g[l], rpb[l], w_pool_up[l], w_attn_up[l], w_o[l])
        return x
    y_prompt = trunk(x_prompt, c_prompt)
    y_sample = trunk(x_sample, c_sample)
    return (y_prompt, y_sample)
```

```python
import numpy as np
import ml_dtypes
from contextlib import ExitStack
import concourse.bass as bass
import concourse.mybir as mybir
from concourse.bass_utils import run_bass_kernel_spmd

F32 = mybir.dt.float32
BF16 = mybir.dt.bfloat16
ALU = mybir.AluOpType
AF = mybir.ActivationFunctionType
AX = mybir.AxisListType

D = 2048
NTILE = 64
NBLK = 32
SEQ_T0 = (0, 16, 32)
SEQ_J = (16, 16, 32)
EPS = 1e-6
POOL_WINDOWS = (2, 4, 8, 16)
G_U, G_PZ, G_Q, G_K, G_V, G_AZ, G_GP, G_GA, G_WO, G_UP = 0, 2, 4, 6, 8, 10, 12, 16, 20, 24
NGRP = 28

ENGS = ("pe", "act", "dve", "pool", "sp")


class Buf:
    __slots__ = ("name", "w", "r")

    def __init__(self, name):
        self.name = name
        self.w = None
        self.r = []


class Prog:
    def __init__(self, nc):
        self.nc = nc
        self.ops = {e: [] for e in ENGS}
        self.cnt = {e: 0 for e in ENGS}
        self.seen = {e: {} for e in ENGS}
        self.dma_sems = {}

    def new_sem(self, name):
        self.dma_sems[name] = 0
        return name

    def _collect(self, eng, reads, writes, is_dma):
        seen = self.seen[eng]
        best = {}

        def add(ev):
            sk, val, e2 = ev
            if seen.get(sk, 0) >= val:
                return
            if best.get(sk, 0) < val:
                best[sk] = val

        for b in reads:
            if b.w is not None:
                if is_dma or not (b.w[2] == eng and eng == "pe"):
                    add(b.w)
        strict = is_dma or eng != "pe"
        for b in writes:
            if b.w is not None and (strict or b.w[2] != eng):
                add(b.w)
            for ev in b.r:
                if strict or ev[2] != eng:
                    add(ev)
        waits = []
        for sk, val in best.items():
            seen[sk] = val
            waits.append((sk, val))
        return waits

    def op(self, eng, fn, reads=(), writes=()):
        waits = self._collect(eng, reads, writes, False)
        self.cnt[eng] += 1
        ev = (eng, self.cnt[eng], eng)
        self.ops[eng].append((waits, fn, (eng, 1)))
        for b in reads:
            b.r.append(ev)
        for b in writes:
            b.w = ev
            b.r = []
        return ev

    def dma(self, eng, fn, semkey, reads=(), writes=()):
        waits = self._collect(eng, reads, writes, True)
        self.dma_sems[semkey] += 16
        ev = (semkey, self.dma_sems[semkey], "dma")
        self.ops[eng].append((waits, fn, (semkey, 16)))
        for b in reads:
            b.r.append(ev)
        for b in writes:
            b.w = ev
            b.r = []
        return ev

    def wait_all(self, eng, bufs):
        seen = self.seen[eng]
        best = {}
        for b in bufs:
            evs = list(b.r)
            if b.w is not None:
                evs.append(b.w)
            for (sk, val, e2) in evs:
                if seen.get(sk, 0) >= val:
                    continue
                if best.get(sk, 0) < val:
                    best[sk] = val
        waits = []
        for sk, val in best.items():
            seen[sk] = val
            waits.append((sk, val))
        self.ops[eng].append((waits, None, None))

    def finish(self, eng="sp"):
        waits = []
        for e in ENGS:
            if self.cnt[e] > 0:
                waits.append((e, self.cnt[e]))
        for k, v in self.dma_sems.items():
            if v > 0:
                waits.append((k, v))
        self.ops[eng].append((waits, None, None))

    def emit(self):
        nc = self.nc
        with ExitStack() as st:
            sems = {}
            for e in ENGS:
                sems[e] = st.enter_context(nc.semaphore("s_" + e))
            for k in self.dma_sems:
                sems[k] = st.enter_context(nc.semaphore("s_" + k))
            block = st.enter_context(nc.Block())
            handles = {"pe": block.tensor, "act": block.scalar, "dve": block.vector,
                       "pool": block.gpsimd, "sp": block.sync}
            for e in ENGS:
                ops = self.ops[e]
                if not ops:
                    continue

                def body(h, ops=ops):
                    for (waits, fn, inc) in ops:
                        for (sk, val) in waits:
                            h.wait_ge(sems[sk], val)
                        if fn is None:
                            continue
                        fn(h).then_inc(sems[inc[0]], inc[1])

                handles[e](body)


def bc_last(ap, n):
    return bass.AP(ap.tensor, ap.offset, [list(x) for x in ap.ap] + [[0, n]])


def seq_of_tile(gt):
    return 0 if gt < 16 else (1 if gt < 32 else 2)


def host_consts():
    bf = ml_dtypes.bfloat16
    ident = np.eye(128, dtype=np.float32)
    band = np.zeros((128, 20, 128), np.float64)
    for g, w in enumerate(POOL_WINDOWS):
        h = w // 2
        prev = band[:, g * 5 + 0, :]
        cur = band[:, g * 5 + 1, :]
        nxt = band[:, g * 5 + 2, :]
        first = band[:, g * 5 + 3, :]
        last = band[:, g * 5 + 4, :]
        for t in range(128):
            lo, hi = t - h, t + h - 1
            for tp in range(lo, hi + 1):
                if tp < 0:
                    prev[128 + tp, t] += 1.0 / w
                elif tp > 127:
                    nxt[tp - 128, t] += 1.0 / w
                else:
                    cur[tp, t] += 1.0 / w
            cur[t, t] -= 1.0
            lo2 = max(lo, 0)
            cnt = hi - lo2 + 1
            for tp in range(lo2, min(hi, 127) + 1):
                first[tp, t] += 1.0 / cnt
            first[t, t] -= 1.0
            hi2 = min(hi, 127)
            cnt = hi2 - lo + 1
            for tp in range(max(lo, 0), hi2 + 1):
                last[tp, t] += 1.0 / cnt
            last[t, t] -= 1.0
    maskr = np.zeros((128, 16, 64), np.float32)
    for p in range(128):
        krl, kc = p // 64, p % 64
        for cp in range(64):
            c = 63 - cp
            cs = min(max(c - 8, 0), 48)
            if cs <= kc < cs + 16:
                for i in range(16):
                    ok = (i <= 14) if krl == 0 else (i >= 1)
                    if ok:
                        maskr[p, i, cp] = 1.0
    return {
        "ident_bf": ident.astype(bf),
        "ident_f": ident,
        "band": band.astype(np.float32).astype(bf),
        "maskr": maskr.reshape(128, 1024),
    }


import os
KLEVEL = int(os.environ.get("KLEVEL", "99"))
KSUB = int(os.environ.get("KSUB", "99"))


class _Stop(Exception):
    pass


def build_program():
    nc = bass.Bass("TRN2", target_bir_lowering=False)
    P = Prog(nc)

    def din(name, shape, dt=F32):
        return nc.dram_tensor(name, list(shape), dt, kind="ExternalInput")

    x_d = din("x", [NTILE * 128, D])
    c_d = din("c3", [3, D])
    wada_d = din("w_ada", [D, 3 * D])
    bada_d = din("b_ada", [1, 3 * D])
    ng_d = din("norm_g", [16, 128])
    win_d = din("w_in", [D, 10240])
    pw_d = din("pool_w", [4, 256, 256])
    psc_d = din("pool_scale", [8, 128])
    qg_d = din("q_norm_g", [1, 64])
    kg_d = din("k_norm_g", [1, 64])
    rpb_d = din("rpb", [16, 15, 31])
    wpu_d = din("w_pool_up", [1024, D])
    wau_d = din("w_attn_up", [1024, D])
    wo_d = din("w_o", [D, D])
    identbf_d = din("ident_bf", [128, 128], BF16)
    identf_d = din("ident_f", [128, 128])
    band_d = din("band", [128, 20, 128], BF16)
    maskr_d = din("maskr", [128, 1024])
    y_d = nc.dram_tensor("y", [NTILE * 128, D], F32, kind="ExternalOutput")
    wsc_d = nc.dram_tensor("wsc", [NGRP, 128, 8192], BF16, kind="Internal")
    ada_dd = nc.dram_tensor("ada_s", [3, 3 * D], F32, kind="Internal")
    rp2_d = nc.dram_tensor("rp2", [16, 15, 160], F32, kind="Internal")
    rtab_d = nc.dram_tensor("rtab", [8, 128, 2048], BF16, kind="Internal")

    with ExitStack() as st:
        def sb(name, shape, dt):
            return st.enter_context(nc.sbuf_tensor("sb_" + name, list(shape), dt))

        hT = sb("hT", [128, 2, 16, 256], BF16)
        kT = sb("kT", [128, 3, 8, 256], BF16)
        vv = sb("vv", [128, 3, 2, 1024], BF16)
        uu = sb("uu", [128, 3, 2, 1024], BF16)
        wst = sb("wst", [128, 3, 16, 512], BF16)
        xt = sb("xt", [128, 2, 2048], F32)
        xnb = sb("xnb", [128, 2048], BF16)
        qT = sb("qT", [128, 2, 4, 256], BF16)
        sazT = sb("sazT", [128, 2, 4, 256], BF16)
        rt = sb("rt", [128, 2, 2, 16, 64], BF16)
        attn_y = sb("attn_y", [128, 8, 256], BF16)
        pool_y = sb("pool_y", [128, 8, 256], BF16)
        mT = sb("mT", [128, 16, 256], BF16)
        dT = sb("dT", [128, 8, 256], BF16)
        spzT = sb("spzT", [128, 8, 256], BF16)
        fa = sb("fa", [128, 3, 512], F32)
        fb = sb("fb", [128, 2, 512], F32)
        bfs = sb("bfs", [128, 3, 512], BF16)
        eb = sb("eb", [128, 2, 512], BF16)
        exb = sb("exb", [64, 2, 64], BF16)
        pt = sb("pt", [128, 6, 512], BF16)
        ptx = sb("ptx", [64, 6, 64], BF16)
        rd = sb("rd", [128, 2, 128], F32)
        wgt = sb("wgt", [128, 2, 128], F32)
        xr = sb("xr", [128, 2, 512], F32)
        gq = sb("gq", [128, 2, 512], F32)
        band = sb("band", [128, 20, 128], BF16)
        pw_sb = sb("pw_sb", [128, 4, 2, 256], BF16)
        gqbc = sb("gqbc", [128, 512], F32)
        gkbc = sb("gkbc", [128, 512], F32)
        ident = sb("ident", [128, 128], BF16)
        identf = sb("identf", [128, 128], F32)
        ones = sb("ones", [128, 64], BF16)
        g64 = sb("g64", [128, 2, 64], F32)
        gsT = sb("gsT", [128, 3, 16], F32)
        shT = sb("shT", [128, 3, 16], F32)
        pscT = sb("pscT", [128, 8], F32)
        ngT = sb("ngT", [128, 16], F32)
        ssx = sb("ssx", [128, 4], F32)
        ss8 = sb("ss8", [128, 4, 8], F32)
        nhalf = sb("nhalf", [128, 8], F32)
        sT = sb("sT", [128, 48], F32)
        small48 = gq[0:48, 0, 0:384].rearrange("p (a n) -> p a n", n=128)
        small16 = gq[0:16, 1, 0:256].rearrange("p (a n) -> p a n", n=128)
        adab = fb[0:3, :, :]
        badab = fa[0:3, 2, :]

        psb = [st.enter_context(nc.psum_tensor("ps%d" % i, [128, 512], F32)) for i in range(8)]
        trps = [psb[2].bitcast(BF16), psb[3].bitcast(BF16)]
        PJ = [0, 1]
        S_BANK = [4, 5]
        OD_BANK = [6, 7]

        B_ps = [Buf("ps%d" % i) for i in range(8)]
        B_trp = [B_ps[2], B_ps[3]]
        B_hT = [[Buf("hT%d_%d" % (s, t)) for t in range(2)] for s in range(2)]
        B_kT = [[Buf("kT%d_%d" % (s, t)) for t in range(2)] for s in range(3)]
        B_vv = [[Buf("vv%d_%d" % (s, t)) for t in range(2)] for s in range(3)]
        B_uu = [[Buf("uu%d_%d" % (s, t)) for t in range(2)] for s in range(3)]
        B_wst = [Buf("wst%d" % i) for i in range(3)]
        B_xt = [Buf("xt0"), Buf("xt1")]
        B_xnb = Buf("xnb")
        B_qT = [Buf("qT0"), Buf("qT1")]
        B_saz = [Buf("saz0"), Buf("saz1")]
        B_rt = [Buf("rt0"), Buf("rt1")]
        B_ay = [[Buf("ay%d_%d" % (c, t)) for t in range(2)] for c in range(8)]
        B_py = Buf("py")
        B_mT = [Buf("mT0"), Buf("mT1")]
        B_dT = Buf("dT")
        B_spz = Buf("spz")
        B_fa = [Buf("fa%d" % i) for i in range(3)]
        B_fb = [Buf("fb%d" % i) for i in range(2)]
        B_bfs = [Buf("bfs%d" % i) for i in range(3)]
        B_eb = [Buf("eb0"), Buf("eb1")]
        B_exb = [Buf("exb0"), Buf("exb1")]
        B_pt = [Buf("pt%d" % i) for i in range(6)]
        B_ptx = [Buf("ptx%d" % i) for i in range(6)]
        B_rd = [Buf("rd0"), Buf("rd1")]
        B_wgt = [Buf("wgt0"), Buf("wgt1")]
        B_xr = [Buf("xr0"), Buf("xr1")]
        B_gq = [Buf("gq0"), Buf("gq1")]
        B_const = Buf("const")
        C_ident = Buf("c_ident"); C_identf = Buf("c_identf"); C_band = Buf("c_band"); C_g64 = Buf("c_g64")
        C_ones = Buf("c_ones"); C_nhalf = Buf("c_nhalf"); C_gq = Buf("c_gqbc"); C_gk = Buf("c_gkbc")
        C_pw = Buf("c_pw"); C_ng = Buf("c_ng"); C_psc = Buf("c_psc"); C_gs = Buf("c_gs"); C_sh = Buf("c_sh")
        C_sT = Buf("c_sT"); C_s48 = [Buf("c_s48_%d" % i) for i in range(3)]; C_s16 = [Buf("c_s16_0"), Buf("c_s16_1")]
        C_badab = Buf("c_badab"); C_adab = [Buf("c_adab0"), Buf("c_adab1")]
        B_yo = [Buf("yo0"), Buf("yo1")]
        B_ss = Buf("ss")
        B_ss8 = [Buf("ss8_%d" % i) for i in range(4)]
        B_wsc = [Buf("wsc%d" % g) for g in range(NGRP)]
        B_ada = Buf("ada")
        B_adad = Buf("adad")
        B_rp2 = Buf("rp2")
        B_rtab = Buf("rtab")
        B_y = Buf("y")
        B_misc = Buf("misc")

        ctr = {"fa": 0, "fb": 0, "bfs": 0, "pj": 0, "trp": 0, "eb": 0, "pt": 0, "rd": 0,
               "xr": 0, "gq": 0, "S": 0, "OD": 0, "wst": 0, "ss8": 0, "xt": 0, "rt": 0}

        def rot(name, n):
            i = ctr[name] % n
            ctr[name] += 1
            return i

        S_wst = [P.new_sem("wst%d" % i) for i in range(3)]
        S_xt = [P.new_sem("xt%d" % i) for i in range(2)]
        S_xr = [P.new_sem("xr%d" % i) for i in range(2)]
        S_gq = [P.new_sem("gq%d" % i) for i in range(2)]
        S_rt = [P.new_sem("rt%d" % i) for i in range(2)]
        S_yo = [P.new_sem("yo0"), P.new_sem("yo1")]

        def MM(out, lhsT, rhs, start, stop, r, w, tp=None):
            if tp is None:
                P.op("pe", lambda e: e.matmul(out, lhsT=lhsT, rhs=rhs, start=start, stop=stop), r, w)
            else:
                P.op("pe", lambda e: e.matmul(out, lhsT=lhsT, rhs=rhs, start=start, stop=stop,
                                              tile_position=tp), r, w)

        def TR(out, in_, idn, r, w):
            P.op("pe", lambda e: e.transpose(out, in_, idn), r, w)

        def ACT(out, in_, func, r, w, scale=1.0, bias=None, accum=None):
            kw = {}
            if bias is not None:
                kw["bias"] = bias
            if accum is not None:
                kw["accum_out"] = accum
            P.op("act", lambda e: e.activation(out=out, in_=in_, func=func, scale=scale, **kw), r, w)

        def TS(eng, out, in0, s1, s2, op0, op1, r, w):
            if s2 is None:
                P.op(eng, lambda e: e.tensor_scalar(out=out, in0=in0, scalar1=s1, scalar2=None, op0=op0), r, w)
            else:
                P.op(eng, lambda e: e.tensor_scalar(out=out, in0=in0, scalar1=s1, scalar2=s2, op0=op0, op1=op1), r, w)

        def TT(eng, out, in0, in1, op, r, w):
            P.op(eng, lambda e: e.tensor_tensor(out=out, in0=in0, in1=in1, op=op), r, w)

        def STT(eng, out, in0, scalar, in1, op0, op1, r, w):
            P.op(eng, lambda e: e.scalar_tensor_tensor(out=out, in0=in0, scalar=scalar, in1=in1, op0=op0, op1=op1), r, w)

        def CP(eng, out, in_, r, w):
            P.op(eng, lambda e: e.tensor_copy(out=out, in_=in_), r, w)

        def MSET(eng, ap, val, r, w):
            P.op(eng, lambda e: e.memset(ap, val), r, w)

        nsem = [0]

        def fresh():
            nsem[0] += 1
            return P.new_sem("one%d" % nsem[0])

        def DMAn(q, out, in_, r, w, sem=None):
            if sem is None:
                sem = fresh()
            return P.dma(q, lambda e: e.dma_start(out=out, in_=in_), sem, r, w)

        def DMAG(items):
            sem = fresh()
            ev = None
            ws = []
            for (q, out, in_, r, w) in items:
                ev = DMAn(q, out, in_, r, w, sem=sem)
                ws.extend(w)
            for b in ws:
                b.w = ev

        def DMA(q, out, in_, sem, r, w, slow=False):
            if slow:
                P.dma(q, lambda e: e.dma_start(out=out, in_=in_, allow_slow_non_contiguous=True), sem, r, w)
            else:
                P.dma(q, lambda e: e.dma_start(out=out, in_=in_), sem, r, w)

        def body():
            cast_gate = []

            def cast_group(g, src_ap_2d, kchunks, dst_k0, sem):
                src = src_ap_2d.rearrange("(kc p) n -> p kc n", p=128)
                dst = wsc_d.ap()[g].rearrange("p (kc n) -> p kc n", n=512)
                step = 8
                ev = None
                for k0 in range(0, kchunks, step):
                    ev = DMAn("pool", dst[:, dst_k0 + k0:dst_k0 + k0 + step, :], src[:, k0:k0 + step, :], list(cast_gate), [Buf("dmy")], sem=sem)
                return ev

            def win_cols(g):
                return win_d.ap()[:, g * 512:(g + 1) * 512]

            def cast_phase(groups):
                sem = fresh()
                ev = None
                for g in groups:
                    ev = cast_group(g, win_cols(g), 16, 0, sem)
                for g in groups:
                    B_wsc[g].w = ev

            cast_phase([G_K, G_K + 1])

            def late_casts():
                cast_phase([G_V, G_V + 1, G_U, G_U + 1])
                cast_phase([G_Q, G_AZ])
                cast_phase([G_Q + 1, G_AZ + 1])
                cast_phase([G_PZ, G_PZ + 1])
                for mg in range(4):
                    sem = fresh()
                    cast_group(G_GP + mg, win_cols(G_GP + mg), 16, 0, sem)
                    cast_group(G_UP + mg, wpu_d.ap()[:, mg * 512:(mg + 1) * 512], 8, 0, sem)
                    cast_group(G_UP + mg, wau_d.ap()[:, mg * 512:(mg + 1) * 512], 8, 8, sem)
                    ev = cast_group(G_GA + mg, win_cols(G_GA + mg), 16, 0, sem)
                    for g in (G_GP + mg, G_UP + mg, G_GA + mg):
                        B_wsc[g].w = ev
                sem = fresh()
                ev = None
                for ngp in range(4):
                    ev = cast_group(G_WO + ngp, wo_d.ap()[:, ngp * 512:(ngp + 1) * 512], 16, 0, sem)
                for ngp in range(4):
                    B_wsc[G_WO + ngp].w = ev
            DMAn("pool", pw_sb[:], pw_d.ap().rearrange("g (ic p) n -> p g ic n", p=128), [], [C_pw])

            if KLEVEL < 1:
                raise _Stop
            DMAG([("sp", ident[:], identbf_d.ap(), [], [C_ident]),
                  ("sp", identf[:], identf_d.ap(), [], [C_identf]),
                  ("sp", band[:], band_d.ap(), [], [C_band]),
                  ("sp", g64[:, 0, :], bass.AP(qg_d, 0, [[0, 128], [1, 64]]), [], [C_g64]),
                  ("sp", g64[:, 1, :], bass.AP(kg_d, 0, [[0, 128], [1, 64]]), [], [C_g64])])
            MSET("dve", ones[:], 1.0, [], [C_ones])
            MSET("dve", nhalf[:], -0.5, [], [C_nhalf])
            a0 = g64[:, 0, :]
            a1 = g64[:, 1, :]
            TS("dve", gqbc[:].rearrange("p (h d) -> p h d", d=64),
               bass.AP(a0.tensor, a0.offset, [list(a0.ap[0]), [0, 8], [1, 64]]), 1.0, None, ALU.mult, None, [C_g64], [C_gq])
            TS("dve", gkbc[:].rearrange("p (h d) -> p h d", d=64),
               bass.AP(a1.tensor, a1.offset, [list(a1.ap[0]), [0, 8], [1, 64]]), 8.0, None, ALU.mult, None, [C_g64], [C_gk])

            if KLEVEL < 2:
                raise _Stop
            DMAG([("sp", small48[:, 0, :], c_d.ap().rearrange("b (kc p) -> (b kc) p", p=128), [], [C_s48[0]]),
                  ("sp", small16[:, 0, :], ng_d.ap(), [], [C_s16[0]]),
                  ("sp", small16[0:8, 1, :], psc_d.ap(), [], [C_s16[1]])])
            pT = psb[0]
            TR(pT[:, 0:48], small48[:, 0, :], identf[0:48, 0:48], [C_s48[0], C_identf], [B_ps[0]])
            TR(pT[:, 48:64], small16[:, 0, :], identf[0:16, 0:16], [C_s16[0], C_identf], [B_ps[0]])
            TR(pT[:, 64:72], small16[0:8, 1, :], identf[0:8, 0:8], [C_s16[1], C_identf], [B_ps[0]])
            cTt = fa[:, 0, 0:48]
            CP("dve", cTt, pT[:, 0:48], [B_ps[0]], [B_fa[0]])
            CP("dve", ngT[:], pT[:, 48:64], [B_ps[0]], [C_ng])
            CP("dve", pscT[:], pT[:, 64:72], [B_ps[0]], [C_psc])
            tht = fa[:, 1, 0:48]
            ACT(tht, cTt, AF.Tanh, [B_fa[0]], [B_fa[1]], scale=0.5)
            STT("dve", sT[:], tht, 1.0, cTt, ALU.add, ALU.mult, [B_fa[0], B_fa[1]], [C_sT])

            if KLEVEL < 3:
                raise _Stop
            wst_f = [wst[:, i].rearrange("p a n -> p (a n)").bitcast(F32) for i in range(3)]
            sT3 = sT[:].rearrange("p (b k) -> p b k", k=16)
            B_adad_p = []
            S_badab = fresh()
            S_adst = [fresh(), fresh()]
            for half in range(2):
                banks = [0, 1, 2, 4, 5, 6]
                for kc in range(16):
                    sl = rot("wst", 3)
                    DMA("sp", wst_f[sl][:, 0:3072], wada_d.ap()[kc * 128:(kc + 1) * 128, half * 3072:(half + 1) * 3072],
                        S_wst[sl], [], [B_wst[sl]])
                    for n in range(6):
                        MM(psb[banks[n]][0:3, :], sT3[:, :, kc], wst_f[sl][:, n * 512:(n + 1) * 512], kc == 0, kc == 15,
                           [C_sT, B_wst[sl]], [B_ps[banks[n]]])
                for n in range(6):
                    col = half * 3072 + n * 512
                    ab = n % 2
                    DMAn("sp", badab, bass.AP(bada_d, col, [[0, 3], [1, 512]]), [], [C_badab], sem=S_badab)
                    STT("dve", adab[:, ab, :], psb[banks[n]][0:3, :], 0.5, badab, ALU.mult, ALU.add,
                        [B_ps[banks[n]], C_badab], [C_adab[ab]])
                    bp = Buf("adad%d" % col)
                    B_adad_p.append(bp)
                    DMAn("sp", ada_dd.ap()[:, col:col + 512], adab[:, ab, :], [C_adab[ab]], [bp], sem=S_adst[ab])
            gate_b = Buf("castgate")
            gate_b.w = B_wst[(ctr["wst"] - 1) % 3].w
            cast_gate.append(gate_b)
            late_casts()
            del cast_gate[:]
            for bp in B_adad_p:
                sk = bp.w[0]
                bp.w = (sk, P.dma_sems[sk], "dma")
            DMAG([("sp", small48[b * 16:(b + 1) * 16, 1, :], bass.AP(ada_dd, b * 3 * D, [[128, 16], [1, 128]]), B_adad_p, [C_s48[1]])
                  for b in range(3)])
            DMAG([("sp", small48[b * 16:(b + 1) * 16, 2, :], bass.AP(ada_dd, b * 3 * D + D, [[128, 16], [1, 128]]), B_adad_p, [C_s48[2]])
                  for b in range(3)])
            TR(pT[:, 0:48], small48[:, 1, :], identf[0:48, 0:48], [C_s48[1], C_identf], [B_ps[0]])
            TR(pT[:, 48:96], small48[:, 2, :], identf[0:48, 0:48], [C_s48[2], C_identf], [B_ps[0]])
            CP("dve", shT[:].rearrange("p b k -> p (b k)"), pT[:, 0:48], [B_ps[0]], [C_sh])
            for b in range(3):
                STT("dve", gsT[:, b, :], pT[:, 48 + 16 * b:64 + 16 * b], 1.0, ngT[:], ALU.add, ALU.mult,
                    [B_ps[0], C_ng], [C_gs])
            B_adad = Buf("adad_all")
            for bp in B_adad_p:
                if bp.w is not None:
                    B_adad.r.append(bp.w)

            B_rtab_h = [Buf("rtab%d" % h) for h in range(16)]

            def build_rtable():
                if KLEVEL < 5:
                    raise _Stop
                xt_flat = xt[:].rearrange("p a n -> p (a n)")
                rpt = xt_flat[0:16, 0:2400].rearrange("p (a b) -> p a b", b=160)
                rtmp = xt_flat[0:16, 2400:2400 + 465].rearrange("p (a b) -> p a b", b=31)
                MSET("pool", rpt, 0.0, [], [B_xt[0], B_xt[1]])
                DMAn("sp", rtmp, rpb_d.ap(), [], [B_xt[0], B_xt[1]])
                CP("pool", rpt[:, :, 64:95], rtmp[:, ::-1, :], [B_xt[0], B_xt[1]], [B_xt[0], B_xt[1]])
                DMAn("sp", rp2_d.ap(), rpt, [B_xt[0], B_xt[1]], [B_rp2])
                mT_f = mT[:].rearrange("p a b -> p (a b)").bitcast(F32)
                ay_f = attn_y[:].rearrange("p a b -> p (a b)").bitcast(F32)
                py_f = pool_y[:].rearrange("p a b -> p (a b)").bitcast(F32)
                qT_f = qT[:].rearrange("p a b c -> p (a b c)").bitcast(F32)
                dT_fl = dT[:].rearrange("p a b -> p (a b)")
                sp_fl = spzT[:].rearrange("p a b -> p (a b)")
                maskt = qT_f[:, 0:1024]
                DMAn("sp", maskt, maskr_d.ap(), [], [B_qT[0], B_qT[1]])
                MSET("pool", mT_f[:, 0:2048], 0.0, [], [B_mT[0], B_mT[1]])
                S_rb = [fresh(), fresh()]
                B_est = [Buf("est0"), B_py]
                B_rb = [B_dT, B_spz]
                S_stg = [P.new_sem("stg0"), P.new_sem("stg1")]
                ests = [ay_f[:, 0:1024], py_f[:, 0:1024]]
                rbufs = [dT_fl[:, 0:1024], sp_fl[:, 0:1024]]
                for h in range(16):
                    s_ = h % 2
                    stg = mT_f[:, s_ * 1024:(s_ + 1) * 1024]
                    stg3 = stg.rearrange("p (i c) -> p i c", c=64)
                    src = bass.AP(rp2_d, h * 15 * 160 + 16, [[1, 64], [160, 15], [1, 64]])
                    DMA("sp", stg3[0:64, 0:15, :], src, S_stg[s_], [B_rp2], [B_mT[s_]])
                    dmy = Buf("dmy")
                    ev_ = P.dma("sp", (lambda o_, i_: (lambda e: e.dma_start(out=o_, in_=i_)))(stg3[64:128, 1:16, :], src),
                                S_stg[s_], [B_rp2], [dmy])
                    B_mT[s_].w = ev_
                    ACT(ests[s_], stg, AF.Exp, [B_mT[s_]], [B_est[s_]])
                    TT("dve", rbufs[s_], ests[s_], maskt, ALU.mult, [B_est[s_], B_qT[0], B_qT[1]], [B_rb[s_]])
                    DMAn("sp", rtab_d.ap()[h // 2][:, (h % 2) * 1024:(h % 2 + 1) * 1024], rbufs[s_], [B_rb[s_]], [B_rtab_h[h]], sem=S_rb[s_])
                for h in range(16):
                    B_rtab_h[h].w = (S_rb[h % 2], P.dma_sems[S_rb[h % 2]], "dma")
                for c_ in range(8):
                    for t_ in range(2):
                        B_ay[c_][t_].r.extend(B_est[0].r)
                        if B_est[0].w is not None:
                            B_ay[c_][t_].r.append(B_est[0].w)

            for cb in C_s48 + C_s16:
                for gb_ in B_gq:
                    gb_.r.extend(cb.r)
                    if cb.w is not None:
                        gb_.r.append(cb.w)
            def load_w(g):
                sl = rot("wst", 3)
                DMA("sp", wst[:, sl].rearrange("p a n -> p (a n)"), wsc_d.ap()[g], S_wst[sl], [B_wsc[g]], [B_wst[sl]])
                return sl

            def proj(hslot, lt, sl, nk=16, k0=0, lhs_fn=None, lhs_bufs=None):
                flush()
                gen[0] += 1
                bk = PJ[rot("pj", 2)]
                for kc in range(nk):
                    if lhs_fn is None:
                        lhsT = hT[:, hslot, kc, lt * 128:(lt + 1) * 128]
                        rb = [B_hT[hslot][lt]]
                    else:
                        lhsT = lhs_fn(kc)
                        rb = lhs_bufs
                    MM(psb[bk][:], lhsT, wst[:, sl, k0 + kc, :], kc == 0, kc == nk - 1, rb + [B_wst[sl]], [B_ps[bk]])
                return bk

            deferred = []
            gen = [0]

            def flush(all_=False):
                keep = []
                for (g_, f_) in deferred:
                    if all_ or g_ <= gen[0] - 2:
                        f_()
                    else:
                        keep.append((g_, f_))
                deferred[:] = keep

            def transpose4(src_bf, src_buf, dst_ap, dst_bufs, eng_i):
                deferred.append((gen[0], lambda: transpose4_now(src_bf, src_buf, dst_ap, dst_bufs, eng_i)))

            def transpose4_now(src_bf, src_buf, dst_ap, dst_bufs, eng_i):
                th = rot("trp", 2)
                trp = trps[th]
                for c in range(4):
                    TR(trp[:, c * 128:(c + 1) * 128], src_bf[:, c * 128:(c + 1) * 128], ident[:],
                       [src_buf, C_ident], [B_trp[th]])
                src = trp[:, 0:512].rearrange("p (c t) -> p c t", t=128)
                if eng_i % 2 == 0:
                    ACT(dst_ap, src, AF.Copy, [B_trp[th]], dst_bufs)
                else:
                    CP("dve", dst_ap, src, [B_trp[th]], dst_bufs)

            def headnorm(bk, gbc, gbuf, dst_ap, dst_bufs, eng_i):
                ia = rot("fa", 3)
                ib = rot("fb", 2)
                ic = rot("bfs", 3)
                i8 = rot("ss8", 4)
                ACT(fa[:, ia, :], psb[bk][:], AF.Copy, [B_ps[bk]], [B_fa[ia]])
                ACT(fb[:, ib, :], fa[:, ia, :], AF.Square, [B_fa[ia]], [B_fb[ib]])
                P.op("dve", lambda e: e.tensor_reduce(out=ss8[:, i8, :], in_=fb[:, ib, :].rearrange("p (h d) -> p h d", d=64),
                                                      axis=AX.X, op=ALU.add), [B_fb[ib]], [B_ss8[i8]])
                TS("dve", ss8[:, i8, :], ss8[:, i8, :], 64.0 * EPS, None, ALU.add, None, [B_ss8[i8]], [B_ss8[i8]])
                TT("pool", ss8[:, i8, :], ss8[:, i8, :], nhalf[:], ALU.pow, [B_ss8[i8], C_nhalf], [B_ss8[i8]])
                TT("dve", fb[:, ib, :].rearrange("p (h d) -> p h d", d=64),
                   fa[:, ia, :].rearrange("p (h d) -> p h d", d=64), bc_last(ss8[:, i8, :], 64), ALU.mult,
                   [B_fa[ia], B_ss8[i8]], [B_fb[ib]])
                TT("pool", bfs[:, ic, :], fb[:, ib, :], gbc[:], ALU.mult, [B_fb[ib], gbuf], [B_bfs[ic]])
                transpose4(bfs[:, ic, :], B_bfs[ic], dst_ap, dst_bufs, eng_i)

            def silu2(bk, dst_ap, dst_bufs, eng_i):
                ia = rot("fa", 3)
                ic = rot("bfs", 3)
                ACT(fa[:, ia, :], psb[bk][:], AF.Tanh, [B_ps[bk]], [B_fa[ia]], scale=0.5)
                STT("dve", bfs[:, ic, :], fa[:, ia, :], 1.0, psb[bk][:], ALU.add, ALU.mult, [B_fa[ia], B_ps[bk]], [B_bfs[ic]])
                transpose4(bfs[:, ic, :], B_bfs[ic], dst_ap, dst_bufs, eng_i)

            def rslot_of(gt):
                return (gt // 2) % 3

            xt_of = {}

            def frontend_load(gb):
                for lt in range(2):
                    gt = gb * 2 + lt
                    xi = rot("xt", 2)
                    xt_of[gt] = xi
                    DMA("sp", xt[:, xi, :], x_d.ap()[gt * 128:(gt + 1) * 128, :], S_xt[xi], [], [B_xt[xi]])

            def fe_prep(gb, lt):
                gt = gb * 2 + lt
                xi = xt_of[gt]
                col = gt % 4
                MSET("dve", ssx[:, col:col + 1], 0.0, [], [B_ss])
                ACT(xnb[:], xt[:, xi, :], AF.Square, [B_xt[xi], B_ss], [B_xnb, B_ss], accum=ssx[:, col:col + 1])
                TS("dve", ssx[:, col:col + 1], ssx[:, col:col + 1], 1.0 / D, EPS, ALU.mult, ALU.add, [B_ss], [B_ss])
                TT("pool", ssx[:, col:col + 1], ssx[:, col:col + 1], nhalf[:, 0:1], ALU.pow, [B_ss, C_nhalf], [B_ss])
                ACT(xnb[:], xt[:, xi, :], AF.Copy, [B_xt[xi], B_ss], [B_xnb], scale=ssx[:, col:col + 1])

            def fe_round(gb, lt, q4):
                hs = gb % 2
                gt = gb * 2 + lt
                sq = seq_of_tile(gt)
                th = rot("trp", 2)
                for c in range(4):
                    kc = q4 * 4 + c
                    TR(trps[th][:, c * 128:(c + 1) * 128], xnb[:, kc * 128:(kc + 1) * 128], ident[:],
                       [B_xnb, C_ident], [B_trp[th]])
                for c in range(4):
                    kc = q4 * 4 + c
                    src = trps[th][:, c * 128:(c + 1) * 128]
                    dst = hT[:, hs, kc, lt * 128:(lt + 1) * 128]
                    if c % 2 == 0:
                        ACT(dst, src, AF.Identity, [B_trp[th], C_gs, C_sh], [B_hT[hs][lt]],
                            scale=gsT[:, sq, kc:kc + 1], bias=shT[:, sq, kc:kc + 1])
                    else:
                        TS("dve", dst, src, gsT[:, sq, kc:kc + 1], shT[:, sq, kc:kc + 1], ALU.mult, ALU.add,
                           [B_trp[th], C_gs, C_sh], [B_hT[hs][lt]])

            def frontend_tile(gb, lt):
                fe_prep(gb, lt)
                for q4 in range(4):
                    fe_round(gb, lt, q4)

            def ahead_proj(gb):
                hs = gb % 2
                rs = gb % 3
                for kg in range(2):
                    sl = load_w(G_K + kg)
                    for lt in range(2):
                        bk = proj(hs, lt, sl)
                        dst = kT[:, rs, 4 * kg:4 * kg + 4, lt * 128:(lt + 1) * 128]
                        headnorm(bk, gkbc, C_gk, dst, [B_kT[rs][lt]], lt)
                for (gbase, ring, Bring) in ((G_V, vv, B_vv), (G_U, uu, B_uu)):
                    for g2 in range(2):
                        sl = load_w(gbase + g2)
                        for lt in range(2):
                            bk = proj(hs, lt, sl)
                            dst = ring[:, rs, lt, g2 * 512:(g2 + 1) * 512]
                            if lt == 0:
                                ACT(dst, psb[bk][:], AF.Copy, [B_ps[bk]], [Bring[rs][lt]])
                            else:
                                CP("dve", dst, psb[bk][:], [B_ps[bk]], [Bring[rs][lt]])

            S4 = [0, 1, 2, 3, 4, 5]

            def att_qk(lb, sq, hp, lt, ri):
                qb, c4 = hp // 4, hp % 4
                gt = lb * 2 + lt
                j = gt - SEQ_T0[sq]
                J = SEQ_J[sq]
                if j == 0:
                    olist, i0, interior = [3, 2, 1, 0], 1, False
                elif j == 1:
                    olist, i0, interior = [2, 1, 0, -1], 3, False
                elif j == J - 2:
                    olist, i0, interior = [1, 0, -1, -2], 5, False
                elif j == J - 1:
                    olist, i0, interior = [0, -1, -2, -3], 7, False
                else:
                    olist, i0, interior = [1, 0, -1, -2], 5, True
                odb = OD_BANK[rot("OD", 2)]
                OD = psb[odb]
                sbis = [S4[rot("S", 6)], S4[rot("S", 6)]]
                for a, o in enumerate(olist):
                    gk = gt + o
                    krs, klt = rslot_of(gk), gk % 2
                    for hh in range(2):
                        p0 = 64 * hh
                        MM(psb[sbis[hh]][:, a * 128:(a + 1) * 128], kT[p0:p0 + 64, krs, hp, klt * 128:(klt + 1) * 128],
                           qT[p0:p0 + 64, qb, c4, lt * 128:(lt + 1) * 128], True, True,
                           [B_kT[krs][klt], B_qT[qb]], [B_ps[sbis[hh]]], tp=(p0, 0))
                if interior:
                    gk2 = gt + 2
                    krs2, klt2 = rslot_of(gk2), gk2 % 2
                    for hh in range(2):
                        p0 = 64 * hh
                        MM(psb[sbis[hh]][0:64, 448:512], kT[p0:p0 + 64, krs2, hp, klt2 * 128:klt2 * 128 + 64],
                           qT[p0:p0 + 64, qb, c4, lt * 128 + 64:(lt + 1) * 128], True, True,
                           [B_kT[krs2][klt2], B_qT[qb]], [B_ps[sbis[hh]]], tp=(p0, 0))
                pts = []
                for hh in range(2):
                    h = 2 * hp + hh
                    p0 = 64 * hh
                    sbi = sbis[hh]
                    S = psb[sbi]
                    ei = rot("eb", 2)
                    ACT(eb[:, ei, :], S[:], AF.Exp, [B_ps[sbi]], [B_eb[ei]])
                    pi = rot("pt", 6)
                    TT("dve", pt[:, pi, :].rearrange("p (i c) -> p i c", c=64),
                       eb[:, ei, :].rearrange("p (i c) -> p i c", c=64),
                       rt[:, ri, hh, i0:i0 + 8, ::-1], ALU.mult, [B_eb[ei], B_rt[ri]], [B_pt[pi]])
                    if interior:
                        TT("dve", ptx[:, pi, :], eb[0:64, ei, 448:512], rt[0:64, ri, hh, 4, ::-1], ALU.mult,
                           [B_eb[ei], B_rt[ri]], [B_ptx[pi]])
                        MSET("dve", pt[0:64, pi, 448:512], 0.0, [], [B_pt[pi]])
                    pts.append((pi, hh, h, p0))
                return (gt, hp, lt, qb, c4, olist, interior, odb, pts)

            def att_pv(st_):
                (gt, hp, lt, qb, c4, olist, interior, odb, pts) = st_
                OD = psb[odb]
                n = len(olist)
                for (pi, hh, h, p0) in pts:
                    for a, o in enumerate(olist):
                        gk = gt + o
                        krs, klt = rslot_of(gk), gk % 2
                        last = (a == n - 1) and not interior
                        MM(OD[p0:p0 + 64, 0:128], vv[:, krs, klt, h * 64:(h + 1) * 64], pt[:, pi, a * 128:(a + 1) * 128],
                           a == 0, last, [B_vv[krs][klt], B_pt[pi]], [B_ps[odb]], tp=(0, p0))
                    if interior:
                        gk = gt + 2
                        krs, klt = rslot_of(gk), gk % 2
                        MM(OD[p0:p0 + 64, 64:128], vv[0:64, krs, klt, h * 64:(h + 1) * 64], ptx[:, pi, :],
                           False, True, [B_vv[krs][klt], B_ptx[pi]], [B_ps[odb]], tp=(0, p0))
                    for a, o in enumerate(olist):
                        last = (a == n - 1) and not interior
                        MM(OD[p0:p0 + 64, 128:256], ones[:, 0:64], pt[:, pi, a * 128:(a + 1) * 128],
                           a == 0, last, [C_ones, B_pt[pi]], [B_ps[odb]], tp=(0, p0))
                    if interior:
                        MM(OD[p0:p0 + 64, 192:256], ones[0:64, 0:64], ptx[:, pi, :],
                           False, True, [C_ones, B_ptx[pi]], [B_ps[odb]], tp=(0, p0))
                di = rot("rd", 2)
                P.op("dve", lambda e: e.reciprocal(out=rd[:, di, :], in_=OD[:, 128:256]), [B_ps[odb]], [B_rd[di]])
                TT("dve", wgt[:, di, :], rd[:, di, :], sazT[:, qb, c4, lt * 128:(lt + 1) * 128], ALU.mult,
                   [B_rd[di], B_saz[qb]], [B_wgt[di]])
                TT("dve", attn_y[:, hp, lt * 128:(lt + 1) * 128], OD[:, 0:128], wgt[:, di, :], ALU.mult,
                   [B_ps[odb], B_wgt[di]], [B_ay[hp][lt]])

            def lagged(lb):
                hs = lb % 2
                sq = seq_of_tile(lb * 2)
                nb = lb + 2
                for qg in range(2):
                    qb = qg
                    sl = load_w(G_Q + qg)
                    for lt in range(2):
                        bk = proj(hs, lt, sl)
                        headnorm(bk, gqbc, C_gq, qT[:, qb, :, lt * 128:(lt + 1) * 128], [B_qT[qb]], lt)
                    sl = load_w(G_AZ + qg)
                    for lt in range(2):
                        bk = proj(hs, lt, sl)
                        silu2(bk, sazT[:, qb, :, lt * 128:(lt + 1) * 128], [B_saz[qb]], lt + 1)
                flush(True)
                pend = []
                for hp in range(8):
                    ri = rot("rt", 2)
                    DMA("pool", rt[:, ri].rearrange("p a i c -> p (a i c)"), rtab_d.ap()[hp], S_rt[ri],
                        [B_rtab_h[2 * hp], B_rtab_h[2 * hp + 1]], [B_rt[ri]])
                    for lt in range(2):
                        pend.append(att_qk(lb, sq, hp, lt, ri))
                        if len(pend) > 2:
                            att_pv(pend.pop(0))
                while pend:
                    att_pv(pend.pop(0))
                for pg in range(2):
                    sl = load_w(G_PZ + pg)
                    for lt in range(2):
                        bk = proj(hs, lt, sl)
                        silu2(bk, spzT[:, 4 * pg:4 * pg + 4, lt * 128:(lt + 1) * 128], [B_spz], lt)
                flush(True)
                for ch in range(8):
                    g = ch // 2
                    bk = PJ[rot("pj", 2)]
                    for lt in range(2):
                        gt = lb * 2 + lt
                        j = gt - SEQ_T0[sq]
                        J = SEQ_J[sq]
                        srcs = []
                        if j > 0:
                            srcs.append((gt - 1, g * 5 + 0))
                        srcs.append((gt, g * 5 + (3 if j == 0 else (4 if j == J - 1 else 1))))
                        if j < J - 1:
                            srcs.append((gt + 1, g * 5 + 2))
                        for si, (sg, bi) in enumerate(srcs):
                            srs, slt = rslot_of(sg), sg % 2
                            MM(psb[bk][:, lt * 128:(lt + 1) * 128], uu[:, srs, slt, ch * 128:(ch + 1) * 128], band[:, bi, :],
                               si == 0, si == len(srcs) - 1, [B_uu[srs][slt], C_band], [B_ps[bk]])
                    if ch % 2 == 0:
                        ACT(dT[:, ch, :], psb[bk][:, 0:256], AF.Copy, [B_ps[bk]], [B_dT])
                    else:
                        CP("dve", dT[:, ch, :], psb[bk][:, 0:256], [B_ps[bk]], [B_dT])
                for oc in range(8):
                    g = oc // 2
                    bk = PJ[rot("pj", 2)]
                    for ic in range(2):
                        MM(psb[bk][:, 0:256], pw_sb[:, g, ic, (oc % 2) * 128:(oc % 2 + 1) * 128], dT[:, 2 * g + ic, :],
                           ic == 0, ic == 1, [B_dT, C_pw], [B_ps[bk]])
                    STT("dve", pool_y[:, oc, :], psb[bk][:, 0:256], pscT[:, oc:oc + 1], spzT[:, oc, :], ALU.mult, ALU.mult,
                        [B_ps[bk], C_psc, B_spz], [B_py])
                if nb < NBLK:
                    frontend_load(nb)
                ayb = [[B_ay[c][lt] for c in range(8)] for lt in range(2)]
                for mg in range(4):
                    slp = load_w(G_GP + mg)
                    tg = []
                    for lt in range(2):
                        bgp = proj(hs, lt, slp)
                        ia = rot("fa", 3)
                        ACT(fa[:, ia, :], psb[bgp][:], AF.Tanh, [B_ps[bgp]], [B_fa[ia]], scale=0.5)
                        tg.append(ia)
                    slu = load_w(G_UP + mg)
                    t1 = []
                    for lt in range(2):
                        bpu = proj(hs, lt, slu, nk=8, k0=0,
                                   lhs_fn=lambda kc: pool_y[:, kc, lt * 128:(lt + 1) * 128], lhs_bufs=[B_py])
                        ib = rot("fb", 2)
                        STT("dve", fb[:, ib, :], fa[:, tg[lt], :], 1.0, psb[bpu][:], ALU.add, ALU.mult,
                            [B_fa[tg[lt]], B_ps[bpu]], [B_fb[ib]])
                        t1.append(ib)
                    sla = load_w(G_GA + mg)
                    tg2 = []
                    for lt in range(2):
                        bga = proj(hs, lt, sla)
                        ia2 = rot("fa", 3)
                        ACT(fa[:, ia2, :], psb[bga][:], AF.Tanh, [B_ps[bga]], [B_fa[ia2]], scale=0.5)
                        tg2.append(ia2)
                    for lt in range(2):
                        bau = proj(hs, lt, slu, nk=8, k0=8,
                                   lhs_fn=lambda kc: attn_y[:, kc, lt * 128:(lt + 1) * 128], lhs_bufs=ayb[lt])
                        ia2 = tg2[lt]
                        STT("dve", fa[:, ia2, :], fa[:, ia2, :], 1.0, psb[bau][:], ALU.add, ALU.mult,
                            [B_fa[ia2], B_ps[bau]], [B_fa[ia2]])
                        ic = rot("bfs", 3)
                        TT("pool", bfs[:, ic, :], fb[:, t1[lt], :], fa[:, ia2, :], ALU.add, [B_fb[t1[lt]], B_fa[ia2]], [B_bfs[ic]])
                        transpose4(bfs[:, ic, :], B_bfs[ic], mT[:, 4 * mg:4 * mg + 4, lt * 128:(lt + 1) * 128], [B_mT[lt]], lt)
                flush(True)
                fe_items = []
                if nb < NBLK:
                    fe_prep(nb, 0)
                    fe_items = [[lambda: fe_round(nb, 0, 0)], [lambda: fe_round(nb, 0, 1)], [lambda: fe_round(nb, 0, 2)],
                                [lambda: fe_round(nb, 0, 3), lambda: fe_prep(nb, 1)], [],
                                [lambda: fe_round(nb, 1, 0)], [lambda: fe_round(nb, 1, 1)],
                                [lambda: fe_round(nb, 1, 2), lambda: fe_round(nb, 1, 3)]]
                for ngp in range(4):
                    sl = load_w(G_WO + ngp)
                    gi = rot("gq", 2)
                    DMA("sp", gq[:, gi, :], bass.AP(ada_dd, sq * 3 * D + 2 * D + ngp * 512, [[0, 128], [1, 512]]),
                        S_gq[gi], [B_adad], [B_gq[gi]])
                    for lt in range(2):
                        gt = lb * 2 + lt
                        xi = rot("xr", 2)
                        DMA("sp", xr[:, xi, :], x_d.ap()[gt * 128:(gt + 1) * 128, ngp * 512:(ngp + 1) * 512], S_xr[xi],
                            [], [B_xr[xi]])
                        bk = proj(hs, lt, sl, lhs_fn=lambda kc: mT[:, kc, lt * 128:(lt + 1) * 128], lhs_bufs=[B_mT[lt]])
                        ia = rot("fa", 3)
                        STT("dve", fa[:, ia, :], psb[bk][:], 0.25, gq[:, gi, :], ALU.mult, ALU.mult,
                            [B_ps[bk], B_gq[gi]], [B_fa[ia]])
                        TT("pool", xr[:, xi, :], fa[:, ia, :], xr[:, xi, :], ALU.add, [B_fa[ia], B_xr[xi]], [B_xr[xi]])
                        DMA("pool", y_d.ap()[gt * 128:(gt + 1) * 128, ngp * 512:(ngp + 1) * 512], xr[:, xi, :], S_yo[xi],
                            [B_xr[xi]], [B_yo[xi]])
                        if fe_items:
                            for f_ in fe_items.pop(0):
                                f_()

            if KLEVEL < 6:
                raise _Stop
            frontend_load(0)
            frontend_tile(0, 0)
            frontend_tile(0, 1)
            for s in range(NBLK + 1):
                if s < NBLK:
                    if KLEVEL < 10 + 2 * s:
                        raise _Stop
                    ahead_proj(s)
                if s == 0:
                    frontend_load(1)
                    frontend_tile(1, 0)
                    frontend_tile(1, 1)
                    build_rtable()
                flush(True)
                if s >= 1:
                    if KLEVEL < 10 + 2 * s + 1:
                        raise _Stop
                    lagged(s - 1)
        try:
            body()
        except _Stop:
            pass
        P.finish("sp")
        P.finish("pool")
        P.emit()
    return nc


_CACHE = {}


def kernel(x_prompt, x_sample, c_prompt, c_sample, w_ada, b_ada, norm_g, w_in, pool_w, pool_scale,
           q_norm_g, k_norm_g, rpb, w_pool_up, w_attn_up, w_o):
    f32 = np.float32
    xp = np.asarray(x_prompt, f32)
    xs = np.asarray(x_sample, f32)
    cp = np.asarray(c_prompt, f32)
    cs = np.asarray(c_sample, f32)
    if "nc" not in _CACHE:
        _CACHE["nc"] = build_program()
        _CACHE["consts"] = host_consts()
    nc = _CACHE["nc"]
    consts = _CACHE["consts"]
    shared = {
        "w_ada": np.ascontiguousarray(np.asarray(w_ada, f32)[0]),
        "b_ada": np.ascontiguousarray(np.asarray(b_ada, f32)[0].reshape(1, 3 * D)),
        "norm_g": np.ascontiguousarray(np.asarray(norm_g, f32)[0].reshape(16, 128)),
        "w_in": np.ascontiguousarray(np.asarray(w_in, f32)[0]),
        "pool_w": np.ascontiguousarray(np.asarray(pool_w, f32)[0]),
        "pool_scale": np.ascontiguousarray(np.asarray(pool_scale, f32)[0].reshape(8, 128)),
        "q_norm_g": np.ascontiguousarray(np.asarray(q_norm_g, f32)[0].reshape(1, 64)),
        "k_norm_g": np.ascontiguousarray(np.asarray(k_norm_g, f32)[0].reshape(1, 64)),
        "rpb": np.ascontiguousarray(np.asarray(rpb, f32)[0]),
        "w_pool_up": np.ascontiguousarray(np.asarray(w_pool_up, f32)[0]),
        "w_attn_up": np.ascontiguousarray(np.asarray(w_attn_up, f32)[0]),
        "w_o": np.ascontiguousarray(np.asarray(w_o, f32)[0]),
    }
    shared.update(consts)
    in_maps = []
    for c in range(8):
        xc = np.concatenate([xp[2 * c], xp[2 * c + 1], xs[c]], axis=0)
        cc = np.stack([cp[2 * c], cp[2 * c + 1], cs[c]], axis=0)
        m = dict(shared)
        m["x"] = np.ascontiguousarray(xc)
        m["c3"] = np.ascontiguousarray(cc)
        in_maps.append(m)
    res = run_bass_kernel_spmd(nc, in_maps, core_ids=list(range(8)))
    y_prompt = np.empty((16, 2048, D), f32)
    y_sample = np.empty((8, 4096, D), f32)
    for c in range(8):
        y = np.asarray(res.results[c]["y"], f32)
        y_prompt[2 * c] = y[0:2048]
        y_prompt[2 * c + 1] = y[2048:4096]
        y_sample[c] = y[4096:8192]
    return (y_prompt, y_sample)
```

```python
import numpy as np
import ml_dtypes
from contextlib import ExitStack
import concourse.bass as bass
import concourse.mybir as mybir
from concourse.bass_utils import run_bass_kernel_spmd

F32 = mybir.dt.float32
BF16 = mybir.dt.bfloat16
ALU = mybir.AluOpType
AF = mybir.ActivationFunctionType
AX = mybir.AxisListType

D = 2048
NTILE = 64
NBLK = 32
SEQ_T0 = (0, 16, 32)
SEQ_J = (16, 16, 32)
EPS = 1e-6
POOL_WINDOWS = (2, 4, 8, 16)
G_U, G_PZ, G_Q, G_K, G_V, G_AZ, G_GP, G_GA, G_WO, G_UP = 0, 2, 4, 6, 8, 10, 12, 16, 20, 24
NGRP = 28

ENGS = ("pe", "act", "dve", "pool", "sp")


class Buf:
    __slots__ = ("name", "w", "r")

    def __init__(self, name):
        self.name = name
        self.w = None
        self.r = []


class Prog:
    def __init__(self, nc):
        self.nc = nc
        self.ops = {e: [] for e in ENGS}
        self.cnt = {e: 0 for e in ENGS}
        self.seen = {e: {} for e in ENGS}
        self.dma_sems = {}

    def new_sem(self, name):
        self.dma_sems[name] = 0
        return name

    def _collect(self, eng, reads, writes, is_dma):
        seen = self.seen[eng]
        best = {}

        def add(ev):
            sk, val, e2 = ev
            if seen.get(sk, 0) >= val:
                return
            if best.get(sk, 0) < val:
                best[sk] = val

        for b in reads:
            if b.w is not None:
                if is_dma or not (b.w[2] == eng and eng == "pe"):
                    add(b.w)
        strict = is_dma or eng != "pe"
        for b in writes:
            if b.w is not None and (strict or b.w[2] != eng):
                add(b.w)
            for ev in b.r:
                if strict or ev[2] != eng:
                    add(ev)
        waits = []
        for sk, val in best.items():
            seen[sk] = val
            waits.append((sk, val))
        return waits

    def op(self, eng, fn, reads=(), writes=()):
        waits = self._collect(eng, reads, writes, False)
        self.cnt[eng] += 1
        ev = (eng, self.cnt[eng], eng)
        self.ops[eng].append((waits, fn, (eng, 1)))
        for b in reads:
            b.r.append(ev)
        for b in writes:
            b.w = ev
            b.r = []
        return ev

    def dma(self, eng, fn, semkey, reads=(), writes=()):
        waits = self._collect(eng, reads, writes, True)
        self.dma_sems[semkey] += 16
        ev = (semkey, self.dma_sems[semkey], "dma")
        self.ops[eng].append((waits, fn, (semkey, 16)))
        for b in reads:
            b.r.append(ev)
        for b in writes:
            b.w = ev
            b.r = []
        return ev

    def wait_all(self, eng, bufs):
        seen = self.seen[eng]
        best = {}
        for b in bufs:
            evs = list(b.r)
            if b.w is not None:
                evs.append(b.w)
            for (sk, val, e2) in evs:
                if seen.get(sk, 0) >= val:
                    continue
                if best.get(sk, 0) < val:
                    best[sk] = val
        waits = []
        for sk, val in best.items():
            seen[sk] = val
            waits.append((sk, val))
        self.ops[eng].append((waits, None, None))

    def finish(self, eng="sp"):
        waits = []
        for e in ENGS:
            if self.cnt[e] > 0:
                waits.append((e, self.cnt[e]))
        for k, v in self.dma_sems.items():
            if v > 0:
                waits.append((k, v))
        self.ops[eng].append((waits, None, None))

    def emit(self):
        nc = self.nc
        with ExitStack() as st:
            sems = {}
            for e in ENGS:
                sems[e] = st.enter_context(nc.semaphore("s_" + e))
            for k in self.dma_sems:
                sems[k] = st.enter_context(nc.semaphore("s_" + k))
            block = st.enter_context(nc.Block())
            handles = {"pe": block.tensor, "act": block.scalar, "dve": block.vector,
                       "pool": block.gpsimd, "sp": block.sync}
            for e in ENGS:
                ops = self.ops[e]
                if not ops:
                    continue

                def body(h, ops=ops):
                    for (waits, fn, inc) in ops:
                        for (sk, val) in waits:
                            h.wait_ge(sems[sk], val)
                        if fn is None:
                            continue
                        fn(h).then_inc(sems[inc[0]], inc[1])

                handles[e](body)


def bc_last(ap, n):
    return bass.AP(ap.tensor, ap.offset, [list(x) for x in ap.ap] + [[0, n]])


def seq_of_tile(gt):
    return 0 if gt < 16 else (1 if gt < 32 else 2)


def host_consts():
    bf = ml_dtypes.bfloat16
    ident = np.eye(128, dtype=np.float32)
    band = np.zeros((128, 20, 128), np.float64)
    for g, w in enumerate(POOL_WINDOWS):
        h = w // 2
        prev = band[:, g * 5 + 0, :]
        cur = band[:, g * 5 + 1, :]
        nxt = band[:, g * 5 + 2, :]
        first = band[:, g * 5 + 3, :]
        last = band[:, g * 5 + 4, :]
        for t in range(128):
            lo, hi = t - h, t + h - 1
            for tp in range(lo, hi + 1):
                if tp < 0:
                    prev[128 + tp, t] += 1.0 / w
                elif tp > 127:
                    nxt[tp - 128, t] += 1.0 / w
                else:
                    cur[tp, t] += 1.0 / w
            cur[t, t] -= 1.0
            lo2 = max(lo, 0)
            cnt = hi - lo2 + 1
            for tp in range(lo2, min(hi, 127) + 1):
                first[tp, t] += 1.0 / cnt
            first[t, t] -= 1.0
            hi2 = min(hi, 127)
            cnt = hi2 - lo + 1
            for tp in range(max(lo, 0), hi2 + 1):
                last[tp, t] += 1.0 / cnt
            last[t, t] -= 1.0
    maskr = np.zeros((128, 16, 64), np.float32)
    for p in range(128):
        krl, kc = p // 64, p % 64
        for cp in range(64):
            c = 63 - cp
            cs = min(max(c - 8, 0), 48)
            if cs <= kc < cs + 16:
                for i in range(16):
                    ok = (i <= 14) if krl == 0 else (i >= 1)
                    if ok:
                        maskr[p, i, cp] = 1.0
    return {
        "ident_bf": ident.astype(bf),
        "ident_f": ident,
        "band": band.astype(np.float32).astype(bf),
        "maskr": maskr.reshape(128, 1024),
    }


import os
KLEVEL = int(os.environ.get("KLEVEL", "99"))
KSUB = int(os.environ.get("KSUB", "99"))


class _Stop(Exception):
    pass


def build_program():
    nc = bass.Bass("TRN2", target_bir_lowering=False)
    P = Prog(nc)

    def din(name, shape, dt=F32):
        return nc.dram_tensor(name, list(shape), dt, kind="ExternalInput")

    x_d = din("x", [NTILE * 128, D])
    c_d = din("c3", [3, D])
    wada_d = din("w_ada", [D, 3 * D])
    bada_d = din("b_ada", [1, 3 * D])
    ng_d = din("norm_g", [16, 128])
    win_d = din("w_in", [D, 10240])
    pw_d = din("pool_w", [4, 256, 256])
    psc_d = din("pool_scale", [8, 128])
    qg_d = din("q_norm_g", [1, 64])
    kg_d = din("k_norm_g", [1, 64])
    rpb_d = din("rpb", [16, 15, 31])
    wpu_d = din("w_pool_up", [1024, D])
    wau_d = din("w_attn_up", [1024, D])
    wo_d = din("w_o", [D, D])
    identbf_d = din("ident_bf", [128, 128], BF16)
    identf_d = din("ident_f", [128, 128])
    band_d = din("band", [128, 20, 128], BF16)
    maskr_d = din("maskr", [128, 1024])
    y_d = nc.dram_tensor("y", [NTILE * 128, D], F32, kind="ExternalOutput")
    wsc_d = nc.dram_tensor("wsc", [NGRP, 128, 8192], BF16, kind="Internal")
    ada_dd = nc.dram_tensor("ada_s", [3, 3 * D], F32, kind="Internal")
    rp2_d = nc.dram_tensor("rp2", [16, 15, 160], F32, kind="Internal")
    rtab_d = nc.dram_tensor("rtab", [8, 128, 2048], BF16, kind="Internal")

    with ExitStack() as st:
        def sb(name, shape, dt):
            return st.enter_context(nc.sbuf_tensor("sb_" + name, list(shape), dt))

        hT = sb("hT", [128, 2, 16, 256], BF16)
        kT = sb("kT", [128, 3, 8, 256], BF16)
        vv = sb("vv", [128, 3, 2, 1024], BF16)
        uu = sb("uu", [128, 3, 2, 1024], BF16)
        wst = sb("wst", [128, 3, 16, 512], BF16)
        xt = sb("xt", [128, 2, 2048], F32)
        xnb = sb("xnb", [128, 2048], BF16)
        qT = sb("qT", [128, 2, 4, 256], BF16)
        sazT = sb("sazT", [128, 2, 4, 256], BF16)
        rt = sb("rt", [128, 2, 2, 16, 64], BF16)
        attn_y = sb("attn_y", [128, 8, 256], BF16)
        pool_y = sb("pool_y", [128, 8, 256], BF16)
        mT = sb("mT", [128, 16, 256], BF16)
        dT = sb("dT", [128, 8, 256], BF16)
        spzT = sb("spzT", [128, 8, 256], BF16)
        fa = sb("fa", [128, 3, 512], F32)
        fb = sb("fb", [128, 2, 512], F32)
        bfs = sb("bfs", [128, 3, 512], BF16)
        eb = sb("eb", [128, 2, 512], BF16)
        exb = sb("exb", [64, 2, 64], BF16)
        pt = sb("pt", [128, 6, 512], BF16)
        ptx = sb("ptx", [64, 6, 64], BF16)
        rd = sb("rd", [128, 2, 128], F32)
        wgt = sb("wgt", [128, 2, 128], F32)
        xr = sb("xr", [128, 2, 512], F32)
        gq = sb("gq", [128, 2, 512], F32)
        band = sb("band", [128, 20, 128], BF16)
        pw_sb = sb("pw_sb", [128, 4, 2, 256], BF16)
        gqbc = sb("gqbc", [128, 512], F32)
        gkbc = sb("gkbc", [128, 512], F32)
        ident = sb("ident", [128, 128], BF16)
        identf = sb("identf", [128, 128], F32)
        ones = sb("ones", [128, 64], BF16)
        g64 = sb("g64", [128, 2, 64], F32)
        gsT = sb("gsT", [128, 3, 16], F32)
        shT = sb("shT", [128, 3, 16], F32)
        pscT = sb("pscT", [128, 8], F32)
        ngT = sb("ngT", [128, 16], F32)
        ssx = sb("ssx", [128, 4], F32)
        ss8 = sb("ss8", [128, 4, 8], F32)
        nhalf = sb("nhalf", [128, 8], F32)
        sT = sb("sT", [128, 48], F32)
        small48 = gq[0:48, 0, 0:384].rearrange("p (a n) -> p a n", n=128)
        small16 = gq[0:16, 1, 0:256].rearrange("p (a n) -> p a n", n=128)
        adab = fb[0:3, :, :]
        badab = fa[0:3, 2, :]

        psb = [st.enter_context(nc.psum_tensor("ps%d" % i, [128, 512], F32)) for i in range(8)]
        trps = [psb[2].bitcast(BF16), psb[3].bitcast(BF16)]
        PJ = [0, 1]
        S_BANK = [4, 5]
        OD_BANK = [6, 7]

        B_ps = [Buf("ps%d" % i) for i in range(8)]
        B_trp = [B_ps[2], B_ps[3]]
        B_hT = [[Buf("hT%d_%d" % (s, t)) for t in range(2)] for s in range(2)]
        B_kT = [[Buf("kT%d_%d" % (s, t)) for t in range(2)] for s in range(3)]
        B_vv = [[Buf("vv%d_%d" % (s, t)) for t in range(2)] for s in range(3)]
        B_uu = [[Buf("uu%d_%d" % (s, t)) for t in range(2)] for s in range(3)]
        B_wst = [Buf("wst%d" % i) for i in range(3)]
        B_xt = [Buf("xt0"), Buf("xt1")]
        B_xnb = Buf("xnb")
        B_qT = [Buf("qT0"), Buf("qT1")]
        B_saz = [Buf("saz0"), Buf("saz1")]
        B_rt = [Buf("rt0"), Buf("rt1")]
        B_ay = [[Buf("ay%d_%d" % (c, t)) for t in range(2)] for c in range(8)]
        B_py = Buf("py")
        B_mT = [Buf("mT0"), Buf("mT1")]
        B_dT = Buf("dT")
        B_spz = Buf("spz")
        B_fa = [Buf("fa%d" % i) for i in range(3)]
        B_fb = [Buf("fb%d" % i) for i in range(2)]
        B_bfs = [Buf("bfs%d" % i) for i in range(3)]
        B_eb = [Buf("eb0"), Buf("eb1")]
        B_exb = [Buf("exb0"), Buf("exb1")]
        B_pt = [Buf("pt%d" % i) for i in range(6)]
        B_ptx = [Buf("ptx%d" % i) for i in range(6)]
        B_rd = [Buf("rd0"), Buf("rd1")]
        B_wgt = [Buf("wgt0"), Buf("wgt1")]
        B_xr = [Buf("xr0"), Buf("xr1")]
        B_gq = [Buf("gq0"), Buf("gq1")]
        B_const = Buf("const")
        C_ident = Buf("c_ident"); C_identf = Buf("c_identf"); C_band = Buf("c_band"); C_g64 = Buf("c_g64")
        C_ones = Buf("c_ones"); C_nhalf = Buf("c_nhalf"); C_gq = Buf("c_gqbc"); C_gk = Buf("c_gkbc")
        C_pw = Buf("c_pw"); C_ng = Buf("c_ng"); C_psc = Buf("c_psc"); C_gs = Buf("c_gs"); C_sh = Buf("c_sh")
        C_sT = Buf("c_sT"); C_s48 = [Buf("c_s48_%d" % i) for i in range(3)]; C_s16 = [Buf("c_s16_0"), Buf("c_s16_1")]
        C_badab = Buf("c_badab"); C_adab = [Buf("c_adab0"), Buf("c_adab1")]
        B_yo = [Buf("yo0"), Buf("yo1")]
        B_ss = Buf("ss")
        B_ss8 = [Buf("ss8_%d" % i) for i in range(4)]
        B_wsc = [Buf("wsc%d" % g) for g in range(NGRP)]
        B_ada = Buf("ada")
        B_adad = Buf("adad")
        B_rp2 = Buf("rp2")
        B_rtab = Buf("rtab")
        B_y = Buf("y")
        B_misc = Buf("misc")

        ctr = {"fa": 0, "fb": 0, "bfs": 0, "pj": 0, "trp": 0, "eb": 0, "pt": 0, "rd": 0,
               "xr": 0, "gq": 0, "S": 0, "OD": 0, "wst": 0, "ss8": 0, "xt": 0, "rt": 0}

        def rot(name, n):
            i = ctr[name] % n
            ctr[name] += 1
            return i

        S_wst = [P.new_sem("wst%d" % i) for i in range(3)]
        S_xt = [P.new_sem("xt%d" % i) for i in range(2)]
        S_xr = [P.new_sem("xr%d" % i) for i in range(2)]
        S_gq = [P.new_sem("gq%d" % i) for i in range(2)]
        S_rt = [P.new_sem("rt%d" % i) for i in range(2)]
        S_yo = [P.new_sem("yo0"), P.new_sem("yo1")]

        def MM(out, lhsT, rhs, start, stop, r, w, tp=None):
            if tp is None:
                P.op("pe", lambda e: e.matmul(out, lhsT=lhsT, rhs=rhs, start=start, stop=stop), r, w)
            else:
                P.op("pe", lambda e: e.matmul(out, lhsT=lhsT, rhs=rhs, start=start, stop=stop,
                                              tile_position=tp), r, w)

        def TR(out, in_, idn, r, w):
            P.op("pe", lambda e: e.transpose(out, in_, idn), r, w)

        def ACT(out, in_, func, r, w, scale=1.0, bias=None, accum=None):
            kw = {}
            if bias is not None:
                kw["bias"] = bias
            if accum is not None:
                kw["accum_out"] = accum
            P.op("act", lambda e: e.activation(out=out, in_=in_, func=func, scale=scale, **kw), r, w)

        def TS(eng, out, in0, s1, s2, op0, op1, r, w):
            if s2 is None:
                P.op(eng, lambda e: e.tensor_scalar(out=out, in0=in0, scalar1=s1, scalar2=None, op0=op0), r, w)
            else:
                P.op(eng, lambda e: e.tensor_scalar(out=out, in0=in0, scalar1=s1, scalar2=s2, op0=op0, op1=op1), r, w)

        def TT(eng, out, in0, in1, op, r, w):
            P.op(eng, lambda e: e.tensor_tensor(out=out, in0=in0, in1=in1, op=op), r, w)

        def STT(eng, out, in0, scalar, in1, op0, op1, r, w):
            P.op(eng, lambda e: e.scalar_tensor_tensor(out=out, in0=in0, scalar=scalar, in1=in1, op0=op0, op1=op1), r, w)

        def CP(eng, out, in_, r, w):
            P.op(eng, lambda e: e.tensor_copy(out=out, in_=in_), r, w)

        def MSET(eng, ap, val, r, w):
            P.op(eng, lambda e: e.memset(ap, val), r, w)

        nsem = [0]

        def fresh():
            nsem[0] += 1
            return P.new_sem("one%d" % nsem[0])

        def DMAn(q, out, in_, r, w, sem=None):
            if sem is None:
                sem = fresh()
            return P.dma(q, lambda e: e.dma_start(out=out, in_=in_), sem, r, w)

        def DMAG(items):
            sem = fresh()
            ev = None
            ws = []
            for (q, out, in_, r, w) in items:
                ev = DMAn(q, out, in_, r, w, sem=sem)
                ws.extend(w)
            for b in ws:
                b.w = ev

        def DMA(q, out, in_, sem, r, w, slow=False):
            if slow:
                P.dma(q, lambda e: e.dma_start(out=out, in_=in_, allow_slow_non_contiguous=True), sem, r, w)
            else:
                P.dma(q, lambda e: e.dma_start(out=out, in_=in_), sem, r, w)

        def body():
            cast_gate = []

            def cast_group(g, src_ap_2d, kchunks, dst_k0, sem):
                src = src_ap_2d.rearrange("(kc p) n -> p kc n", p=128)
                dst = wsc_d.ap()[g].rearrange("p (kc n) -> p kc n", n=512)
                step = 8
                ev = None
                for k0 in range(0, kchunks, step):
                    ev = DMAn("pool", dst[:, dst_k0 + k0:dst_k0 + k0 + step, :], src[:, k0:k0 + step, :], list(cast_gate), [Buf("dmy")], sem=sem)
                return ev

            def win_cols(g):
                return win_d.ap()[:, g * 512:(g + 1) * 512]

            def cast_phase(groups):
                sem = fresh()
                ev = None
                for g in groups:
                    ev = cast_group(g, win_cols(g), 16, 0, sem)
                for g in groups:
                    B_wsc[g].w = ev

            cast_phase([G_K, G_K + 1])

            def late_casts():
                cast_phase([G_V, G_V + 1, G_U, G_U + 1])
                cast_phase([G_Q, G_AZ])
                cast_phase([G_Q + 1, G_AZ + 1])
                cast_phase([G_PZ, G_PZ + 1])
                for mg in range(4):
                    sem = fresh()
                    cast_group(G_GP + mg, win_cols(G_GP + mg), 16, 0, sem)
                    cast_group(G_UP + mg, wpu_d.ap()[:, mg * 512:(mg + 1) * 512], 8, 0, sem)
                    cast_group(G_UP + mg, wau_d.ap()[:, mg * 512:(mg + 1) * 512], 8, 8, sem)
                    ev = cast_group(G_GA + mg, win_cols(G_GA + mg), 16, 0, sem)
                    for g in (G_GP + mg, G_UP + mg, G_GA + mg):
                        B_wsc[g].w = ev
                sem = fresh()
                ev = None
                for ngp in range(4):
                    ev = cast_group(G_WO + ngp, wo_d.ap()[:, ngp * 512:(ngp + 1) * 512], 16, 0, sem)
                for ngp in range(4):
                    B_wsc[G_WO + ngp].w = ev
            DMAn("pool", pw_sb[:], pw_d.ap().rearrange("g (ic p) n -> p g ic n", p=128), [], [C_pw])

            if KLEVEL < 1:
                raise _Stop
            DMAG([("sp", ident[:], identbf_d.ap(), [], [C_ident]),
                  ("sp", identf[:], identf_d.ap(), [], [C_identf]),
                  ("sp", band[:], band_d.ap(), [], [C_band]),
                  ("sp", g64[:, 0, :], bass.AP(qg_d, 0, [[0, 128], [1, 64]]), [], [C_g64]),
                  ("sp", g64[:, 1, :], bass.AP(kg_d, 0, [[0, 128], [1, 64]]), [], [C_g64])])
            MSET("dve", ones[:], 1.0, [], [C_ones])
            MSET("dve", nhalf[:], -0.5, [], [C_nhalf])
            a0 = g64[:, 0, :]
            a1 = g64[:, 1, :]
            TS("dve", gqbc[:].rearrange("p (h d) -> p h d", d=64),
               bass.AP(a0.tensor, a0.offset, [list(a0.ap[0]), [0, 8], [1, 64]]), 1.0, None, ALU.mult, None, [C_g64], [C_gq])
            TS("dve", gkbc[:].rearrange("p (h d) -> p h d", d=64),
               bass.AP(a1.tensor, a1.offset, [list(a1.ap[0]), [0, 8], [1, 64]]), 8.0, None, ALU.mult, None, [C_g64], [C_gk])

            if KLEVEL < 2:
                raise _Stop
            DMAG([("sp", small48[:, 0, :], c_d.ap().rearrange("b (kc p) -> (b kc) p", p=128), [], [C_s48[0]]),
                  ("sp", small16[:, 0, :], ng_d.ap(), [], [C_s16[0]]),
                  ("sp", small16[0:8, 1, :], psc_d.ap(), [], [C_s16[1]])])
            pT = psb[0]
            TR(pT[:, 0:48], small48[:, 0, :], identf[0:48, 0:48], [C_s48[0], C_identf], [B_ps[0]])
            TR(pT[:, 48:64], small16[:, 0, :], identf[0:16, 0:16], [C_s16[0], C_identf], [B_ps[0]])
            TR(pT[:, 64:72], small16[0:8, 1, :], identf[0:8, 0:8], [C_s16[1], C_identf], [B_ps[0]])
            cTt = fa[:, 0, 0:48]
            CP("dve", cTt, pT[:, 0:48], [B_ps[0]], [B_fa[0]])
            CP("dve", ngT[:], pT[:, 48:64], [B_ps[0]], [C_ng])
            CP("dve", pscT[:], pT[:, 64:72], [B_ps[0]], [C_psc])
            tht = fa[:, 1, 0:48]
            ACT(tht, cTt, AF.Tanh, [B_fa[0]], [B_fa[1]], scale=0.5)
            STT("dve", sT[:], tht, 1.0, cTt, ALU.add, ALU.mult, [B_fa[0], B_fa[1]], [C_sT])

            if KLEVEL < 3:
                raise _Stop
            wst_f = [wst[:, i].rearrange("p a n -> p (a n)").bitcast(F32) for i in range(3)]
            sT3 = sT[:].rearrange("p (b k) -> p b k", k=16)
            B_adad_p = []
            S_badab = fresh()
            S_adst = [fresh(), fresh()]
            for half in range(2):
                banks = [0, 1, 2, 4, 5, 6]
                for kc in range(16):
                    sl = rot("wst", 3)
                    DMA("sp", wst_f[sl][:, 0:3072], wada_d.ap()[kc * 128:(kc + 1) * 128, half * 3072:(half + 1) * 3072],
                        S_wst[sl], [], [B_wst[sl]])
                    for n in range(6):
                        MM(psb[banks[n]][0:3, :], sT3[:, :, kc], wst_f[sl][:, n * 512:(n + 1) * 512], kc == 0, kc == 15,
                           [C_sT, B_wst[sl]], [B_ps[banks[n]]])
                for n in range(6):
                    col = half * 3072 + n * 512
                    ab = n % 2
                    DMAn("sp", badab, bass.AP(bada_d, col, [[0, 3], [1, 512]]), [], [C_badab], sem=S_badab)
                    STT("dve", adab[:, ab, :], psb[banks[n]][0:3, :], 0.5, badab, ALU.mult, ALU.add,
                        [B_ps[banks[n]], C_badab], [C_adab[ab]])
                    bp = Buf("adad%d" % col)
                    B_adad_p.append(bp)
                    DMAn("sp", ada_dd.ap()[:, col:col + 512], adab[:, ab, :], [C_adab[ab]], [bp], sem=S_adst[ab])
            gate_b = Buf("castgate")
            gate_b.w = B_wst[(ctr["wst"] - 1) % 3].w
            cast_gate.append(gate_b)
            late_casts()
            del cast_gate[:]
            for bp in B_adad_p:
                sk = bp.w[0]
                bp.w = (sk, P.dma_sems[sk], "dma")
            DMAG([("sp", small48[b * 16:(b + 1) * 16, 1, :], bass.AP(ada_dd, b * 3 * D, [[128, 16], [1, 128]]), B_adad_p, [C_s48[1]])
                  for b in range(3)])
            DMAG([("sp", small48[b * 16:(b + 1) * 16, 2, :], bass.AP(ada_dd, b * 3 * D + D, [[128, 16], [1, 128]]), B_adad_p, [C_s48[2]])
                  for b in range(3)])
            TR(pT[:, 0:48], small48[:, 1, :], identf[0:48, 0:48], [C_s48[1], C_identf], [B_ps[0]])
            TR(pT[:, 48:96], small48[:, 2, :], identf[0:48, 0:48], [C_s48[2], C_identf], [B_ps[0]])
            CP("dve", shT[:].rearrange("p b k -> p (b k)"), pT[:, 0:48], [B_ps[0]], [C_sh])
            for b in range(3):
                STT("dve", gsT[:, b, :], pT[:, 48 + 16 * b:64 + 16 * b], 1.0, ngT[:], ALU.add, ALU.mult,
                    [B_ps[0], C_ng], [C_gs])
            B_adad = Buf("adad_all")
            for bp in B_adad_p:
                if bp.w is not None:
                    B_adad.r.append(bp.w)

            B_rtab_h = [Buf("rtab%d" % h) for h in range(16)]

            def build_rtable():
                if KLEVEL < 5:
                    raise _Stop
                xt_flat = xt[:].rearrange("p a n -> p (a n)")
                rpt = xt_flat[0:16, 0:2400].rearrange("p (a b) -> p a b", b=160)
                rtmp = xt_flat[0:16, 2400:2400 + 465].rearrange("p (a b) -> p a b", b=31)
                MSET("pool", rpt, 0.0, [], [B_xt[0], B_xt[1]])
                DMAn("sp", rtmp, rpb_d.ap(), [], [B_xt[0], B_xt[1]])
                CP("pool", rpt[:, :, 64:95], rtmp[:, ::-1, :], [B_xt[0], B_xt[1]], [B_xt[0], B_xt[1]])
                DMAn("sp", rp2_d.ap(), rpt, [B_xt[0], B_xt[1]], [B_rp2])
                mT_f = mT[:].rearrange("p a b -> p (a b)").bitcast(F32)
                ay_f = attn_y[:].rearrange("p a b -> p (a b)").bitcast(F32)
                py_f = pool_y[:].rearrange("p a b -> p (a b)").bitcast(F32)
                qT_f = qT[:].rearrange("p a b c -> p (a b c)").bitcast(F32)
                dT_fl = dT[:].rearrange("p a b -> p (a b)")
                sp_fl = spzT[:].rearrange("p a b -> p (a b)")
                maskt = qT_f[:, 0:1024]
                DMAn("sp", maskt, maskr_d.ap(), [], [B_qT[0], B_qT[1]])
                MSET("pool", mT_f[:, 0:2048], 0.0, [], [B_mT[0], B_mT[1]])
                S_rb = [fresh(), fresh()]
                B_est = [Buf("est0"), B_py]
                B_rb = [B_dT, B_spz]
                S_stg = [P.new_sem("stg0"), P.new_sem("stg1")]
                ests = [ay_f[:, 0:1024], py_f[:, 0:1024]]
                rbufs = [dT_fl[:, 0:1024], sp_fl[:, 0:1024]]
                for h in range(16):
                    s_ = h % 2
                    stg = mT_f[:, s_ * 1024:(s_ + 1) * 1024]
                    stg3 = stg.rearrange("p (i c) -> p i c", c=64)
                    src = bass.AP(rp2_d, h * 15 * 160 + 16, [[1, 64], [160, 15], [1, 64]])
                    DMA("sp", stg3[0:64, 0:15, :], src, S_stg[s_], [B_rp2], [B_mT[s_]])
                    dmy = Buf("dmy")
                    ev_ = P.dma("sp", (lambda o_, i_: (lambda e: e.dma_start(out=o_, in_=i_)))(stg3[64:128, 1:16, :], src),
                                S_stg[s_], [B_rp2], [dmy])
                    B_mT[s_].w = ev_
                    ACT(ests[s_], stg, AF.Exp, [B_mT[s_]], [B_est[s_]])
                    TT("dve", rbufs[s_], ests[s_], maskt, ALU.mult, [B_est[s_], B_qT[0], B_qT[1]], [B_rb[s_]])
                    DMAn("sp", rtab_d.ap()[h // 2][:, (h % 2) * 1024:(h % 2 + 1) * 1024], rbufs[s_], [B_rb[s_]], [B_rtab_h[h]], sem=S_rb[s_])
                for h in range(16):
                    B_rtab_h[h].w = (S_rb[h % 2], P.dma_sems[S_rb[h % 2]], "dma")
                for c_ in range(8):
                    for t_ in range(2):
                        B_ay[c_][t_].r.extend(B_est[0].r)
                        if B_est[0].w is not None:
                            B_ay[c_][t_].r.append(B_est[0].w)

            for cb in C_s48 + C_s16:
                for gb_ in B_gq:
                    gb_.r.extend(cb.r)
                    if cb.w is not None:
                        gb_.r.append(cb.w)
            def load_w(g):
                sl = rot("wst", 3)
                DMA("sp", wst[:, sl].rearrange("p a n -> p (a n)"), wsc_d.ap()[g], S_wst[sl], [B_wsc[g]], [B_wst[sl]])
                return sl

            def proj(hslot, lt, sl, nk=16, k0=0, lhs_fn=None, lhs_bufs=None):
                flush()
                gen[0] += 1
                bk = PJ[rot("pj", 2)]
                for kc in range(nk):
                    if lhs_fn is None:
                        lhsT = hT[:, hslot, kc, lt * 128:(lt + 1) * 128]
                        rb = [B_hT[hslot][lt]]
                    else:
                        lhsT = lhs_fn(kc)
                        rb = lhs_bufs
                    MM(psb[bk][:], lhsT, wst[:, sl, k0 + kc, :], kc == 0, kc == nk - 1, rb + [B_wst[sl]], [B_ps[bk]])
                return bk

            deferred = []
            gen = [0]

            def flush(all_=False):
                keep = []
                for (g_, f_) in deferred:
                    if all_ or g_ <= gen[0] - 2:
                        f_()
                    else:
                        keep.append((g_, f_))
                deferred[:] = keep

            def transpose4(src_bf, src_buf, dst_ap, dst_bufs, eng_i):
                deferred.append((gen[0], lambda: transpose4_now(src_bf, src_buf, dst_ap, dst_bufs, eng_i)))

            def transpose4_now(src_bf, src_buf, dst_ap, dst_bufs, eng_i):
                th = rot("trp", 2)
                trp = trps[th]
                for c in range(4):
                    TR(trp[:, c * 128:(c + 1) * 128], src_bf[:, c * 128:(c + 1) * 128], ident[:],
                       [src_buf, C_ident], [B_trp[th]])
                src = trp[:, 0:512].rearrange("p (c t) -> p c t", t=128)
                if eng_i % 2 == 0:
                    ACT(dst_ap, src, AF.Copy, [B_trp[th]], dst_bufs)
                else:
                    CP("dve", dst_ap, src, [B_trp[th]], dst_bufs)

            def headnorm(bk, gbc, gbuf, dst_ap, dst_bufs, eng_i):
                ia = rot("fa", 3)
                ib = rot("fb", 2)
                ic = rot("bfs", 3)
                i8 = rot("ss8", 4)
                ACT(fa[:, ia, :], psb[bk][:], AF.Copy, [B_ps[bk]], [B_fa[ia]])
                ACT(fb[:, ib, :], fa[:, ia, :], AF.Square, [B_fa[ia]], [B_fb[ib]])
                P.op("dve", lambda e: e.tensor_reduce(out=ss8[:, i8, :], in_=fb[:, ib, :].rearrange("p (h d) -> p h d", d=64),
                                                      axis=AX.X, op=ALU.add), [B_fb[ib]], [B_ss8[i8]])
                TS("dve", ss8[:, i8, :], ss8[:, i8, :], 64.0 * EPS, None, ALU.add, None, [B_ss8[i8]], [B_ss8[i8]])
                TT("pool", ss8[:, i8, :], ss8[:, i8, :], nhalf[:], ALU.pow, [B_ss8[i8], C_nhalf], [B_ss8[i8]])
                TT("dve", fb[:, ib, :].rearrange("p (h d) -> p h d", d=64),
                   fa[:, ia, :].rearrange("p (h d) -> p h d", d=64), bc_last(ss8[:, i8, :], 64), ALU.mult,
                   [B_fa[ia], B_ss8[i8]], [B_fb[ib]])
                TT("pool", bfs[:, ic, :], fb[:, ib, :], gbc[:], ALU.mult, [B_fb[ib], gbuf], [B_bfs[ic]])
                transpose4(bfs[:, ic, :], B_bfs[ic], dst_ap, dst_bufs, eng_i)

            def silu2(bk, dst_ap, dst_bufs, eng_i):
                ia = rot("fa", 3)
                ic = rot("bfs", 3)
                ACT(fa[:, ia, :], psb[bk][:], AF.Tanh, [B_ps[bk]], [B_fa[ia]], scale=0.5)
                STT("dve", bfs[:, ic, :], fa[:, ia, :], 1.0, psb[bk][:], ALU.add, ALU.mult, [B_fa[ia], B_ps[bk]], [B_bfs[ic]])
                transpose4(bfs[:, ic, :], B_bfs[ic], dst_ap, dst_bufs, eng_i)

            def rslot_of(gt):
                return (gt // 2) % 3

            xt_of = {}

            def frontend_load(gb):
                for lt in range(2):
                    gt = gb * 2 + lt
                    xi = rot("xt", 2)
                    xt_of[gt] = xi
                    DMA("sp", xt[:, xi, :], x_d.ap()[gt * 128:(gt + 1) * 128, :], S_xt[xi], [], [B_xt[xi]])

            def fe_prep(gb, lt):
                gt = gb * 2 + lt
                xi = xt_of[gt]
                col = gt % 4
                MSET("dve", ssx[:, col:col + 1], 0.0, [], [B_ss])
                ACT(xnb[:], xt[:, xi, :], AF.Square, [B_xt[xi], B_ss], [B_xnb, B_ss], accum=ssx[:, col:col + 1])
                TS("dve", ssx[:, col:col + 1], ssx[:, col:col + 1], 1.0 / D, EPS, ALU.mult, ALU.add, [B_ss], [B_ss])
                TT("pool", ssx[:, col:col + 1], ssx[:, col:col + 1], nhalf[:, 0:1], ALU.pow, [B_ss, C_nhalf], [B_ss])
                ACT(xnb[:], xt[:, xi, :], AF.Copy, [B_xt[xi], B_ss], [B_xnb], scale=ssx[:, col:col + 1])

            def fe_round(gb, lt, q4):
                hs = gb % 2
                gt = gb * 2 + lt
                sq = seq_of_tile(gt)
                th = rot("trp", 2)
                for c in range(4):
                    kc = q4 * 4 + c
                    TR(trps[th][:, c * 128:(c + 1) * 128], xnb[:, kc * 128:(kc + 1) * 128], ident[:],
                       [B_xnb, C_ident], [B_trp[th]])
                for c in range(4):
                    kc = q4 * 4 + c
                    src = trps[th][:, c * 128:(c + 1) * 128]
                    dst = hT[:, hs, kc, lt * 128:(lt + 1) * 128]
                    if c % 2 == 0:
                        ACT(dst, src, AF.Identity, [B_trp[th], C_gs, C_sh], [B_hT[hs][lt]],
                            scale=gsT[:, sq, kc:kc + 1], bias=shT[:, sq, kc:kc + 1])
                    else:
                        TS("dve", dst, src, gsT[:, sq, kc:kc + 1], shT[:, sq, kc:kc + 1], ALU.mult, ALU.add,
                           [B_trp[th], C_gs, C_sh], [B_hT[hs][lt]])

            def frontend_tile(gb, lt):
                fe_prep(gb, lt)
                for q4 in range(4):
                    fe_round(gb, lt, q4)

            def ahead_proj(gb):
                hs = gb % 2
                rs = gb % 3
                for kg in range(2):
                    sl = load_w(G_K + kg)
                    for lt in range(2):
                        bk = proj(hs, lt, sl)
                        dst = kT[:, rs, 4 * kg:4 * kg + 4, lt * 128:(lt + 1) * 128]
                        headnorm(bk, gkbc, C_gk, dst, [B_kT[rs][lt]], lt)
                for (gbase, ring, Bring) in ((G_V, vv, B_vv), (G_U, uu, B_uu)):
                    for g2 in range(2):
                        sl = load_w(gbase + g2)
                        for lt in range(2):
                            bk = proj(hs, lt, sl)
                            dst = ring[:, rs, lt, g2 * 512:(g2 + 1) * 512]
                            if lt == 0:
                                ACT(dst, psb[bk][:], AF.Copy, [B_ps[bk]], [Bring[rs][lt]])
                            else:
                                CP("dve", dst, psb[bk][:], [B_ps[bk]], [Bring[rs][lt]])

            S4 = [0, 1, 2, 3, 4, 5]

            def att_qk(lb, sq, hp, lt, ri):
                qb, c4 = hp // 4, hp % 4
                gt = lb * 2 + lt
                j = gt - SEQ_T0[sq]
                J = SEQ_J[sq]
                if j == 0:
                    olist, i0, interior = [3, 2, 1, 0], 1, False
                elif j == 1:
                    olist, i0, interior = [2, 1, 0, -1], 3, False
                elif j == J - 2:
                    olist, i0, interior = [1, 0, -1, -2], 5, False
                elif j == J - 1:
                    olist, i0, interior = [0, -1, -2, -3], 7, False
                else:
                    olist, i0, interior = [1, 0, -1, -2], 5, True
                odb = OD_BANK[rot("OD", 2)]
                OD = psb[odb]
                sbis = [S4[rot("S", 6)], S4[rot("S", 6)]]
                for a, o in enumerate(olist):
                    gk = gt + o
                    krs, klt = rslot_of(gk), gk % 2
                    for hh in range(2):
                        p0 = 64 * hh
                        MM(psb[sbis[hh]][:, a * 128:(a + 1) * 128], kT[p0:p0 + 64, krs, hp, klt * 128:(klt + 1) * 128],
                           qT[p0:p0 + 64, qb, c4, lt * 128:(lt + 1) * 128], True, True,
                           [B_kT[krs][klt], B_qT[qb]], [B_ps[sbis[hh]]], tp=(p0, 0))
                if interior:
                    gk2 = gt + 2
                    krs2, klt2 = rslot_of(gk2), gk2 % 2
                    for hh in range(2):
                        p0 = 64 * hh
                        MM(psb[sbis[hh]][0:64, 448:512], kT[p0:p0 + 64, krs2, hp, klt2 * 128:klt2 * 128 + 64],
                           qT[p0:p0 + 64, qb, c4, lt * 128 + 64:(lt + 1) * 128], True, True,
                           [B_kT[krs2][klt2], B_qT[qb]], [B_ps[sbis[hh]]], tp=(p0, 0))
                pts = []
                for hh in range(2):
                    h = 2 * hp + hh
                    p0 = 64 * hh
                    sbi = sbis[hh]
                    S = psb[sbi]
                    ei = rot("eb", 2)
                    ACT(eb[:, ei, :], S[:], AF.Exp, [B_ps[sbi]], [B_eb[ei]])
                    pi = rot("pt", 6)
                    TT("dve", pt[:, pi, :].rearrange("p (i c) -> p i c", c=64),
                       eb[:, ei, :].rearrange("p (i c) -> p i c", c=64),
                       rt[:, ri, hh, i0:i0 + 8, ::-1], ALU.mult, [B_eb[ei], B_rt[ri]], [B_pt[pi]])
                    if interior:
                        TT("dve", ptx[:, pi, :], eb[0:64, ei, 448:512], rt[0:64, ri, hh, 4, ::-1], ALU.mult,
                           [B_eb[ei], B_rt[ri]], [B_ptx[pi]])
                        MSET("dve", pt[0:64, pi, 448:512], 0.0, [], [B_pt[pi]])
                    pts.append((pi, hh, h, p0))
                return (gt, hp, lt, qb, c4, olist, interior, odb, pts)

            def att_pv(st_):
                (gt, hp, lt, qb, c4, olist, interior, odb, pts) = st_
                OD = psb[odb]
                n = len(olist)
                for (pi, hh, h, p0) in pts:
                    for a, o in enumerate(olist):
                        gk = gt + o
                        krs, klt = rslot_of(gk), gk % 2
                        last = (a == n - 1) and not interior
                        MM(OD[p0:p0 + 64, 0:128], vv[:, krs, klt, h * 64:(h + 1) * 64], pt[:, pi, a * 128:(a + 1) * 128],
                           a == 0, last, [B_vv[krs][klt], B_pt[pi]], [B_ps[odb]], tp=(0, p0))
                    if interior:
                        gk = gt + 2
                        krs, klt = rslot_of(gk), gk % 2
                        MM(OD[p0:p0 + 64, 64:128], vv[0:64, krs, klt, h * 64:(h + 1) * 64], ptx[:, pi, :],
                           False, True, [B_vv[krs][klt], B_ptx[pi]], [B_ps[odb]], tp=(0, p0))
                    for a, o in enumerate(olist):
                        last = (a == n - 1) and not interior
                        MM(OD[p0:p0 + 64, 128:256], ones[:, 0:64], pt[:, pi, a * 128:(a + 1) * 128],
                           a == 0, last, [C_ones, B_pt[pi]], [B_ps[odb]], tp=(0, p0))
                    if interior:
                        MM(OD[p0:p0 + 64, 192:256], ones[0:64, 0:64], ptx[:, pi, :],
                           False, True, [C_ones, B_ptx[pi]], [B_ps[odb]], tp=(0, p0))
                di = rot("rd", 2)
                P.op("dve", lambda e: e.reciprocal(out=rd[:, di, :], in_=OD[:, 128:256]), [B_ps[odb]], [B_rd[di]])
                TT("dve", wgt[:, di, :], rd[:, di, :], sazT[:, qb, c4, lt * 128:(lt + 1) * 128], ALU.mult,
                   [B_rd[di], B_saz[qb]], [B_wgt[di]])
                TT("dve", attn_y[:, hp, lt * 128:(lt + 1) * 128], OD[:, 0:128], wgt[:, di, :], ALU.mult,
                   [B_ps[odb], B_wgt[di]], [B_ay[hp][lt]])

            def lagged(lb):
                hs = lb % 2
                sq = seq_of_tile(lb * 2)
                nb = lb + 2
                rt_next = []

                def rt_load(hp_):
                    ri_ = rot("rt", 2)
                    DMA("pool", rt[:, ri_].rearrange("p a i c -> p (a i c)"), rtab_d.ap()[hp_], S_rt[ri_],
                        [B_rtab_h[2 * hp_], B_rtab_h[2 * hp_ + 1]], [B_rt[ri_]])
                    rt_next.append(ri_)

                rt_load(0)
                for qg in range(2):
                    qb = qg
                    sl = load_w(G_Q + qg)
                    for lt in range(2):
                        bk = proj(hs, lt, sl)
                        headnorm(bk, gqbc, C_gq, qT[:, qb, :, lt * 128:(lt + 1) * 128], [B_qT[qb]], lt)
                    sl = load_w(G_AZ + qg)
                    for lt in range(2):
                        bk = proj(hs, lt, sl)
                        silu2(bk, sazT[:, qb, :, lt * 128:(lt + 1) * 128], [B_saz[qb]], lt + 1)
                flush(True)
                pend = []
                for hp in range(8):
                    ri = rt_next.pop(0)
                    if hp + 1 < 8:
                        rt_load(hp + 1)
                    for lt in range(2):
                        pend.append(att_qk(lb, sq, hp, lt, ri))
                        if len(pend) > 2:
                            att_pv(pend.pop(0))
                while pend:
                    att_pv(pend.pop(0))
                for pg in range(2):
                    sl = load_w(G_PZ + pg)
                    for lt in range(2):
                        bk = proj(hs, lt, sl)
                        silu2(bk, spzT[:, 4 * pg:4 * pg + 4, lt * 128:(lt + 1) * 128], [B_spz], lt)
                flush(True)
                for ch in range(8):
                    g = ch // 2
                    bk = PJ[rot("pj", 2)]
                    for lt in range(2):
                        gt = lb * 2 + lt
                        j = gt - SEQ_T0[sq]
                        J = SEQ_J[sq]
                        srcs = []
                        if j > 0:
                            srcs.append((gt - 1, g * 5 + 0))
                        srcs.append((gt, g * 5 + (3 if j == 0 else (4 if j == J - 1 else 1))))
                        if j < J - 1:
                            srcs.append((gt + 1, g * 5 + 2))
                        for si, (sg, bi) in enumerate(srcs):
                            srs, slt = rslot_of(sg), sg % 2
                            MM(psb[bk][:, lt * 128:(lt + 1) * 128], uu[:, srs, slt, ch * 128:(ch + 1) * 128], band[:, bi, :],
                               si == 0, si == len(srcs) - 1, [B_uu[srs][slt], C_band], [B_ps[bk]])
                    if ch % 2 == 0:
                        ACT(dT[:, ch, :], psb[bk][:, 0:256], AF.Copy, [B_ps[bk]], [B_dT])
                    else:
                        CP("dve", dT[:, ch, :], psb[bk][:, 0:256], [B_ps[bk]], [B_dT])
                for oc in range(8):
                    g = oc // 2
                    bk = PJ[rot("pj", 2)]
                    for ic in range(2):
                        MM(psb[bk][:, 0:256], pw_sb[:, g, ic, (oc % 2) * 128:(oc % 2 + 1) * 128], dT[:, 2 * g + ic, :],
                           ic == 0, ic == 1, [B_dT, C_pw], [B_ps[bk]])
                    STT("dve", pool_y[:, oc, :], psb[bk][:, 0:256], pscT[:, oc:oc + 1], spzT[:, oc, :], ALU.mult, ALU.mult,
                        [B_ps[bk], C_psc, B_spz], [B_py])
                if nb < NBLK:
                    frontend_load(nb)
                ayb = [[B_ay[c][lt] for c in range(8)] for lt in range(2)]
                for mg in range(4):
                    slp = load_w(G_GP + mg)
                    tg = []
                    for lt in range(2):
                        bgp = proj(hs, lt, slp)
                        ia = rot("fa", 3)
                        ACT(fa[:, ia, :], psb[bgp][:], AF.Tanh, [B_ps[bgp]], [B_fa[ia]], scale=0.5)
                        tg.append(ia)
                    slu = load_w(G_UP + mg)
                    t1 = []
                    for lt in range(2):
                        bpu = proj(hs, lt, slu, nk=8, k0=0,
                                   lhs_fn=lambda kc: pool_y[:, kc, lt * 128:(lt + 1) * 128], lhs_bufs=[B_py])
                        ib = rot("fb", 2)
                        STT("dve", fb[:, ib, :], fa[:, tg[lt], :], 1.0, psb[bpu][:], ALU.add, ALU.mult,
                            [B_fa[tg[lt]], B_ps[bpu]], [B_fb[ib]])
                        t1.append(ib)
                    sla = load_w(G_GA + mg)
                    tg2 = []
                    for lt in range(2):
                        bga = proj(hs, lt, sla)
                        ia2 = rot("fa", 3)
                        ACT(fa[:, ia2, :], psb[bga][:], AF.Tanh, [B_ps[bga]], [B_fa[ia2]], scale=0.5)
                        tg2.append(ia2)
                    for lt in range(2):
                        bau = proj(hs, lt, slu, nk=8, k0=8,
                                   lhs_fn=lambda kc: attn_y[:, kc, lt * 128:(lt + 1) * 128], lhs_bufs=ayb[lt])
                        ia2 = tg2[lt]
                        STT("dve", fa[:, ia2, :], fa[:, ia2, :], 1.0, psb[bau][:], ALU.add, ALU.mult,
                            [B_fa[ia2], B_ps[bau]], [B_fa[ia2]])
                        ic = rot("bfs", 3)
                        TT("pool", bfs[:, ic, :], fb[:, t1[lt], :], fa[:, ia2, :], ALU.add, [B_fb[t1[lt]], B_fa[ia2]], [B_bfs[ic]])
                        transpose4(bfs[:, ic, :], B_bfs[ic], mT[:, 4 * mg:4 * mg + 4, lt * 128:(lt + 1) * 128], [B_mT[lt]], lt)
                flush(True)
                fe_items = []
                if nb < NBLK:
                    fe_prep(nb, 0)
                    fe_items = [[lambda: fe_round(nb, 0, 0)], [lambda: fe_round(nb, 0, 1)], [lambda: fe_round(nb, 0, 2)],
                                [lambda: fe_round(nb, 0, 3), lambda: fe_prep(nb, 1)], [],
                                [lambda: fe_round(nb, 1, 0)], [lambda: fe_round(nb, 1, 1)],
                                [lambda: fe_round(nb, 1, 2), lambda: fe_round(nb, 1, 3)]]
                for ngp in range(4):
                    sl = load_w(G_WO + ngp)
                    gi = rot("gq", 2)
                    DMA("act", gq[:, gi, :], bass.AP(ada_dd, sq * 3 * D + 2 * D + ngp * 512, [[0, 128], [1, 512]]),
                        S_gq[gi], [B_adad], [B_gq[gi]])
                    for lt in range(2):
                        gt = lb * 2 + lt
                        xi = rot("xr", 2)
                        DMA("act", xr[:, xi, :], x_d.ap()[gt * 128:(gt + 1) * 128, ngp * 512:(ngp + 1) * 512], S_xr[xi],
                            [], [B_xr[xi]])
                        bk = proj(hs, lt, sl, lhs_fn=lambda kc: mT[:, kc, lt * 128:(lt + 1) * 128], lhs_bufs=[B_mT[lt]])
                        ia = rot("fa", 3)
                        STT("dve", fa[:, ia, :], psb[bk][:], 0.25, gq[:, gi, :], ALU.mult, ALU.mult,
                            [B_ps[bk], B_gq[gi]], [B_fa[ia]])
                        TT("pool", xr[:, xi, :], fa[:, ia, :], xr[:, xi, :], ALU.add, [B_fa[ia], B_xr[xi]], [B_xr[xi]])
                        DMA("pool", y_d.ap()[gt * 128:(gt + 1) * 128, ngp * 512:(ngp + 1) * 512], xr[:, xi, :], S_yo[xi],
                            [B_xr[xi]], [B_yo[xi]])
                        if fe_items:
                            for f_ in fe_items.pop(0):
                                f_()

            if KLEVEL < 6:
                raise _Stop
            frontend_load(0)
            frontend_tile(0, 0)
            frontend_tile(0, 1)
            for s in range(NBLK + 1):
                if s < NBLK:
                    if KLEVEL < 10 + 2 * s:
                        raise _Stop
                    ahead_proj(s)
                if s == 0:
                    frontend_load(1)
                    frontend_tile(1, 0)
                    frontend_tile(1, 1)
                    build_rtable()
                flush(True)
                if s >= 1:
                    if KLEVEL < 10 + 2 * s + 1:
                        raise _Stop
                    lagged(s - 1)
        try:
            body()
        except _Stop:
            pass
        P.finish("sp")
        P.finish("pool")
        P.emit()
    return nc


_CACHE = {}


def kernel(x_prompt, x_sample, c_prompt, c_sample, w_ada, b_ada, norm_g, w_in, pool_w, pool_scale,
           q_norm_g, k_norm_g, rpb, w_pool_up, w_attn_up, w_o):
    f32 = np.float32
    xp = np.asarray(x_prompt, f32)
    xs = np.asarray(x_sample, f32)
    cp = np.asarray(c_prompt, f32)
    cs = np.asarray(c_sample, f32)
    if "nc" not in _CACHE:
        _CACHE["nc"] = build_program()
        _CACHE["consts"] = host_consts()
    nc = _CACHE["nc"]
    consts = _CACHE["consts"]
    shared = {
        "w_ada": np.ascontiguousarray(np.asarray(w_ada, f32)[0]),
        "b_ada": np.ascontiguousarray(np.asarray(b_ada, f32)[0].reshape(1, 3 * D)),
        "norm_g": np.ascontiguousarray(np.asarray(norm_g, f32)[0].reshape(16, 128)),
        "w_in": np.ascontiguousarray(np.asarray(w_in, f32)[0]),
        "pool_w": np.ascontiguousarray(np.asarray(pool_w, f32)[0]),
        "pool_scale": np.ascontiguousarray(np.asarray(pool_scale, f32)[0].reshape(8, 128)),
        "q_norm_g": np.ascontiguousarray(np.asarray(q_norm_g, f32)[0].reshape(1, 64)),
        "k_norm_g": np.ascontiguousarray(np.asarray(k_norm_g, f32)[0].reshape(1, 64)),
        "rpb": np.ascontiguousarray(np.asarray(rpb, f32)[0]),
        "w_pool_up": np.ascontiguousarray(np.asarray(w_pool_up, f32)[0]),
        "w_attn_up": np.ascontiguousarray(np.asarray(w_attn_up, f32)[0]),
        "w_o": np.ascontiguousarray(np.asarray(w_o, f32)[0]),
    }
    shared.update(consts)
    in_maps = []
    for c in range(8):
        xc = np.concatenate([xp[2 * c], xp[2 * c + 1], xs[c]], axis=0)
        cc = np.stack([cp[2 * c], cp[2 * c + 1], cs[c]], axis=0)
        m = dict(shared)
        m["x"] = np.ascontiguousarray(xc)
        m["c3"] = np.ascontiguousarray(cc)
        in_maps.append(m)
    res = run_bass_kernel_spmd(nc, in_maps, core_ids=list(range(8)))
    y_prompt = np.empty((16, 2048, D), f32)
    y_sample = np.empty((8, 4096, D), f32)
    for c in range(8):
        y = np.asarray(res.results[c]["y"], f32)
        y_prompt[2 * c] = y[0:2048]
        y_prompt[2 * c + 1] = y[2048:4096]
        y_sample[c] = y[4096:8192]
    return (y_prompt, y_sample)
```

```python
import numpy as np
import ml_dtypes
from contextlib import ExitStack
import concourse.bass as bass
import concourse.mybir as mybir
from concourse.bass_utils import run_bass_kernel_spmd

F32 = mybir.dt.float32
BF16 = mybir.dt.bfloat16
ALU = mybir.AluOpType
AF = mybir.ActivationFunctionType
AX = mybir.AxisListType

D = 2048
NTILE = 64
NBLK = 32
SEQ_T0 = (0, 16, 32)
SEQ_J = (16, 16, 32)
EPS = 1e-6
POOL_WINDOWS = (2, 4, 8, 16)
G_U, G_PZ, G_Q, G_K, G_V, G_AZ, G_GP, G_GA, G_WO, G_UP = 0, 2, 4, 6, 8, 10, 12, 16, 20, 24
NGRP = 28

ENGS = ("pe", "act", "dve", "pool", "sp")


class Buf:
    __slots__ = ("name", "w", "r")

    def __init__(self, name):
        self.name = name
        self.w = None
        self.r = []


class Prog:
    def __init__(self, nc):
        self.nc = nc
        self.ops = {e: [] for e in ENGS}
        self.cnt = {e: 0 for e in ENGS}
        self.seen = {e: {} for e in ENGS}
        self.dma_sems = {}

    def new_sem(self, name):
        self.dma_sems[name] = 0
        return name

    def _collect(self, eng, reads, writes, is_dma):
        seen = self.seen[eng]
        best = {}

        def add(ev):
            sk, val, e2 = ev
            if seen.get(sk, 0) >= val:
                return
            if best.get(sk, 0) < val:
                best[sk] = val

        for b in reads:
            if b.w is not None:
                if is_dma or not (b.w[2] == eng and eng == "pe"):
                    add(b.w)
        strict = is_dma or eng != "pe"
        for b in writes:
            if b.w is not None and (strict or b.w[2] != eng):
                add(b.w)
            for ev in b.r:
                if strict or ev[2] != eng:
                    add(ev)
        waits = []
        for sk, val in best.items():
            seen[sk] = val
            waits.append((sk, val))
        return waits

    def op(self, eng, fn, reads=(), writes=()):
        waits = self._collect(eng, reads, writes, False)
        self.cnt[eng] += 1
        ev = (eng, self.cnt[eng], eng)
        self.ops[eng].append((waits, fn, (eng, 1)))
        for b in reads:
            b.r.append(ev)
        for b in writes:
            b.w = ev
            b.r = []
        return ev

    def dma(self, eng, fn, semkey, reads=(), writes=()):
        waits = self._collect(eng, reads, writes, True)
        self.dma_sems[semkey] += 16
        ev = (semkey, self.dma_sems[semkey], "dma")
        self.ops[eng].append((waits, fn, (semkey, 16)))
        for b in reads:
            b.r.append(ev)
        for b in writes:
            b.w = ev
            b.r = []
        return ev

    def wait_all(self, eng, bufs):
        seen = self.seen[eng]
        best = {}
        for b in bufs:
            evs = list(b.r)
            if b.w is not None:
                evs.append(b.w)
            for (sk, val, e2) in evs:
                if seen.get(sk, 0) >= val:
                    continue
                if best.get(sk, 0) < val:
                    best[sk] = val
        waits = []
        for sk, val in best.items():
            seen[sk] = val
            waits.append((sk, val))
        self.ops[eng].append((waits, None, None))

    def finish(self, eng="sp"):
        waits = []
        for e in ENGS:
            if self.cnt[e] > 0:
                waits.append((e, self.cnt[e]))
        for k, v in self.dma_sems.items():
            if v > 0:
                waits.append((k, v))
        self.ops[eng].append((waits, None, None))

    def emit(self):
        nc = self.nc
        with ExitStack() as st:
            sems = {}
            for e in ENGS:
                sems[e] = st.enter_context(nc.semaphore("s_" + e))
            for k in self.dma_sems:
                sems[k] = st.enter_context(nc.semaphore("s_" + k))
            block = st.enter_context(nc.Block())
            handles = {"pe": block.tensor, "act": block.scalar, "dve": block.vector,
                       "pool": block.gpsimd, "sp": block.sync}
            for e in ENGS:
                ops = self.ops[e]
                if not ops:
                    continue

                def body(h, ops=ops):
                    for (waits, fn, inc) in ops:
                        for (sk, val) in waits:
                            h.wait_ge(sems[sk], val)
                        if fn is None:
                            continue
                        fn(h).then_inc(sems[inc[0]], inc[1])

                handles[e](body)


def bc_last(ap, n):
    return bass.AP(ap.tensor, ap.offset, [list(x) for x in ap.ap] + [[0, n]])


def seq_of_tile(gt):
    return 0 if gt < 16 else (1 if gt < 32 else 2)


def host_consts():
    bf = ml_dtypes.bfloat16
    ident = np.eye(128, dtype=np.float32)
    band = np.zeros((128, 20, 128), np.float64)
    for g, w in enumerate(POOL_WINDOWS):
        h = w // 2
        prev = band[:, g * 5 + 0, :]
        cur = band[:, g * 5 + 1, :]
        nxt = band[:, g * 5 + 2, :]
        first = band[:, g * 5 + 3, :]
        last = band[:, g * 5 + 4, :]
        for t in range(128):
            lo, hi = t - h, t + h - 1
            for tp in range(lo, hi + 1):
                if tp < 0:
                    prev[128 + tp, t] += 1.0 / w
                elif tp > 127:
                    nxt[tp - 128, t] += 1.0 / w
                else:
                    cur[tp, t] += 1.0 / w
            cur[t, t] -= 1.0
            lo2 = max(lo, 0)
            cnt = hi - lo2 + 1
            for tp in range(lo2, min(hi, 127) + 1):
                first[tp, t] += 1.0 / cnt
            first[t, t] -= 1.0
            hi2 = min(hi, 127)
            cnt = hi2 - lo + 1
            for tp in range(max(lo, 0), hi2 + 1):
                last[tp, t] += 1.0 / cnt
            last[t, t] -= 1.0
    maskr = np.zeros((128, 16, 64), np.float32)
    for p in range(128):
        krl, kc = p // 64, p % 64
        for cp in range(64):
            c = 63 - cp
            cs = min(max(c - 8, 0), 48)
            if cs <= kc < cs + 16:
                for i in range(16):
                    ok = (i <= 14) if krl == 0 else (i >= 1)
                    if ok:
                        maskr[p, i, cp] = 1.0
    return {
        "ident_bf": ident.astype(bf),
        "ident_f": ident,
        "band": band.astype(np.float32).astype(bf),
        "maskr": maskr.reshape(128, 1024),
    }


import os
KLEVEL = int(os.environ.get("KLEVEL", "99"))
KSUB = int(os.environ.get("KSUB", "99"))


class _Stop(Exception):
    pass


def build_program():
    nc = bass.Bass("TRN2", target_bir_lowering=False)
    P = Prog(nc)

    def din(name, shape, dt=F32):
        return nc.dram_tensor(name, list(shape), dt, kind="ExternalInput")

    x_d = din("x", [NTILE * 128, D])
    c_d = din("c3", [3, D])
    wada_d = din("w_ada", [D, 3 * D])
    bada_d = din("b_ada", [1, 3 * D])
    ng_d = din("norm_g", [16, 128])
    win_d = din("w_in", [D, 10240])
    pw_d = din("pool_w", [4, 256, 256])
    psc_d = din("pool_scale", [8, 128])
    qg_d = din("q_norm_g", [1, 64])
    kg_d = din("k_norm_g", [1, 64])
    rpb_d = din("rpb", [16, 15, 31])
    wpu_d = din("w_pool_up", [1024, D])
    wau_d = din("w_attn_up", [1024, D])
    wo_d = din("w_o", [D, D])
    identbf_d = din("ident_bf", [128, 128], BF16)
    identf_d = din("ident_f", [128, 128])
    band_d = din("band", [128, 20, 128], BF16)
    maskr_d = din("maskr", [128, 1024])
    y_d = nc.dram_tensor("y", [NTILE * 128, D], F32, kind="ExternalOutput")
    wsc_d = nc.dram_tensor("wsc", [NGRP, 128, 8192], BF16, kind="Internal")
    ada_dd = nc.dram_tensor("ada_s", [3, 3 * D], F32, kind="Internal")
    rp2_d = nc.dram_tensor("rp2", [16, 15, 160], F32, kind="Internal")
    rtab_d = nc.dram_tensor("rtab", [8, 128, 2048], BF16, kind="Internal")

    with ExitStack() as st:
        def sb(name, shape, dt):
            return st.enter_context(nc.sbuf_tensor("sb_" + name, list(shape), dt))

        hT = sb("hT", [128, 2, 16, 256], BF16)
        kT = sb("kT", [128, 3, 8, 256], BF16)
        vv = sb("vv", [128, 3, 2, 1024], BF16)
        uu = sb("uu", [128, 3, 2, 1024], BF16)
        wst = sb("wst", [128, 3, 16, 512], BF16)
        xt = sb("xt", [128, 2, 2048], F32)
        xnb = sb("xnb", [128, 2048], BF16)
        qT = sb("qT", [128, 2, 4, 256], BF16)
        sazT = sb("sazT", [128, 2, 4, 256], BF16)
        rt = sb("rt", [128, 2, 2, 16, 64], BF16)
        attn_y = sb("attn_y", [128, 8, 256], BF16)
        pool_y = sb("pool_y", [128, 8, 256], BF16)
        mT = sb("mT", [128, 16, 256], BF16)
        dT = sb("dT", [128, 8, 256], BF16)
        spzT = sb("spzT", [128, 8, 256], BF16)
        fa = sb("fa", [128, 3, 512], F32)
        fb = sb("fb", [128, 2, 512], F32)
        bfs = sb("bfs", [128, 3, 512], BF16)
        eb = sb("eb", [128, 2, 512], BF16)
        exb = sb("exb", [64, 2, 64], BF16)
        pt = sb("pt", [128, 6, 512], BF16)
        ptx = sb("ptx", [64, 6, 64], BF16)
        rd = sb("rd", [128, 2, 128], F32)
        wgt = sb("wgt", [128, 2, 128], F32)
        xr = sb("xr", [128, 2, 512], F32)
        gq = sb("gq", [128, 2, 512], F32)
        band = sb("band", [128, 20, 128], BF16)
        pw_sb = sb("pw_sb", [128, 4, 2, 256], BF16)
        gqbc = sb("gqbc", [128, 512], F32)
        gkbc = sb("gkbc", [128, 512], F32)
        ident = sb("ident", [128, 128], BF16)
        identf = sb("identf", [128, 128], F32)
        ones = sb("ones", [128, 64], BF16)
        g64 = sb("g64", [128, 2, 64], F32)
        gsT = sb("gsT", [128, 3, 16], F32)
        shT = sb("shT", [128, 3, 16], F32)
        pscT = sb("pscT", [128, 8], F32)
        ngT = sb("ngT", [128, 16], F32)
        ssx = sb("ssx", [128, 4], F32)
        ss8 = sb("ss8", [128, 4, 8], F32)
        nhalf = sb("nhalf", [128, 8], F32)
        sT = sb("sT", [128, 48], F32)
        small48 = gq[0:48, 0, 0:384].rearrange("p (a n) -> p a n", n=128)
        small16 = gq[0:16, 1, 0:256].rearrange("p (a n) -> p a n", n=128)
        adab = fb[0:3, :, :]
        badab = fa[0:3, 2, :]

        psb = [st.enter_context(nc.psum_tensor("ps%d" % i, [128, 512], F32)) for i in range(8)]
        trps = [psb[2].bitcast(BF16), psb[3].bitcast(BF16)]
        PJ = [0, 1]
        S_BANK = [4, 5]
        OD_BANK = [6, 7]

        B_ps = [Buf("ps%d" % i) for i in range(8)]
        B_trp = [B_ps[2], B_ps[3]]
        B_hT = [[Buf("hT%d_%d" % (s, t)) for t in range(2)] for s in range(2)]
        B_kT = [[Buf("kT%d_%d" % (s, t)) for t in range(2)] for s in range(3)]
        B_vv = [[Buf("vv%d_%d" % (s, t)) for t in range(2)] for s in range(3)]
        B_uu = [[Buf("uu%d_%d" % (s, t)) for t in range(2)] for s in range(3)]
        B_wst = [Buf("wst%d" % i) for i in range(3)]
        B_xt = [Buf("xt0"), Buf("xt1")]
        B_xnb = Buf("xnb")
        B_qT = [Buf("qT0"), Buf("qT1")]
        B_saz = [Buf("saz0"), Buf("saz1")]
        B_rt = [Buf("rt0"), Buf("rt1")]
        B_ay = [[Buf("ay%d_%d" % (c, t)) for t in range(2)] for c in range(8)]
        B_py = Buf("py")
        B_mT = [Buf("mT0"), Buf("mT1")]
        B_dT = Buf("dT")
        B_spz = Buf("spz")
        B_fa = [Buf("fa%d" % i) for i in range(3)]
        B_fb = [Buf("fb%d" % i) for i in range(2)]
        B_bfs = [Buf("bfs%d" % i) for i in range(3)]
        B_eb = [Buf("eb0"), Buf("eb1")]
        B_exb = [Buf("exb0"), Buf("exb1")]
        B_pt = [Buf("pt%d" % i) for i in range(6)]
        B_ptx = [Buf("ptx%d" % i) for i in range(6)]
        B_rd = [Buf("rd0"), Buf("rd1")]
        B_wgt = [Buf("wgt0"), Buf("wgt1")]
        B_xr = [Buf("xr0"), Buf("xr1")]
        B_gq = [Buf("gq0"), Buf("gq1")]
        B_const = Buf("const")
        C_ident = Buf("c_ident"); C_identf = Buf("c_identf"); C_band = Buf("c_band"); C_g64 = Buf("c_g64")
        C_ones = Buf("c_ones"); C_nhalf = Buf("c_nhalf"); C_gq = Buf("c_gqbc"); C_gk = Buf("c_gkbc")
        C_pw = Buf("c_pw"); C_ng = Buf("c_ng"); C_psc = Buf("c_psc"); C_gs = Buf("c_gs"); C_sh = Buf("c_sh")
        C_sT = Buf("c_sT"); C_s48 = [Buf("c_s48_%d" % i) for i in range(3)]; C_s16 = [Buf("c_s16_0"), Buf("c_s16_1")]
        C_badab = Buf("c_badab"); C_adab = [Buf("c_adab0"), Buf("c_adab1")]
        B_yo = [Buf("yo0"), Buf("yo1")]
        B_ss = Buf("ss")
        B_ss8 = [Buf("ss8_%d" % i) for i in range(4)]
        B_wsc = [Buf("wsc%d" % g) for g in range(NGRP)]
        B_ada = Buf("ada")
        B_adad = Buf("adad")
        B_rp2 = Buf("rp2")
        B_rtab = Buf("rtab")
        B_y = Buf("y")
        B_misc = Buf("misc")

        ctr = {"fa": 0, "fb": 0, "bfs": 0, "pj": 0, "trp": 0, "eb": 0, "pt": 0, "rd": 0,
               "xr": 0, "gq": 0, "S": 0, "OD": 0, "wst": 0, "ss8": 0, "xt": 0, "rt": 0}

        def rot(name, n):
            i = ctr[name] % n
            ctr[name] += 1
            return i

        S_wst = [P.new_sem("wst%d" % i) for i in range(3)]
        S_xt = [P.new_sem("xt%d" % i) for i in range(2)]
        S_xr = [P.new_sem("xr%d" % i) for i in range(2)]
        S_gq = [P.new_sem("gq%d" % i) for i in range(2)]
        S_rt = [P.new_sem("rt%d" % i) for i in range(2)]
        S_yo = [P.new_sem("yo0"), P.new_sem("yo1")]

        def MM(out, lhsT, rhs, start, stop, r, w, tp=None):
            if tp is None:
                P.op("pe", lambda e: e.matmul(out, lhsT=lhsT, rhs=rhs, start=start, stop=stop), r, w)
            else:
                P.op("pe", lambda e: e.matmul(out, lhsT=lhsT, rhs=rhs, start=start, stop=stop,
                                              tile_position=tp), r, w)

        def TR(out, in_, idn, r, w):
            P.op("pe", lambda e: e.transpose(out, in_, idn), r, w)

        def ACT(out, in_, func, r, w, scale=1.0, bias=None, accum=None):
            kw = {}
            if bias is not None:
                kw["bias"] = bias
            if accum is not None:
                kw["accum_out"] = accum
            P.op("act", lambda e: e.activation(out=out, in_=in_, func=func, scale=scale, **kw), r, w)

        def TS(eng, out, in0, s1, s2, op0, op1, r, w):
            if s2 is None:
                P.op(eng, lambda e: e.tensor_scalar(out=out, in0=in0, scalar1=s1, scalar2=None, op0=op0), r, w)
            else:
                P.op(eng, lambda e: e.tensor_scalar(out=out, in0=in0, scalar1=s1, scalar2=s2, op0=op0, op1=op1), r, w)

        def TT(eng, out, in0, in1, op, r, w):
            P.op(eng, lambda e: e.tensor_tensor(out=out, in0=in0, in1=in1, op=op), r, w)

        def STT(eng, out, in0, scalar, in1, op0, op1, r, w):
            P.op(eng, lambda e: e.scalar_tensor_tensor(out=out, in0=in0, scalar=scalar, in1=in1, op0=op0, op1=op1), r, w)

        def CP(eng, out, in_, r, w):
            P.op(eng, lambda e: e.tensor_copy(out=out, in_=in_), r, w)

        def MSET(eng, ap, val, r, w):
            P.op(eng, lambda e: e.memset(ap, val), r, w)

        nsem = [0]

        def fresh():
            nsem[0] += 1
            return P.new_sem("one%d" % nsem[0])

        def DMAn(q, out, in_, r, w, sem=None):
            if sem is None:
                sem = fresh()
            return P.dma(q, lambda e: e.dma_start(out=out, in_=in_), sem, r, w)

        def DMAG(items):
            sem = fresh()
            ev = None
            ws = []
            for (q, out, in_, r, w) in items:
                ev = DMAn(q, out, in_, r, w, sem=sem)
                ws.extend(w)
            for b in ws:
                b.w = ev

        def DMA(q, out, in_, sem, r, w, slow=False):
            if slow:
                P.dma(q, lambda e: e.dma_start(out=out, in_=in_, allow_slow_non_contiguous=True), sem, r, w)
            else:
                P.dma(q, lambda e: e.dma_start(out=out, in_=in_), sem, r, w)

        def body():
            cast_gate = []

            def cast_group(g, src_ap_2d, kchunks, dst_k0, sem):
                src = src_ap_2d.rearrange("(kc p) n -> p kc n", p=128)
                dst = wsc_d.ap()[g].rearrange("p (kc n) -> p kc n", n=512)
                step = 8
                ev = None
                for k0 in range(0, kchunks, step):
                    ev = DMAn("pool", dst[:, dst_k0 + k0:dst_k0 + k0 + step, :], src[:, k0:k0 + step, :], list(cast_gate), [Buf("dmy")], sem=sem)
                return ev

            def win_cols(g):
                return win_d.ap()[:, g * 512:(g + 1) * 512]

            def cast_phase(groups):
                sem = fresh()
                ev = None
                for g in groups:
                    ev = cast_group(g, win_cols(g), 16, 0, sem)
                for g in groups:
                    B_wsc[g].w = ev

            cast_phase([G_K, G_K + 1])

            def late_casts():
                cast_phase([G_V, G_V + 1, G_U, G_U + 1])
                cast_phase([G_Q, G_AZ])
                cast_phase([G_Q + 1, G_AZ + 1])
                cast_phase([G_PZ, G_PZ + 1])
                for mg in range(4):
                    sem = fresh()
                    cast_group(G_GP + mg, win_cols(G_GP + mg), 16, 0, sem)
                    cast_group(G_UP + mg, wpu_d.ap()[:, mg * 512:(mg + 1) * 512], 8, 0, sem)
                    cast_group(G_UP + mg, wau_d.ap()[:, mg * 512:(mg + 1) * 512], 8, 8, sem)
                    ev = cast_group(G_GA + mg, win_cols(G_GA + mg), 16, 0, sem)
                    for g in (G_GP + mg, G_UP + mg, G_GA + mg):
                        B_wsc[g].w = ev
                sem = fresh()
                ev = None
                for ngp in range(4):
                    ev = cast_group(G_WO + ngp, wo_d.ap()[:, ngp * 512:(ngp + 1) * 512], 16, 0, sem)
                for ngp in range(4):
                    B_wsc[G_WO + ngp].w = ev
            DMAn("pool", pw_sb[:], pw_d.ap().rearrange("g (ic p) n -> p g ic n", p=128), [], [C_pw])

            if KLEVEL < 1:
                raise _Stop
            DMAG([("sp", ident[:], identbf_d.ap(), [], [C_ident]),
                  ("sp", identf[:], identf_d.ap(), [], [C_identf]),
                  ("sp", band[:], band_d.ap(), [], [C_band]),
                  ("sp", g64[:, 0, :], bass.AP(qg_d, 0, [[0, 128], [1, 64]]), [], [C_g64]),
                  ("sp", g64[:, 1, :], bass.AP(kg_d, 0, [[0, 128], [1, 64]]), [], [C_g64])])
            MSET("dve", ones[:], 1.0, [], [C_ones])
            MSET("dve", nhalf[:], -0.5, [], [C_nhalf])
            a0 = g64[:, 0, :]
            a1 = g64[:, 1, :]
            TS("dve", gqbc[:].rearrange("p (h d) -> p h d", d=64),
               bass.AP(a0.tensor, a0.offset, [list(a0.ap[0]), [0, 8], [1, 64]]), 1.0, None, ALU.mult, None, [C_g64], [C_gq])
            TS("dve", gkbc[:].rearrange("p (h d) -> p h d", d=64),
               bass.AP(a1.tensor, a1.offset, [list(a1.ap[0]), [0, 8], [1, 64]]), 8.0, None, ALU.mult, None, [C_g64], [C_gk])

            if KLEVEL < 2:
                raise _Stop
            DMAG([("sp", small48[:, 0, :], c_d.ap().rearrange("b (kc p) -> (b kc) p", p=128), [], [C_s48[0]]),
                  ("sp", small16[:, 0, :], ng_d.ap(), [], [C_s16[0]]),
                  ("sp", small16[0:8, 1, :], psc_d.ap(), [], [C_s16[1]])])
            pT = psb[0]
            TR(pT[:, 0:48], small48[:, 0, :], identf[0:48, 0:48], [C_s48[0], C_identf], [B_ps[0]])
            TR(pT[:, 48:64], small16[:, 0, :], identf[0:16, 0:16], [C_s16[0], C_identf], [B_ps[0]])
            TR(pT[:, 64:72], small16[0:8, 1, :], identf[0:8, 0:8], [C_s16[1], C_identf], [B_ps[0]])
            cTt = fa[:, 0, 0:48]
            CP("dve", cTt, pT[:, 0:48], [B_ps[0]], [B_fa[0]])
            CP("dve", ngT[:], pT[:, 48:64], [B_ps[0]], [C_ng])
            CP("dve", pscT[:], pT[:, 64:72], [B_ps[0]], [C_psc])
            tht = fa[:, 1, 0:48]
            ACT(tht, cTt, AF.Tanh, [B_fa[0]], [B_fa[1]], scale=0.5)
            STT("dve", sT[:], tht, 1.0, cTt, ALU.add, ALU.mult, [B_fa[0], B_fa[1]], [C_sT])

            if KLEVEL < 3:
                raise _Stop
            wst_f = [wst[:, i].rearrange("p a n -> p (a n)").bitcast(F32) for i in range(3)]
            sT3 = sT[:].rearrange("p (b k) -> p b k", k=16)
            B_adad_p = []
            S_badab = fresh()
            S_adst = [fresh(), fresh()]
            for half in range(2):
                banks = [0, 1, 2, 4, 5, 6]
                for kc in range(16):
                    sl = rot("wst", 3)
                    DMA("sp", wst_f[sl][:, 0:3072], wada_d.ap()[kc * 128:(kc + 1) * 128, half * 3072:(half + 1) * 3072],
                        S_wst[sl], [], [B_wst[sl]])
                    for n in range(6):
                        MM(psb[banks[n]][0:3, :], sT3[:, :, kc], wst_f[sl][:, n * 512:(n + 1) * 512], kc == 0, kc == 15,
                           [C_sT, B_wst[sl]], [B_ps[banks[n]]])
                for n in range(6):
                    col = half * 3072 + n * 512
                    ab = n % 2
                    DMAn("sp", badab, bass.AP(bada_d, col, [[0, 3], [1, 512]]), [], [C_badab], sem=S_badab)
                    STT("dve", adab[:, ab, :], psb[banks[n]][0:3, :], 0.5, badab, ALU.mult, ALU.add,
                        [B_ps[banks[n]], C_badab], [C_adab[ab]])
                    bp = Buf("adad%d" % col)
                    B_adad_p.append(bp)
                    DMAn("sp", ada_dd.ap()[:, col:col + 512], adab[:, ab, :], [C_adab[ab]], [bp], sem=S_adst[ab])
            gate_b = Buf("castgate")
            gate_b.w = B_wst[(ctr["wst"] - 1) % 3].w
            cast_gate.append(gate_b)
            late_casts()
            del cast_gate[:]
            for bp in B_adad_p:
                sk = bp.w[0]
                bp.w = (sk, P.dma_sems[sk], "dma")
            DMAG([("sp", small48[b * 16:(b + 1) * 16, 1, :], bass.AP(ada_dd, b * 3 * D, [[128, 16], [1, 128]]), B_adad_p, [C_s48[1]])
                  for b in range(3)])
            DMAG([("sp", small48[b * 16:(b + 1) * 16, 2, :], bass.AP(ada_dd, b * 3 * D + D, [[128, 16], [1, 128]]), B_adad_p, [C_s48[2]])
                  for b in range(3)])
            TR(pT[:, 0:48], small48[:, 1, :], identf[0:48, 0:48], [C_s48[1], C_identf], [B_ps[0]])
            TR(pT[:, 48:96], small48[:, 2, :], identf[0:48, 0:48], [C_s48[2], C_identf], [B_ps[0]])
            CP("dve", shT[:].rearrange("p b k -> p (b k)"), pT[:, 0:48], [B_ps[0]], [C_sh])
            for b in range(3):
                STT("dve", gsT[:, b, :], pT[:, 48 + 16 * b:64 + 16 * b], 1.0, ngT[:], ALU.add, ALU.mult,
                    [B_ps[0], C_ng], [C_gs])
            B_adad = Buf("adad_all")
            for bp in B_adad_p:
                if bp.w is not None:
                    B_adad.r.append(bp.w)

            B_rtab_h = [Buf("rtab%d" % h) for h in range(16)]

            def build_rtable():
                if KLEVEL < 5:
                    raise _Stop
                xt_flat = xt[:].rearrange("p a n -> p (a n)")
                rpt = xt_flat[0:16, 0:2400].rearrange("p (a b) -> p a b", b=160)
                rtmp = xt_flat[0:16, 2400:2400 + 465].rearrange("p (a b) -> p a b", b=31)
                MSET("pool", rpt, 0.0, [], [B_xt[0], B_xt[1]])
                DMAn("sp", rtmp, rpb_d.ap(), [], [B_xt[0], B_xt[1]])
                CP("pool", rpt[:, :, 64:95], rtmp[:, ::-1, :], [B_xt[0], B_xt[1]], [B_xt[0], B_xt[1]])
                DMAn("sp", rp2_d.ap(), rpt, [B_xt[0], B_xt[1]], [B_rp2])
                mT_f = mT[:].rearrange("p a b -> p (a b)").bitcast(F32)
                ay_f = attn_y[:].rearrange("p a b -> p (a b)").bitcast(F32)
                py_f = pool_y[:].rearrange("p a b -> p (a b)").bitcast(F32)
                qT_f = qT[:].rearrange("p a b c -> p (a b c)").bitcast(F32)
                dT_fl = dT[:].rearrange("p a b -> p (a b)")
                sp_fl = spzT[:].rearrange("p a b -> p (a b)")
                maskt = qT_f[:, 0:1024]
                DMAn("sp", maskt, maskr_d.ap(), [], [B_qT[0], B_qT[1]])
                MSET("pool", mT_f[:, 0:2048], 0.0, [], [B_mT[0], B_mT[1]])
                S_rb = [fresh(), fresh()]
                B_est = [Buf("est0"), B_py]
                B_rb = [B_dT, B_spz]
                S_stg = [P.new_sem("stg0"), P.new_sem("stg1")]
                ests = [ay_f[:, 0:1024], py_f[:, 0:1024]]
                rbufs = [dT_fl[:, 0:1024], sp_fl[:, 0:1024]]
                for h in range(16):
                    s_ = h % 2
                    stg = mT_f[:, s_ * 1024:(s_ + 1) * 1024]
                    stg3 = stg.rearrange("p (i c) -> p i c", c=64)
                    src = bass.AP(rp2_d, h * 15 * 160 + 16, [[1, 64], [160, 15], [1, 64]])
                    DMA("sp", stg3[0:64, 0:15, :], src, S_stg[s_], [B_rp2], [B_mT[s_]])
                    dmy = Buf("dmy")
                    ev_ = P.dma("sp", (lambda o_, i_: (lambda e: e.dma_start(out=o_, in_=i_)))(stg3[64:128, 1:16, :], src),
                                S_stg[s_], [B_rp2], [dmy])
                    B_mT[s_].w = ev_
                    ACT(ests[s_], stg, AF.Exp, [B_mT[s_]], [B_est[s_]])
                    TT("dve", rbufs[s_], ests[s_], maskt, ALU.mult, [B_est[s_], B_qT[0], B_qT[1]], [B_rb[s_]])
                    DMAn("sp", rtab_d.ap()[h // 2][:, (h % 2) * 1024:(h % 2 + 1) * 1024], rbufs[s_], [B_rb[s_]], [B_rtab_h[h]], sem=S_rb[s_])
                for h in range(16):
                    B_rtab_h[h].w = (S_rb[h % 2], P.dma_sems[S_rb[h % 2]], "dma")
                for c_ in range(8):
                    for t_ in range(2):
                        B_ay[c_][t_].r.extend(B_est[0].r)
                        if B_est[0].w is not None:
                            B_ay[c_][t_].r.append(B_est[0].w)

            for cb in C_s48 + C_s16:
                for gb_ in B_gq:
                    gb_.r.extend(cb.r)
                    if cb.w is not None:
                        gb_.r.append(cb.w)
            def load_w(g):
                sl = rot("wst", 3)
                DMA("sp", wst[:, sl].rearrange("p a n -> p (a n)"), wsc_d.ap()[g], S_wst[sl], [B_wsc[g]], [B_wst[sl]])
                return sl

            def proj(hslot, lt, sl, nk=16, k0=0, lhs_fn=None, lhs_bufs=None):
                flush()
                gen[0] += 1
                bk = PJ[rot("pj", 2)]
                for kc in range(nk):
                    if lhs_fn is None:
                        lhsT = hT[:, hslot, kc, lt * 128:(lt + 1) * 128]
                        rb = [B_hT[hslot][lt]]
                    else:
                        lhsT = lhs_fn(kc)
                        rb = lhs_bufs
                    MM(psb[bk][:], lhsT, wst[:, sl, k0 + kc, :], kc == 0, kc == nk - 1, rb + [B_wst[sl]], [B_ps[bk]])
                return bk

            deferred = []
            gen = [0]

            def flush(all_=False):
                keep = []
                for (g_, f_) in deferred:
                    if all_ or g_ <= gen[0] - 2:
                        f_()
                    else:
                        keep.append((g_, f_))
                deferred[:] = keep

            def transpose4(src_bf, src_buf, dst_ap, dst_bufs, eng_i):
                deferred.append((gen[0], lambda: transpose4_now(src_bf, src_buf, dst_ap, dst_bufs, eng_i)))

            def transpose4_now(src_bf, src_buf, dst_ap, dst_bufs, eng_i):
                th = rot("trp", 2)
                trp = trps[th]
                for c in range(4):
                    TR(trp[:, c * 128:(c + 1) * 128], src_bf[:, c * 128:(c + 1) * 128], ident[:],
                       [src_buf, C_ident], [B_trp[th]])
                src = trp[:, 0:512].rearrange("p (c t) -> p c t", t=128)
                if eng_i % 2 == 0:
                    ACT(dst_ap, src, AF.Copy, [B_trp[th]], dst_bufs)
                else:
                    CP("dve", dst_ap, src, [B_trp[th]], dst_bufs)

            def headnorm(bk, gbc, gbuf, dst_ap, dst_bufs, eng_i):
                ia = rot("fa", 3)
                ib = rot("fb", 2)
                ic = rot("bfs", 3)
                i8 = rot("ss8", 4)
                ACT(fa[:, ia, :], psb[bk][:], AF.Copy, [B_ps[bk]], [B_fa[ia]])
                ACT(fb[:, ib, :], fa[:, ia, :], AF.Square, [B_fa[ia]], [B_fb[ib]])
                P.op("dve", lambda e: e.tensor_reduce(out=ss8[:, i8, :], in_=fb[:, ib, :].rearrange("p (h d) -> p h d", d=64),
                                                      axis=AX.X, op=ALU.add), [B_fb[ib]], [B_ss8[i8]])
                TS("dve", ss8[:, i8, :], ss8[:, i8, :], 64.0 * EPS, None, ALU.add, None, [B_ss8[i8]], [B_ss8[i8]])
                TT("pool", ss8[:, i8, :], ss8[:, i8, :], nhalf[:], ALU.pow, [B_ss8[i8], C_nhalf], [B_ss8[i8]])
                TT("dve", fb[:, ib, :].rearrange("p (h d) -> p h d", d=64),
                   fa[:, ia, :].rearrange("p (h d) -> p h d", d=64), bc_last(ss8[:, i8, :], 64), ALU.mult,
                   [B_fa[ia], B_ss8[i8]], [B_fb[ib]])
                TT("pool", bfs[:, ic, :], fb[:, ib, :], gbc[:], ALU.mult, [B_fb[ib], gbuf], [B_bfs[ic]])
                transpose4(bfs[:, ic, :], B_bfs[ic], dst_ap, dst_bufs, eng_i)

            def silu2(bk, dst_ap, dst_bufs, eng_i):
                ia = rot("fa", 3)
                ic = rot("bfs", 3)
                ACT(fa[:, ia, :], psb[bk][:], AF.Tanh, [B_ps[bk]], [B_fa[ia]], scale=0.5)
                STT("dve", bfs[:, ic, :], fa[:, ia, :], 1.0, psb[bk][:], ALU.add, ALU.mult, [B_fa[ia], B_ps[bk]], [B_bfs[ic]])
                transpose4(bfs[:, ic, :], B_bfs[ic], dst_ap, dst_bufs, eng_i)

            def rslot_of(gt):
                return (gt // 2) % 3

            xt_of = {}

            def frontend_load(gb):
                for lt in range(2):
                    gt = gb * 2 + lt
                    xi = rot("xt", 2)
                    xt_of[gt] = xi
                    DMA("sp", xt[:, xi, :], x_d.ap()[gt * 128:(gt + 1) * 128, :], S_xt[xi], [], [B_xt[xi]])

            def fe_prep(gb, lt):
                gt = gb * 2 + lt
                xi = xt_of[gt]
                col = gt % 4
                MSET("dve", ssx[:, col:col + 1], 0.0, [], [B_ss])
                ACT(xnb[:], xt[:, xi, :], AF.Square, [B_xt[xi], B_ss], [B_xnb, B_ss], accum=ssx[:, col:col + 1])
                TS("dve", ssx[:, col:col + 1], ssx[:, col:col + 1], 1.0 / D, EPS, ALU.mult, ALU.add, [B_ss], [B_ss])
                TT("pool", ssx[:, col:col + 1], ssx[:, col:col + 1], nhalf[:, 0:1], ALU.pow, [B_ss, C_nhalf], [B_ss])
                ACT(xnb[:], xt[:, xi, :], AF.Copy, [B_xt[xi], B_ss], [B_xnb], scale=ssx[:, col:col + 1])

            def fe_round(gb, lt, q4):
                hs = gb % 2
                gt = gb * 2 + lt
                sq = seq_of_tile(gt)
                th = rot("trp", 2)
                for c in range(4):
                    kc = q4 * 4 + c
                    TR(trps[th][:, c * 128:(c + 1) * 128], xnb[:, kc * 128:(kc + 1) * 128], ident[:],
                       [B_xnb, C_ident], [B_trp[th]])
                for c in range(4):
                    kc = q4 * 4 + c
                    src = trps[th][:, c * 128:(c + 1) * 128]
                    dst = hT[:, hs, kc, lt * 128:(lt + 1) * 128]
                    if c % 2 == 0:
                        ACT(dst, src, AF.Identity, [B_trp[th], C_gs, C_sh], [B_hT[hs][lt]],
                            scale=gsT[:, sq, kc:kc + 1], bias=shT[:, sq, kc:kc + 1])
                    else:
                        TS("dve", dst, src, gsT[:, sq, kc:kc + 1], shT[:, sq, kc:kc + 1], ALU.mult, ALU.add,
                           [B_trp[th], C_gs, C_sh], [B_hT[hs][lt]])

            def frontend_tile(gb, lt):
                fe_prep(gb, lt)
                for q4 in range(4):
                    fe_round(gb, lt, q4)

            def ahead_proj(gb):
                hs = gb % 2
                rs = gb % 3
                for kg in range(2):
                    sl = load_w(G_K + kg)
                    for lt in range(2):
                        bk = proj(hs, lt, sl)
                        dst = kT[:, rs, 4 * kg:4 * kg + 4, lt * 128:(lt + 1) * 128]
                        headnorm(bk, gkbc, C_gk, dst, [B_kT[rs][lt]], lt)
                for (gbase, ring, Bring) in ((G_V, vv, B_vv), (G_U, uu, B_uu)):
                    for g2 in range(2):
                        sl = load_w(gbase + g2)
                        for lt in range(2):
                            bk = proj(hs, lt, sl)
                            dst = ring[:, rs, lt, g2 * 512:(g2 + 1) * 512]
                            if lt == 0:
                                ACT(dst, psb[bk][:], AF.Copy, [B_ps[bk]], [Bring[rs][lt]])
                            else:
                                CP("dve", dst, psb[bk][:], [B_ps[bk]], [Bring[rs][lt]])

            S4 = [0, 1, 2, 3, 4, 5]

            def att_qk(lb, sq, hp, lt, ri):
                qb, c4 = hp // 4, hp % 4
                gt = lb * 2 + lt
                j = gt - SEQ_T0[sq]
                J = SEQ_J[sq]
                if j == 0:
                    olist, i0, interior = [3, 2, 1, 0], 1, False
                elif j == 1:
                    olist, i0, interior = [2, 1, 0, -1], 3, False
                elif j == J - 2:
                    olist, i0, interior = [1, 0, -1, -2], 5, False
                elif j == J - 1:
                    olist, i0, interior = [0, -1, -2, -3], 7, False
                else:
                    olist, i0, interior = [1, 0, -1, -2], 5, True
                odb = OD_BANK[rot("OD", 2)]
                OD = psb[odb]
                sbis = [S4[rot("S", 6)], S4[rot("S", 6)]]
                for a, o in enumerate(olist):
                    gk = gt + o
                    krs, klt = rslot_of(gk), gk % 2
                    for hh in range(2):
                        p0 = 64 * hh
                        MM(psb[sbis[hh]][:, a * 128:(a + 1) * 128], kT[p0:p0 + 64, krs, hp, klt * 128:(klt + 1) * 128],
                           qT[p0:p0 + 64, qb, c4, lt * 128:(lt + 1) * 128], True, True,
                           [B_kT[krs][klt], B_qT[qb]], [B_ps[sbis[hh]]], tp=(p0, 0))
                if interior:
                    gk2 = gt + 2
                    krs2, klt2 = rslot_of(gk2), gk2 % 2
                    for hh in range(2):
                        p0 = 64 * hh
                        MM(psb[sbis[hh]][0:64, 448:512], kT[p0:p0 + 64, krs2, hp, klt2 * 128:klt2 * 128 + 64],
                           qT[p0:p0 + 64, qb, c4, lt * 128 + 64:(lt + 1) * 128], True, True,
                           [B_kT[krs2][klt2], B_qT[qb]], [B_ps[sbis[hh]]], tp=(p0, 0))
                pts = []
                for hh in range(2):
                    h = 2 * hp + hh
                    p0 = 64 * hh
                    sbi = sbis[hh]
                    S = psb[sbi]
                    ei = rot("eb", 2)
                    ACT(eb[:, ei, :], S[:], AF.Exp, [B_ps[sbi]], [B_eb[ei]])
                    pi = rot("pt", 6)
                    TT("dve", pt[:, pi, :].rearrange("p (i c) -> p i c", c=64),
                       eb[:, ei, :].rearrange("p (i c) -> p i c", c=64),
                       rt[:, ri, hh, i0:i0 + 8, ::-1], ALU.mult, [B_eb[ei], B_rt[ri]], [B_pt[pi]])
                    if interior:
                        TT("dve", ptx[:, pi, :], eb[0:64, ei, 448:512], rt[0:64, ri, hh, 4, ::-1], ALU.mult,
                           [B_eb[ei], B_rt[ri]], [B_ptx[pi]])
                        MSET("dve", pt[0:64, pi, 448:512], 0.0, [], [B_pt[pi]])
                    pts.append((pi, hh, h, p0))
                return (gt, hp, lt, qb, c4, olist, interior, odb, pts)

            def att_pv(st_):
                (gt, hp, lt, qb, c4, olist, interior, odb, pts) = st_
                OD = psb[odb]
                n = len(olist)
                for (pi, hh, h, p0) in pts:
                    for a, o in enumerate(olist):
                        gk = gt + o
                        krs, klt = rslot_of(gk), gk % 2
                        last = (a == n - 1) and not interior
                        MM(OD[p0:p0 + 64, 0:128], vv[:, krs, klt, h * 64:(h + 1) * 64], pt[:, pi, a * 128:(a + 1) * 128],
                           a == 0, last, [B_vv[krs][klt], B_pt[pi]], [B_ps[odb]], tp=(0, p0))
                    if interior:
                        gk = gt + 2
                        krs, klt = rslot_of(gk), gk % 2
                        MM(OD[p0:p0 + 64, 64:128], vv[0:64, krs, klt, h * 64:(h + 1) * 64], ptx[:, pi, :],
                           False, True, [B_vv[krs][klt], B_ptx[pi]], [B_ps[odb]], tp=(0, p0))
                    for a, o in enumerate(olist):
                        last = (a == n - 1) and not interior
                        MM(OD[p0:p0 + 64, 128:256], ones[:, 0:64], pt[:, pi, a * 128:(a + 1) * 128],
                           a == 0, last, [C_ones, B_pt[pi]], [B_ps[odb]], tp=(0, p0))
                    if interior:
                        MM(OD[p0:p0 + 64, 192:256], ones[0:64, 0:64], ptx[:, pi, :],
                           False, True, [C_ones, B_ptx[pi]], [B_ps[odb]], tp=(0, p0))
                di = rot("rd", 2)
                P.op("dve", lambda e: e.reciprocal(out=rd[:, di, :], in_=OD[:, 128:256]), [B_ps[odb]], [B_rd[di]])
                TT("dve", wgt[:, di, :], rd[:, di, :], sazT[:, qb, c4, lt * 128:(lt + 1) * 128], ALU.mult,
                   [B_rd[di], B_saz[qb]], [B_wgt[di]])
                TT("dve", attn_y[:, hp, lt * 128:(lt + 1) * 128], OD[:, 0:128], wgt[:, di, :], ALU.mult,
                   [B_ps[odb], B_wgt[di]], [B_ay[hp][lt]])

            def lagged(lb):
                hs = lb % 2
                sq = seq_of_tile(lb * 2)
                nb = lb + 2
                rt_next = []

                def rt_load(hp_):
                    ri_ = rot("rt", 2)
                    DMA("pool", rt[:, ri_].rearrange("p a i c -> p (a i c)"), rtab_d.ap()[hp_], S_rt[ri_],
                        [B_rtab_h[2 * hp_], B_rtab_h[2 * hp_ + 1]], [B_rt[ri_]])
                    rt_next.append(ri_)

                rt_load(0)
                for qg in range(2):
                    qb = qg
                    sl = load_w(G_Q + qg)
                    for lt in range(2):
                        bk = proj(hs, lt, sl)
                        headnorm(bk, gqbc, C_gq, qT[:, qb, :, lt * 128:(lt + 1) * 128], [B_qT[qb]], lt)
                    sl = load_w(G_AZ + qg)
                    for lt in range(2):
                        bk = proj(hs, lt, sl)
                        silu2(bk, sazT[:, qb, :, lt * 128:(lt + 1) * 128], [B_saz[qb]], lt + 1)
                flush(True)
                pend = []
                for hp in range(8):
                    ri = rt_next.pop(0)
                    if hp + 1 < 8:
                        rt_load(hp + 1)
                    for lt in range(2):
                        pend.append(att_qk(lb, sq, hp, lt, ri))
                        if len(pend) > 2:
                            att_pv(pend.pop(0))
                while pend:
                    att_pv(pend.pop(0))
                for pg in range(2):
                    sl = load_w(G_PZ + pg)
                    for lt in range(2):
                        bk = proj(hs, lt, sl)
                        silu2(bk, spzT[:, 4 * pg:4 * pg + 4, lt * 128:(lt + 1) * 128], [B_spz], lt)
                flush(True)
                for ch in range(8):
                    g = ch // 2
                    bk = PJ[rot("pj", 2)]
                    for lt in range(2):
                        gt = lb * 2 + lt
                        j = gt - SEQ_T0[sq]
                        J = SEQ_J[sq]
                        srcs = []
                        if j > 0:
                            srcs.append((gt - 1, g * 5 + 0))
                        srcs.append((gt, g * 5 + (3 if j == 0 else (4 if j == J - 1 else 1))))
                        if j < J - 1:
                            srcs.append((gt + 1, g * 5 + 2))
                        for si, (sg, bi) in enumerate(srcs):
                            srs, slt = rslot_of(sg), sg % 2
                            MM(psb[bk][:, lt * 128:(lt + 1) * 128], uu[:, srs, slt, ch * 128:(ch + 1) * 128], band[:, bi, :],
                               si == 0, si == len(srcs) - 1, [B_uu[srs][slt], C_band], [B_ps[bk]])
                    if ch % 2 == 0:
                        ACT(dT[:, ch, :], psb[bk][:, 0:256], AF.Copy, [B_ps[bk]], [B_dT])
                    else:
                        CP("dve", dT[:, ch, :], psb[bk][:, 0:256], [B_ps[bk]], [B_dT])
                for oc in range(8):
                    g = oc // 2
                    bk = PJ[rot("pj", 2)]
                    for ic in range(2):
                        MM(psb[bk][:, 0:256], pw_sb[:, g, ic, (oc % 2) * 128:(oc % 2 + 1) * 128], dT[:, 2 * g + ic, :],
                           ic == 0, ic == 1, [B_dT, C_pw], [B_ps[bk]])
                    STT("dve", pool_y[:, oc, :], psb[bk][:, 0:256], pscT[:, oc:oc + 1], spzT[:, oc, :], ALU.mult, ALU.mult,
                        [B_ps[bk], C_psc, B_spz], [B_py])
                if nb < NBLK:
                    frontend_load(nb)
                ayb = [[B_ay[c][lt] for c in range(8)] for lt in range(2)]
                for mg in range(4):
                    slp = load_w(G_GP + mg)
                    tg = []
                    for lt in range(2):
                        bgp = proj(hs, lt, slp)
                        ia = rot("fa", 3)
                        ACT(fa[:, ia, :], psb[bgp][:], AF.Tanh, [B_ps[bgp]], [B_fa[ia]], scale=0.5)
                        tg.append(ia)
                    slu = load_w(G_UP + mg)
                    t1 = []
                    for lt in range(2):
                        bpu = proj(hs, lt, slu, nk=8, k0=0,
                                   lhs_fn=lambda kc: pool_y[:, kc, lt * 128:(lt + 1) * 128], lhs_bufs=[B_py])
                        ib = rot("fb", 2)
                        STT("dve", fb[:, ib, :], fa[:, tg[lt], :], 1.0, psb[bpu][:], ALU.add, ALU.mult,
                            [B_fa[tg[lt]], B_ps[bpu]], [B_fb[ib]])
                        t1.append(ib)
                    sla = load_w(G_GA + mg)
                    tg2 = []
                    for lt in range(2):
                        bga = proj(hs, lt, sla)
                        ia2 = rot("fa", 3)
                        ACT(fa[:, ia2, :], psb[bga][:], AF.Tanh, [B_ps[bga]], [B_fa[ia2]], scale=0.5)
                        tg2.append(ia2)
                    for lt in range(2):
                        bau = proj(hs, lt, slu, nk=8, k0=8,
                                   lhs_fn=lambda kc: attn_y[:, kc, lt * 128:(lt + 1) * 128], lhs_bufs=ayb[lt])
                        ia2 = tg2[lt]
                        STT("dve", fa[:, ia2, :], fa[:, ia2, :], 1.0, psb[bau][:], ALU.add, ALU.mult,
                            [B_fa[ia2], B_ps[bau]], [B_fa[ia2]])
                        ic = rot("bfs", 3)
                        TT("pool", bfs[:, ic, :], fb[:, t1[lt], :], fa[:, ia2, :], ALU.add, [B_fb[t1[lt]], B_fa[ia2]], [B_bfs[ic]])
                        transpose4(bfs[:, ic, :], B_bfs[ic], mT[:, 4 * mg:4 * mg + 4, lt * 128:(lt + 1) * 128], [B_mT[lt]], lt)
                flush(True)
                fe_items = []
                if nb < NBLK:
                    fe_prep(nb, 0)
                    fe_items = [[lambda: fe_round(nb, 0, 0)], [lambda: fe_round(nb, 0, 1)], [lambda: fe_round(nb, 0, 2)],
                                [lambda: fe_round(nb, 0, 3), lambda: fe_prep(nb, 1)], [],
                                [lambda: fe_round(nb, 1, 0)], [lambda: fe_round(nb, 1, 1)],
                                [lambda: fe_round(nb, 1, 2), lambda: fe_round(nb, 1, 3)]]
                for ngp in range(4):
                    sl = load_w(G_WO + ngp)
                    gi = rot("gq", 2)
                    DMA("sp", gq[:, gi, :], bass.AP(ada_dd, sq * 3 * D + 2 * D + ngp * 512, [[0, 128], [1, 512]]),
                        S_gq[gi], [B_adad], [B_gq[gi]])
                    for lt in range(2):
                        gt = lb * 2 + lt
                        xi = rot("xr", 2)
                        DMA("sp", xr[:, xi, :], x_d.ap()[gt * 128:(gt + 1) * 128, ngp * 512:(ngp + 1) * 512], S_xr[xi],
                            [], [B_xr[xi]])
                        bk = proj(hs, lt, sl, lhs_fn=lambda kc: mT[:, kc, lt * 128:(lt + 1) * 128], lhs_bufs=[B_mT[lt]])
                        ia = rot("fa", 3)
                        STT("dve", fa[:, ia, :], psb[bk][:], 0.25, gq[:, gi, :], ALU.mult, ALU.mult,
                            [B_ps[bk], B_gq[gi]], [B_fa[ia]])
                        TT("pool", xr[:, xi, :], fa[:, ia, :], xr[:, xi, :], ALU.add, [B_fa[ia], B_xr[xi]], [B_xr[xi]])
                        DMA("pool", y_d.ap()[gt * 128:(gt + 1) * 128, ngp * 512:(ngp + 1) * 512], xr[:, xi, :], S_yo[xi],
                            [B_xr[xi]], [B_yo[xi]])
                        if fe_items:
                            for f_ in fe_items.pop(0):
                                f_()

            if KLEVEL < 6:
                raise _Stop
            frontend_load(0)
            frontend_tile(0, 0)
            frontend_tile(0, 1)
            for s in range(NBLK + 1):
                if s < NBLK:
                    if KLEVEL < 10 + 2 * s:
                        raise _Stop
                    ahead_proj(s)
                if s == 0:
                    frontend_load(1)
                    frontend_tile(1, 0)
                    frontend_tile(1, 1)
                    build_rtable()
                flush(True)
                if s >= 1:
                    if KLEVEL < 10 + 2 * s + 1:
                        raise _Stop
                    lagged(s - 1)
        try:
            body()
        except _Stop:
            pass
        P.finish("sp")
        P.finish("pool")
        P.emit()
    return nc


_CACHE = {}


def kernel(x_prompt, x_sample, c_prompt, c_sample, w_ada, b_ada, norm_g, w_in, pool_w, pool_scale,
           q_norm_g, k_norm_g, rpb, w_pool_up, w_attn_up, w_o):
    f32 = np.float32
    xp = np.asarray(x_prompt, f32)
    xs = np.asarray(x_sample, f32)
    cp = np.asarray(c_prompt, f32)
    cs = np.asarray(c_sample, f32)
    if "nc" not in _CACHE:
        _CACHE["nc"] = build_program()
        _CACHE["consts"] = host_consts()
    nc = _CACHE["nc"]
    consts = _CACHE["consts"]
    shared = {
        "w_ada": np.ascontiguousarray(np.asarray(w_ada, f32)[0]),
        "b_ada": np.ascontiguousarray(np.asarray(b_ada, f32)[0].reshape(1, 3 * D)),
        "norm_g": np.ascontiguousarray(np.asarray(norm_g, f32)[0].reshape(16, 128)),
        "w_in": np.ascontiguousarray(np.asarray(w_in, f32)[0]),
        "pool_w": np.ascontiguousarray(np.asarray(pool_w, f32)[0]),
        "pool_scale": np.ascontiguousarray(np.asarray(pool_scale, f32)[0].reshape(8, 128)),
        "q_norm_g": np.ascontiguousarray(np.asarray(q_norm_g, f32)[0].reshape(1, 64)),
        "k_norm_g": np.ascontiguousarray(np.asarray(k_norm_g, f32)[0].reshape(1, 64)),
        "rpb": np.ascontiguousarray(np.asarray(rpb, f32)[0]),
        "w_pool_up": np.ascontiguousarray(np.asarray(w_pool_up, f32)[0]),
        "w_attn_up": np.ascontiguousarray(np.asarray(w_attn_up, f32)[0]),
        "w_o": np.ascontiguousarray(np.asarray(w_o, f32)[0]),
    }
    shared.update(consts)
    in_maps = []
    for c in range(8):
        xc = np.concatenate([xp[2 * c], xp[2 * c + 1], xs[c]], axis=0)
        cc = np.stack([cp[2 * c], cp[2 * c + 1], cs[c]], axis=0)
        m = dict(shared)
        m["x"] = np.ascontiguousarray(xc)
        m["c3"] = np.ascontiguousarray(cc)
        in_maps.append(m)
    res = run_bass_kernel_spmd(nc, in_maps, core_ids=list(range(8)))
    y_prompt = np.empty((16, 2048, D), f32)
    y_sample = np.empty((8, 4096, D), f32)
    for c in range(8):
        y = np.asarray(res.results[c]["y"], f32)
        y_prompt[2 * c] = y[0:2048]
        y_prompt[2 * c + 1] = y[2048:4096]
        y_sample[c] = y[4096:8192]
    return (y_prompt, y_sample)
```

```python
import numpy as np
import ml_dtypes
from contextlib import ExitStack
import concourse.bass as bass
import concourse.mybir as mybir
from concourse.bass_utils import run_bass_kernel_spmd

F32 = mybir.dt.float32
BF16 = mybir.dt.bfloat16
ALU = mybir.AluOpType
AF = mybir.ActivationFunctionType
AX = mybir.AxisListType

D = 2048
NTILE = 64
NBLK = 32
SEQ_T0 = (0, 16, 32)
SEQ_J = (16, 16, 32)
EPS = 1e-6
POOL_WINDOWS = (2, 4, 8, 16)
G_U, G_PZ, G_Q, G_K, G_V, G_AZ, G_GP, G_GA, G_WO, G_UP = 0, 2, 4, 6, 8, 10, 12, 16, 20, 24
NGRP = 28

ENGS = ("pe", "act", "dve", "pool", "sp")


class Buf:
    __slots__ = ("name", "w", "r")

    def __init__(self, name):
        self.name = name
        self.w = None
        self.r = []


class Prog:
    def __init__(self, nc):
        self.nc = nc
        self.ops = {e: [] for e in ENGS}
        self.cnt = {e: 0 for e in ENGS}
        self.seen = {e: {} for e in ENGS}
        self.dma_sems = {}

    def new_sem(self, name):
        self.dma_sems[name] = 0
        return name

    def _collect(self, eng, reads, writes, is_dma):
        seen = self.seen[eng]
        best = {}

        def add(ev):
            sk, val, e2 = ev
            if seen.get(sk, 0) >= val:
                return
            if best.get(sk, 0) < val:
                best[sk] = val

        for b in reads:
            if b.w is not None:
                if is_dma or not (b.w[2] == eng and eng == "pe"):
                    add(b.w)
        strict = is_dma or eng != "pe"
        for b in writes:
            if b.w is not None and (strict or b.w[2] != eng):
                add(b.w)
            for ev in b.r:
                if strict or ev[2] != eng:
                    add(ev)
        waits = []
        for sk, val in best.items():
            seen[sk] = val
            waits.append((sk, val))
        return waits

    def op(self, eng, fn, reads=(), writes=()):
        waits = self._collect(eng, reads, writes, False)
        self.cnt[eng] += 1
        ev = (eng, self.cnt[eng], eng)
        self.ops[eng].append((waits, fn, (eng, 1)))
        for b in reads:
            b.r.append(ev)
        for b in writes:
            b.w = ev
            b.r = []
        return ev

    def dma(self, eng, fn, semkey, reads=(), writes=()):
        waits = self._collect(eng, reads, writes, True)
        self.dma_sems[semkey] += 16
        ev = (semkey, self.dma_sems[semkey], "dma")
        self.ops[eng].append((waits, fn, (semkey, 16)))
        for b in reads:
            b.r.append(ev)
        for b in writes:
            b.w = ev
            b.r = []
        return ev

    def wait_all(self, eng, bufs):
        seen = self.seen[eng]
        best = {}
        for b in bufs:
            evs = list(b.r)
            if b.w is not None:
                evs.append(b.w)
            for (sk, val, e2) in evs:
                if seen.get(sk, 0) >= val:
                    continue
                if best.get(sk, 0) < val:
                    best[sk] = val
        waits = []
        for sk, val in best.items():
            seen[sk] = val
            waits.append((sk, val))
        self.ops[eng].append((waits, None, None))

    def finish(self, eng="sp"):
        waits = []
        for e in ENGS:
            if self.cnt[e] > 0:
                waits.append((e, self.cnt[e]))
        for k, v in self.dma_sems.items():
            if v > 0:
                waits.append((k, v))
        self.ops[eng].append((waits, None, None))

    def emit(self):
        nc = self.nc
        with ExitStack() as st:
            sems = {}
            for e in ENGS:
                sems[e] = st.enter_context(nc.semaphore("s_" + e))
            for k in self.dma_sems:
                sems[k] = st.enter_context(nc.semaphore("s_" + k))
            block = st.enter_context(nc.Block())
            handles = {"pe": block.tensor, "act": block.scalar, "dve": block.vector,
                       "pool": block.gpsimd, "sp": block.sync}
            for e in ENGS:
                ops = self.ops[e]
                if not ops:
                    continue

                def body(h, ops=ops):
                    for (waits, fn, inc) in ops:
                        for (sk, val) in waits:
                            h.wait_ge(sems[sk], val)
                        if fn is None:
                            continue
                        fn(h).then_inc(sems[inc[0]], inc[1])

                handles[e](body)


def bc_last(ap, n):
    return bass.AP(ap.tensor, ap.offset, [list(x) for x in ap.ap] + [[0, n]])


def seq_of_tile(gt):
    return 0 if gt < 16 else (1 if gt < 32 else 2)


def host_consts():
    bf = ml_dtypes.bfloat16
    ident = np.eye(128, dtype=np.float32)
    band = np.zeros((128, 20, 128), np.float64)
    for g, w in enumerate(POOL_WINDOWS):
        h = w // 2
        prev = band[:, g * 5 + 0, :]
        cur = band[:, g * 5 + 1, :]
        nxt = band[:, g * 5 + 2, :]
        first = band[:, g * 5 + 3, :]
        last = band[:, g * 5 + 4, :]
        for t in range(128):
            lo, hi = t - h, t + h - 1
            for tp in range(lo, hi + 1):
                if tp < 0:
                    prev[128 + tp, t] += 1.0 / w
                elif tp > 127:
                    nxt[tp - 128, t] += 1.0 / w
                else:
                    cur[tp, t] += 1.0 / w
            cur[t, t] -= 1.0
            lo2 = max(lo, 0)
            cnt = hi - lo2 + 1
            for tp in range(lo2, min(hi, 127) + 1):
                first[tp, t] += 1.0 / cnt
            first[t, t] -= 1.0
            hi2 = min(hi, 127)
            cnt = hi2 - lo + 1
            for tp in range(max(lo, 0), hi2 + 1):
                last[tp, t] += 1.0 / cnt
            last[t, t] -= 1.0
    maskr = np.zeros((128, 16, 64), np.float32)
    for p in range(128):
        krl, kc = p // 64, p % 64
        for cp in range(64):
            c = 63 - cp
            cs = min(max(c - 8, 0), 48)
            if cs <= kc < cs + 16:
                for i in range(16):
                    ok = (i <= 14) if krl == 0 else (i >= 1)
                    if ok:
                        maskr[p, i, cp] = 1.0
    return {
        "ident_bf": ident.astype(bf),
        "ident_f": ident,
        "band": band.astype(np.float32).astype(bf),
        "maskr": maskr.reshape(128, 1024),
    }


import os
KLEVEL = int(os.environ.get("KLEVEL", "99"))
KSUB = int(os.environ.get("KSUB", "99"))


class _Stop(Exception):
    pass


def build_program():
    nc = bass.Bass("TRN2", target_bir_lowering=False)
    P = Prog(nc)

    def din(name, shape, dt=F32):
        return nc.dram_tensor(name, list(shape), dt, kind="ExternalInput")

    x_d = din("x", [NTILE * 128, D])
    c_d = din("c3", [3, D])
    wada_d = din("w_ada", [D, 3 * D])
    bada_d = din("b_ada", [1, 3 * D])
    ng_d = din("norm_g", [16, 128])
    win_d = din("w_in", [D, 10240])
    pw_d = din("pool_w", [4, 256, 256])
    psc_d = din("pool_scale", [8, 128])
    qg_d = din("q_norm_g", [1, 64])
    kg_d = din("k_norm_g", [1, 64])
    rpb_d = din("rpb", [16, 15, 31])
    wpu_d = din("w_pool_up", [1024, D])
    wau_d = din("w_attn_up", [1024, D])
    wo_d = din("w_o", [D, D])
    identbf_d = din("ident_bf", [128, 128], BF16)
    identf_d = din("ident_f", [128, 128])
    band_d = din("band", [128, 20, 128], BF16)
    maskr_d = din("maskr", [128, 1024])
    y_d = nc.dram_tensor("y", [NTILE * 128, D], F32, kind="ExternalOutput")
    wsc_d = nc.dram_tensor("wsc", [NGRP, 128, 8192], BF16, kind="Internal")
    ada_dd = nc.dram_tensor("ada_s", [3, 3 * D], F32, kind="Internal")
    rp2_d = nc.dram_tensor("rp2", [16, 15, 160], F32, kind="Internal")
    rtab_d = nc.dram_tensor("rtab", [8, 128, 2048], BF16, kind="Internal")

    with ExitStack() as st:
        def sb(name, shape, dt):
            return st.enter_context(nc.sbuf_tensor("sb_" + name, list(shape), dt))

        hT = sb("hT", [128, 2, 16, 256], BF16)
        kT = sb("kT", [128, 3, 8, 256], BF16)
        vv = sb("vv", [128, 3, 2, 1024], BF16)
        uu = sb("uu", [128, 3, 2, 1024], BF16)
        wst = sb("wst", [128, 3, 16, 512], BF16)
        xt = sb("xt", [128, 2, 2048], F32)
        xnb = sb("xnb", [128, 2048], BF16)
        qT = sb("qT", [128, 2, 4, 256], BF16)
        sazT = sb("sazT", [128, 2, 4, 256], BF16)
        rt = sb("rt", [128, 2, 2, 16, 64], BF16)
        attn_y = sb("attn_y", [128, 8, 256], BF16)
        pool_y = sb("pool_y", [128, 8, 256], BF16)
        mT = sb("mT", [128, 16, 256], BF16)
        dT = sb("dT", [128, 8, 256], BF16)
        spzT = sb("spzT", [128, 8, 256], BF16)
        fa = sb("fa", [128, 3, 512], F32)
        fb = sb("fb", [128, 2, 512], F32)
        bfs = sb("bfs", [128, 3, 512], BF16)
        eb = sb("eb", [128, 2, 512], BF16)
        exb = sb("exb", [64, 2, 64], BF16)
        pt = sb("pt", [128, 6, 512], BF16)
        ptx = sb("ptx", [64, 6, 64], BF16)
        rd = sb("rd", [128, 2, 128], F32)
        wgt = sb("wgt", [128, 2, 128], F32)
        xr = sb("xr", [128, 2, 512], F32)
        gq = sb("gq", [128, 2, 512], F32)
        band = sb("band", [128, 20, 128], BF16)
        pw_sb = sb("pw_sb", [128, 4, 2, 256], BF16)
        gqbc = sb("gqbc", [128, 512], F32)
        gkbc = sb("gkbc", [128, 512], F32)
        ident = sb("ident", [128, 128], BF16)
        identf = sb("identf", [128, 128], F32)
        ones = sb("ones", [128, 64], BF16)
        g64 = sb("g64", [128, 2, 64], F32)
        gsT = sb("gsT", [128, 3, 16], F32)
        shT = sb("shT", [128, 3, 16], F32)
        pscT = sb("pscT", [128, 8], F32)
        ngT = sb("ngT", [128, 16], F32)
        ssx = sb("ssx", [128, 4], F32)
        ss8 = sb("ss8", [128, 4, 8], F32)
        nhalf = sb("nhalf", [128, 8], F32)
        sT = sb("sT", [128, 48], F32)
        small48 = gq[0:48, 0, 0:384].rearrange("p (a n) -> p a n", n=128)
        small16 = gq[0:16, 1, 0:256].rearrange("p (a n) -> p a n", n=128)
        adab = fb[0:3, :, :]
        badab = fa[0:3, 2, :]

        psb = [st.enter_context(nc.psum_tensor("ps%d" % i, [128, 512], F32)) for i in range(8)]
        trps = [psb[2].bitcast(BF16), psb[3].bitcast(BF16)]
        PJ = [0, 1]
        S_BANK = [4, 5]
        OD_BANK = [6, 7]

        B_ps = [Buf("ps%d" % i) for i in range(8)]
        B_trp = [B_ps[2], B_ps[3]]
        B_hT = [[Buf("hT%d_%d" % (s, t)) for t in range(2)] for s in range(2)]
        B_kT = [[Buf("kT%d_%d" % (s, t)) for t in range(2)] for s in range(3)]
        B_vv = [[Buf("vv%d_%d" % (s, t)) for t in range(2)] for s in range(3)]
        B_uu = [[Buf("uu%d_%d" % (s, t)) for t in range(2)] for s in range(3)]
        B_wst = [Buf("wst%d" % i) for i in range(3)]
        B_xt = [Buf("xt0"), Buf("xt1")]
        B_xnb = Buf("xnb")
        B_qT = [Buf("qT0"), Buf("qT1")]
        B_saz = [Buf("saz0"), Buf("saz1")]
        B_rt = [Buf("rt0"), Buf("rt1")]
        B_ay = [[Buf("ay%d_%d" % (c, t)) for t in range(2)] for c in range(8)]
        B_py = Buf("py")
        B_mT = [Buf("mT0"), Buf("mT1")]
        B_dT = Buf("dT")
        B_spz = Buf("spz")
        B_fa = [Buf("fa%d" % i) for i in range(3)]
        B_fb = [Buf("fb%d" % i) for i in range(2)]
        B_bfs = [Buf("bfs%d" % i) for i in range(3)]
        B_eb = [Buf("eb0"), Buf("eb1")]
        B_exb = [Buf("exb0"), Buf("exb1")]
        B_pt = [Buf("pt%d" % i) for i in range(6)]
        B_ptx = [Buf("ptx%d" % i) for i in range(6)]
        B_rd = [Buf("rd0"), Buf("rd1")]
        B_wgt = [Buf("wgt0"), Buf("wgt1")]
        B_xr = [Buf("xr0"), Buf("xr1")]
        B_gq = [Buf("gq0"), Buf("gq1")]
        B_const = Buf("const")
        C_ident = Buf("c_ident"); C_identf = Buf("c_identf"); C_band = Buf("c_band"); C_g64 = Buf("c_g64")
        C_ones = Buf("c_ones"); C_nhalf = Buf("c_nhalf"); C_gq = Buf("c_gqbc"); C_gk = Buf("c_gkbc")
        C_pw = Buf("c_pw"); C_ng = Buf("c_ng"); C_psc = Buf("c_psc"); C_gs = Buf("c_gs"); C_sh = Buf("c_sh")
        C_sT = Buf("c_sT"); C_s48 = [Buf("c_s48_%d" % i) for i in range(3)]; C_s16 = [Buf("c_s16_0"), Buf("c_s16_1")]
        C_badab = Buf("c_badab"); C_adab = [Buf("c_adab0"), Buf("c_adab1")]
        B_yo = [Buf("yo0"), Buf("yo1")]
        B_ss = Buf("ss")
        B_ss8 = [Buf("ss8_%d" % i) for i in range(4)]
        B_wsc = [Buf("wsc%d" % g) for g in range(NGRP)]
        B_ada = Buf("ada")
        B_adad = Buf("adad")
        B_rp2 = Buf("rp2")
        B_rtab = Buf("rtab")
        B_y = Buf("y")
        B_misc = Buf("misc")

        ctr = {"fa": 0, "fb": 0, "bfs": 0, "pj": 0, "trp": 0, "eb": 0, "pt": 0, "rd": 0,
               "xr": 0, "gq": 0, "S": 0, "OD": 0, "wst": 0, "ss8": 0, "xt": 0, "rt": 0}

        def rot(name, n):
            i = ctr[name] % n
            ctr[name] += 1
            return i

        S_wst = [P.new_sem("wst%d" % i) for i in range(3)]
        S_xt = [P.new_sem("xt%d" % i) for i in range(2)]
        S_xr = [P.new_sem("xr%d" % i) for i in range(2)]
        S_gq = [P.new_sem("gq%d" % i) for i in range(2)]
        S_rt = [P.new_sem("rt%d" % i) for i in range(2)]
        S_yo = [P.new_sem("yo0"), P.new_sem("yo1")]

        def MM(out, lhsT, rhs, start, stop, r, w, tp=None):
            if tp is None:
                P.op("pe", lambda e: e.matmul(out, lhsT=lhsT, rhs=rhs, start=start, stop=stop), r, w)
            else:
                P.op("pe", lambda e: e.matmul(out, lhsT=lhsT, rhs=rhs, start=start, stop=stop,
                                              tile_position=tp), r, w)

        def TR(out, in_, idn, r, w):
            P.op("pe", lambda e: e.transpose(out, in_, idn), r, w)

        def ACT(out, in_, func, r, w, scale=1.0, bias=None, accum=None):
            kw = {}
            if bias is not None:
                kw["bias"] = bias
            if accum is not None:
                kw["accum_out"] = accum
            P.op("act", lambda e: e.activation(out=out, in_=in_, func=func, scale=scale, **kw), r, w)

        def TS(eng, out, in0, s1, s2, op0, op1, r, w):
            if s2 is None:
                P.op(eng, lambda e: e.tensor_scalar(out=out, in0=in0, scalar1=s1, scalar2=None, op0=op0), r, w)
            else:
                P.op(eng, lambda e: e.tensor_scalar(out=out, in0=in0, scalar1=s1, scalar2=s2, op0=op0, op1=op1), r, w)

        def TT(eng, out, in0, in1, op, r, w):
            P.op(eng, lambda e: e.tensor_tensor(out=out, in0=in0, in1=in1, op=op), r, w)

        def STT(eng, out, in0, scalar, in1, op0, op1, r, w):
            P.op(eng, lambda e: e.scalar_tensor_tensor(out=out, in0=in0, scalar=scalar, in1=in1, op0=op0, op1=op1), r, w)

        def CP(eng, out, in_, r, w):
            P.op(eng, lambda e: e.tensor_copy(out=out, in_=in_), r, w)

        def MSET(eng, ap, val, r, w):
            P.op(eng, lambda e: e.memset(ap, val), r, w)

        nsem = [0]

        def fresh():
            nsem[0] += 1
            return P.new_sem("one%d" % nsem[0])

        def DMAn(q, out, in_, r, w, sem=None):
            if sem is None:
                sem = fresh()
            return P.dma(q, lambda e: e.dma_start(out=out, in_=in_), sem, r, w)

        def DMAG(items):
            sem = fresh()
            ev = None
            ws = []
            for (q, out, in_, r, w) in items:
                ev = DMAn(q, out, in_, r, w, sem=sem)
                ws.extend(w)
            for b in ws:
                b.w = ev

        def DMA(q, out, in_, sem, r, w, slow=False):
            if slow:
                P.dma(q, lambda e: e.dma_start(out=out, in_=in_, allow_slow_non_contiguous=True), sem, r, w)
            else:
                P.dma(q, lambda e: e.dma_start(out=out, in_=in_), sem, r, w)

        def body():
            cast_gate = []

            def cast_group(g, src_ap_2d, kchunks, dst_k0, sem):
                src = src_ap_2d.rearrange("(kc p) n -> p kc n", p=128)
                dst = wsc_d.ap()[g].rearrange("p (kc n) -> p kc n", n=512)
                step = 8
                ev = None
                for k0 in range(0, kchunks, step):
                    ev = DMAn("pool", dst[:, dst_k0 + k0:dst_k0 + k0 + step, :], src[:, k0:k0 + step, :], list(cast_gate), [Buf("dmy")], sem=sem)
                return ev

            def win_cols(g):
                return win_d.ap()[:, g * 512:(g + 1) * 512]

            def cast_phase(groups):
                sem = fresh()
                ev = None
                for g in groups:
                    ev = cast_group(g, win_cols(g), 16, 0, sem)
                for g in groups:
                    B_wsc[g].w = ev

            cast_phase([G_K, G_K + 1])

            def late_casts():
                cast_phase([G_V, G_V + 1, G_U, G_U + 1])
                cast_phase([G_Q, G_AZ])
                cast_phase([G_Q + 1, G_AZ + 1])
                cast_phase([G_PZ, G_PZ + 1])
                for mg in range(4):
                    sem = fresh()
                    cast_group(G_GP + mg, win_cols(G_GP + mg), 16, 0, sem)
                    cast_group(G_UP + mg, wpu_d.ap()[:, mg * 512:(mg + 1) * 512], 8, 0, sem)
                    cast_group(G_UP + mg, wau_d.ap()[:, mg * 512:(mg + 1) * 512], 8, 8, sem)
                    ev = cast_group(G_GA + mg, win_cols(G_GA + mg), 16, 0, sem)
                    for g in (G_GP + mg, G_UP + mg, G_GA + mg):
                        B_wsc[g].w = ev
                sem = fresh()
                ev = None
                for ngp in range(4):
                    ev = cast_group(G_WO + ngp, wo_d.ap()[:, ngp * 512:(ngp + 1) * 512], 16, 0, sem)
                for ngp in range(4):
                    B_wsc[G_WO + ngp].w = ev
            DMAn("pool", pw_sb[:], pw_d.ap().rearrange("g (ic p) n -> p g ic n", p=128), [], [C_pw])

            if KLEVEL < 1:
                raise _Stop
            DMAG([("sp", ident[:], identbf_d.ap(), [], [C_ident]),
                  ("sp", identf[:], identf_d.ap(), [], [C_identf]),
                  ("sp", band[:], band_d.ap(), [], [C_band]),
                  ("sp", g64[:, 0, :], bass.AP(qg_d, 0, [[0, 128], [1, 64]]), [], [C_g64]),
                  ("sp", g64[:, 1, :], bass.AP(kg_d, 0, [[0, 128], [1, 64]]), [], [C_g64])])
            MSET("dve", ones[:], 1.0, [], [C_ones])
            MSET("dve", nhalf[:], -0.5, [], [C_nhalf])
            a0 = g64[:, 0, :]
            a1 = g64[:, 1, :]
            TS("dve", gqbc[:].rearrange("p (h d) -> p h d", d=64),
               bass.AP(a0.tensor, a0.offset, [list(a0.ap[0]), [0, 8], [1, 64]]), 1.0, None, ALU.mult, None, [C_g64], [C_gq])
            TS("dve", gkbc[:].rearrange("p (h d) -> p h d", d=64),
               bass.AP(a1.tensor, a1.offset, [list(a1.ap[0]), [0, 8], [1, 64]]), 8.0, None, ALU.mult, None, [C_g64], [C_gk])

            if KLEVEL < 2:
                raise _Stop
            DMAG([("sp", small48[:, 0, :], c_d.ap().rearrange("b (kc p) -> (b kc) p", p=128), [], [C_s48[0]]),
                  ("sp", small16[:, 0, :], ng_d.ap(), [], [C_s16[0]]),
                  ("sp", small16[0:8, 1, :], psc_d.ap(), [], [C_s16[1]])])
            pT = psb[0]
            TR(pT[:, 0:48], small48[:, 0, :], identf[0:48, 0:48], [C_s48[0], C_identf], [B_ps[0]])
            TR(pT[:, 48:64], small16[:, 0, :], identf[0:16, 0:16], [C_s16[0], C_identf], [B_ps[0]])
            TR(pT[:, 64:72], small16[0:8, 1, :], identf[0:8, 0:8], [C_s16[1], C_identf], [B_ps[0]])
            cTt = fa[:, 0, 0:48]
            CP("dve", cTt, pT[:, 0:48], [B_ps[0]], [B_fa[0]])
            CP("dve", ngT[:], pT[:, 48:64], [B_ps[0]], [C_ng])
            CP("dve", pscT[:], pT[:, 64:72], [B_ps[0]], [C_psc])
            tht = fa[:, 1, 0:48]
            ACT(tht, cTt, AF.Tanh, [B_fa[0]], [B_fa[1]], scale=0.5)
            STT("dve", sT[:], tht, 1.0, cTt, ALU.add, ALU.mult, [B_fa[0], B_fa[1]], [C_sT])

            if KLEVEL < 3:
                raise _Stop
            wst_f = [wst[:, i].rearrange("p a n -> p (a n)").bitcast(F32) for i in range(3)]
            sT3 = sT[:].rearrange("p (b k) -> p b k", k=16)
            B_adad_p = []
            S_badab = fresh()
            S_adst = [fresh(), fresh()]
            for half in range(2):
                banks = [0, 1, 2, 4, 5, 6]
                for kc in range(16):
                    sl = rot("wst", 3)
                    DMA("sp", wst_f[sl][:, 0:3072], wada_d.ap()[kc * 128:(kc + 1) * 128, half * 3072:(half + 1) * 3072],
                        S_wst[sl], [], [B_wst[sl]])
                    for n in range(6):
                        MM(psb[banks[n]][0:3, :], sT3[:, :, kc], wst_f[sl][:, n * 512:(n + 1) * 512], kc == 0, kc == 15,
                           [C_sT, B_wst[sl]], [B_ps[banks[n]]])
                for n in range(6):
                    col = half * 3072 + n * 512
                    ab = n % 2
                    DMAn("sp", badab, bass.AP(bada_d, col, [[0, 3], [1, 512]]), [], [C_badab], sem=S_badab)
                    STT("dve", adab[:, ab, :], psb[banks[n]][0:3, :], 0.5, badab, ALU.mult, ALU.add,
                        [B_ps[banks[n]], C_badab], [C_adab[ab]])
                    bp = Buf("adad%d" % col)
                    B_adad_p.append(bp)
                    DMAn("sp", ada_dd.ap()[:, col:col + 512], adab[:, ab, :], [C_adab[ab]], [bp], sem=S_adst[ab])
            gate_b = Buf("castgate")
            gate_b.w = B_wst[(ctr["wst"] - 1) % 3].w
            cast_gate.append(gate_b)
            late_casts()
            del cast_gate[:]
            for bp in B_adad_p:
                sk = bp.w[0]
                bp.w = (sk, P.dma_sems[sk], "dma")
            DMAG([("sp", small48[b * 16:(b + 1) * 16, 1, :], bass.AP(ada_dd, b * 3 * D, [[128, 16], [1, 128]]), B_adad_p, [C_s48[1]])
                  for b in range(3)])
            DMAG([("sp", small48[b * 16:(b + 1) * 16, 2, :], bass.AP(ada_dd, b * 3 * D + D, [[128, 16], [1, 128]]), B_adad_p, [C_s48[2]])
                  for b in range(3)])
            TR(pT[:, 0:48], small48[:, 1, :], identf[0:48, 0:48], [C_s48[1], C_identf], [B_ps[0]])
            TR(pT[:, 48:96], small48[:, 2, :], identf[0:48, 0:48], [C_s48[2], C_identf], [B_ps[0]])
            CP("dve", shT[:].rearrange("p b k -> p (b k)"), pT[:, 0:48], [B_ps[0]], [C_sh])
            for b in range(3):
                STT("dve", gsT[:, b, :], pT[:, 48 + 16 * b:64 + 16 * b], 1.0, ngT[:], ALU.add, ALU.mult,
                    [B_ps[0], C_ng], [C_gs])
            B_adad = Buf("adad_all")
            for bp in B_adad_p:
                if bp.w is not None:
                    B_adad.r.append(bp.w)

            B_rtab_h = [Buf("rtab%d" % h) for h in range(16)]

            def build_rtable():
                if KLEVEL < 5:
                    raise _Stop
                xt_flat = xt[:].rearrange("p a n -> p (a n)")
                rpt = xt_flat[0:16, 0:2400].rearrange("p (a b) -> p a b", b=160)
                rtmp = xt_flat[0:16, 2400:2400 + 465].rearrange("p (a b) -> p a b", b=31)
                MSET("pool", rpt, 0.0, [], [B_xt[0], B_xt[1]])
                DMAn("sp", rtmp, rpb_d.ap(), [], [B_xt[0], B_xt[1]])
                CP("pool", rpt[:, :, 64:95], rtmp[:, ::-1, :], [B_xt[0], B_xt[1]], [B_xt[0], B_xt[1]])
                DMAn("sp", rp2_d.ap(), rpt, [B_xt[0], B_xt[1]], [B_rp2])
                mT_f = mT[:].rearrange("p a b -> p (a b)").bitcast(F32)
                ay_f = attn_y[:].rearrange("p a b -> p (a b)").bitcast(F32)
                py_f = pool_y[:].rearrange("p a b -> p (a b)").bitcast(F32)
                qT_f = qT[:].rearrange("p a b c -> p (a b c)").bitcast(F32)
                dT_fl = dT[:].rearrange("p a b -> p (a b)")
                sp_fl = spzT[:].rearrange("p a b -> p (a b)")
                maskt = qT_f[:, 0:1024]
                DMAn("sp", maskt, maskr_d.ap(), [], [B_qT[0], B_qT[1]])
                MSET("pool", mT_f[:, 0:2048], 0.0, [], [B_mT[0], B_mT[1]])
                S_rb = [fresh(), fresh()]
                B_est = [Buf("est0"), B_py]
                B_rb = [B_dT, B_spz]
                S_stg = [P.new_sem("stg0"), P.new_sem("stg1")]
                ests = [ay_f[:, 0:1024], py_f[:, 0:1024]]
                rbufs = [dT_fl[:, 0:1024], sp_fl[:, 0:1024]]
                for h in range(16):
                    s_ = h % 2
                    stg = mT_f[:, s_ * 1024:(s_ + 1) * 1024]
                    stg3 = stg.rearrange("p (i c) -> p i c", c=64)
                    src = bass.AP(rp2_d, h * 15 * 160 + 16, [[1, 64], [160, 15], [1, 64]])
                    DMA("sp", stg3[0:64, 0:15, :], src, S_stg[s_], [B_rp2], [B_mT[s_]])
                    dmy = Buf("dmy")
                    ev_ = P.dma("sp", (lambda o_, i_: (lambda e: e.dma_start(out=o_, in_=i_)))(stg3[64:128, 1:16, :], src),
                                S_stg[s_], [B_rp2], [dmy])
                    B_mT[s_].w = ev_
                    ACT(ests[s_], stg, AF.Exp, [B_mT[s_]], [B_est[s_]])
                    TT("dve", rbufs[s_], ests[s_], maskt, ALU.mult, [B_est[s_], B_qT[0], B_qT[1]], [B_rb[s_]])
                    DMAn("sp", rtab_d.ap()[h // 2][:, (h % 2) * 1024:(h % 2 + 1) * 1024], rbufs[s_], [B_rb[s_]], [B_rtab_h[h]], sem=S_rb[s_])
                for h in range(16):
                    B_rtab_h[h].w = (S_rb[h % 2], P.dma_sems[S_rb[h % 2]], "dma")
                for c_ in range(8):
                    for t_ in range(2):
                        B_ay[c_][t_].r.extend(B_est[0].r)
                        if B_est[0].w is not None:
                            B_ay[c_][t_].r.append(B_est[0].w)

            for cb in C_s48 + C_s16:
                for gb_ in B_gq:
                    gb_.r.extend(cb.r)
                    if cb.w is not None:
                        gb_.r.append(cb.w)
            def load_w(g):
                sl = rot("wst", 3)
                DMA("sp", wst[:, sl].rearrange("p a n -> p (a n)"), wsc_d.ap()[g], S_wst[sl], [B_wsc[g]], [B_wst[sl]])
                return sl

            def proj(hslot, lt, sl, nk=16, k0=0, lhs_fn=None, lhs_bufs=None):
                flush()
                gen[0] += 1
                bk = PJ[rot("pj", 2)]
                for kc in range(nk):
                    if lhs_fn is None:
                        lhsT = hT[:, hslot, kc, lt * 128:(lt + 1) * 128]
                        rb = [B_hT[hslot][lt]]
                    else:
                        lhsT = lhs_fn(kc)
                        rb = lhs_bufs
                    MM(psb[bk][:], lhsT, wst[:, sl, k0 + kc, :], kc == 0, kc == nk - 1, rb + [B_wst[sl]], [B_ps[bk]])
                return bk

            deferred = []
            gen = [0]

            def flush(all_=False):
                keep = []
                for (g_, f_) in deferred:
                    if all_ or g_ <= gen[0] - 2:
                        f_()
                    else:
                        keep.append((g_, f_))
                deferred[:] = keep

            def transpose4(src_bf, src_buf, dst_ap, dst_bufs, eng_i):
                deferred.append((gen[0], lambda: transpose4_now(src_bf, src_buf, dst_ap, dst_bufs, eng_i)))

            def transpose4_now(src_bf, src_buf, dst_ap, dst_bufs, eng_i):
                th = rot("trp", 2)
                trp = trps[th]
                for c in range(4):
                    TR(trp[:, c * 128:(c + 1) * 128], src_bf[:, c * 128:(c + 1) * 128], ident[:],
                       [src_buf, C_ident], [B_trp[th]])
                src = trp[:, 0:512].rearrange("p (c t) -> p c t", t=128)
                if eng_i % 2 == 0:
                    ACT(dst_ap, src, AF.Copy, [B_trp[th]], dst_bufs)
                else:
                    CP("dve", dst_ap, src, [B_trp[th]], dst_bufs)

            def headnorm(bk, gbc, gbuf, dst_ap, dst_bufs, eng_i, early=False):
                ia = rot("fa", 3)
                ib = rot("fb", 2)
                ic = rot("bfs", 3)
                i8 = rot("ss8", 4)
                ACT(fa[:, ia, :], psb[bk][:], AF.Copy, [B_ps[bk]], [B_fa[ia]])
                ACT(fb[:, ib, :], fa[:, ia, :], AF.Square, [B_fa[ia]], [B_fb[ib]])
                P.op("dve", lambda e: e.tensor_reduce(out=ss8[:, i8, :], in_=fb[:, ib, :].rearrange("p (h d) -> p h d", d=64),
                                                      axis=AX.X, op=ALU.add), [B_fb[ib]], [B_ss8[i8]])
                if early:
                    TS("dve", ss8[:, i8, :], ss8[:, i8, :], 64.0 * EPS, None, ALU.add, None, [B_ss8[i8]], [B_ss8[i8]])
                    ACT(ss8[:, i8, :], ss8[:, i8, :], AF.Ln, [B_ss8[i8]], [B_ss8[i8]])
                    ACT(ss8[:, i8, :], ss8[:, i8, :], AF.Exp, [B_ss8[i8]], [B_ss8[i8]], scale=-0.5)
                else:
                    TS("dve", ss8[:, i8, :], ss8[:, i8, :], 64.0 * EPS, None, ALU.add, None, [B_ss8[i8]], [B_ss8[i8]])
                    TT("pool", ss8[:, i8, :], ss8[:, i8, :], nhalf[:], ALU.pow, [B_ss8[i8], C_nhalf], [B_ss8[i8]])
                TT("dve", fb[:, ib, :].rearrange("p (h d) -> p h d", d=64),
                   fa[:, ia, :].rearrange("p (h d) -> p h d", d=64), bc_last(ss8[:, i8, :], 64), ALU.mult,
                   [B_fa[ia], B_ss8[i8]], [B_fb[ib]])
                TT("dve" if early else "pool", bfs[:, ic, :], fb[:, ib, :], gbc[:], ALU.mult, [B_fb[ib], gbuf], [B_bfs[ic]])
                transpose4(bfs[:, ic, :], B_bfs[ic], dst_ap, dst_bufs, eng_i)

            def silu2(bk, dst_ap, dst_bufs, eng_i):
                ia = rot("fa", 3)
                ic = rot("bfs", 3)
                ACT(fa[:, ia, :], psb[bk][:], AF.Tanh, [B_ps[bk]], [B_fa[ia]], scale=0.5)
                STT("dve", bfs[:, ic, :], fa[:, ia, :], 1.0, psb[bk][:], ALU.add, ALU.mult, [B_fa[ia], B_ps[bk]], [B_bfs[ic]])
                transpose4(bfs[:, ic, :], B_bfs[ic], dst_ap, dst_bufs, eng_i)

            def rslot_of(gt):
                return (gt // 2) % 3

            xt_of = {}

            def frontend_load(gb):
                for lt in range(2):
                    gt = gb * 2 + lt
                    xi = rot("xt", 2)
                    xt_of[gt] = xi
                    DMA("sp", xt[:, xi, :], x_d.ap()[gt * 128:(gt + 1) * 128, :], S_xt[xi], [], [B_xt[xi]])

            def fe_prep(gb, lt):
                gt = gb * 2 + lt
                xi = xt_of[gt]
                col = gt % 4
                MSET("dve", ssx[:, col:col + 1], 0.0, [], [B_ss])
                ACT(xnb[:], xt[:, xi, :], AF.Square, [B_xt[xi], B_ss], [B_xnb, B_ss], accum=ssx[:, col:col + 1])
                TS("dve", ssx[:, col:col + 1], ssx[:, col:col + 1], 1.0 / D, EPS, ALU.mult, ALU.add, [B_ss], [B_ss])
                if gb < 2:
                    ACT(ssx[:, col:col + 1], ssx[:, col:col + 1], AF.Ln, [B_ss], [B_ss])
                    ACT(ssx[:, col:col + 1], ssx[:, col:col + 1], AF.Exp, [B_ss], [B_ss], scale=-0.5)
                else:
                    TT("pool", ssx[:, col:col + 1], ssx[:, col:col + 1], nhalf[:, 0:1], ALU.pow, [B_ss, C_nhalf], [B_ss])
                ACT(xnb[:], xt[:, xi, :], AF.Copy, [B_xt[xi], B_ss], [B_xnb], scale=ssx[:, col:col + 1])

            def fe_round(gb, lt, q4):
                hs = gb % 2
                gt = gb * 2 + lt
                sq = seq_of_tile(gt)
                th = rot("trp", 2)
                for c in range(4):
                    kc = q4 * 4 + c
                    TR(trps[th][:, c * 128:(c + 1) * 128], xnb[:, kc * 128:(kc + 1) * 128], ident[:],
                       [B_xnb, C_ident], [B_trp[th]])
                for c in range(4):
                    kc = q4 * 4 + c
                    src = trps[th][:, c * 128:(c + 1) * 128]
                    dst = hT[:, hs, kc, lt * 128:(lt + 1) * 128]
                    if c % 2 == 0:
                        ACT(dst, src, AF.Identity, [B_trp[th], C_gs, C_sh], [B_hT[hs][lt]],
                            scale=gsT[:, sq, kc:kc + 1], bias=shT[:, sq, kc:kc + 1])
                    else:
                        TS("dve", dst, src, gsT[:, sq, kc:kc + 1], shT[:, sq, kc:kc + 1], ALU.mult, ALU.add,
                           [B_trp[th], C_gs, C_sh], [B_hT[hs][lt]])

            def frontend_tile(gb, lt):
                fe_prep(gb, lt)
                for q4 in range(4):
                    fe_round(gb, lt, q4)

            def ahead_proj(gb):
                hs = gb % 2
                rs = gb % 3
                for kg in range(2):
                    sl = load_w(G_K + kg)
                    for lt in range(2):
                        bk = proj(hs, lt, sl)
                        dst = kT[:, rs, 4 * kg:4 * kg + 4, lt * 128:(lt + 1) * 128]
                        headnorm(bk, gkbc, C_gk, dst, [B_kT[rs][lt]], lt, early=(gb < 2))
                for (gbase, ring, Bring) in ((G_V, vv, B_vv), (G_U, uu, B_uu)):
                    for g2 in range(2):
                        sl = load_w(gbase + g2)
                        for lt in range(2):
                            bk = proj(hs, lt, sl)
                            dst = ring[:, rs, lt, g2 * 512:(g2 + 1) * 512]
                            if lt == 0:
                                ACT(dst, psb[bk][:], AF.Copy, [B_ps[bk]], [Bring[rs][lt]])
                            else:
                                CP("dve", dst, psb[bk][:], [B_ps[bk]], [Bring[rs][lt]])

            S4 = [0, 1, 2, 3, 4, 5]

            def att_qk(lb, sq, hp, lt, ri):
                qb, c4 = hp // 4, hp % 4
                gt = lb * 2 + lt
                j = gt - SEQ_T0[sq]
                J = SEQ_J[sq]
                if j == 0:
                    olist, i0, interior = [3, 2, 1, 0], 1, False
                elif j == 1:
                    olist, i0, interior = [2, 1, 0, -1], 3, False
                elif j == J - 2:
                    olist, i0, interior = [1, 0, -1, -2], 5, False
                elif j == J - 1:
                    olist, i0, interior = [0, -1, -2, -3], 7, False
                else:
                    olist, i0, interior = [1, 0, -1, -2], 5, True
                odb = OD_BANK[rot("OD", 2)]
                OD = psb[odb]
                sbis = [S4[rot("S", 6)], S4[rot("S", 6)]]
                for a, o in enumerate(olist):
                    gk = gt + o
                    krs, klt = rslot_of(gk), gk % 2
                    for hh in range(2):
                        p0 = 64 * hh
                        MM(psb[sbis[hh]][:, a * 128:(a + 1) * 128], kT[p0:p0 + 64, krs, hp, klt * 128:(klt + 1) * 128],
                           qT[p0:p0 + 64, qb, c4, lt * 128:(lt + 1) * 128], True, True,
                           [B_kT[krs][klt], B_qT[qb]], [B_ps[sbis[hh]]], tp=(p0, 0))
                if interior:
                    gk2 = gt + 2
                    krs2, klt2 = rslot_of(gk2), gk2 % 2
                    for hh in range(2):
                        p0 = 64 * hh
                        MM(psb[sbis[hh]][0:64, 448:512], kT[p0:p0 + 64, krs2, hp, klt2 * 128:klt2 * 128 + 64],
                           qT[p0:p0 + 64, qb, c4, lt * 128 + 64:(lt + 1) * 128], True, True,
                           [B_kT[krs2][klt2], B_qT[qb]], [B_ps[sbis[hh]]], tp=(p0, 0))
                pts = []
                for hh in range(2):
                    h = 2 * hp + hh
                    p0 = 64 * hh
                    sbi = sbis[hh]
                    S = psb[sbi]
                    ei = rot("eb", 2)
                    ACT(eb[:, ei, :], S[:], AF.Exp, [B_ps[sbi]], [B_eb[ei]])
                    pi = rot("pt", 6)
                    TT("dve", pt[:, pi, :].rearrange("p (i c) -> p i c", c=64),
                       eb[:, ei, :].rearrange("p (i c) -> p i c", c=64),
                       rt[:, ri, hh, i0:i0 + 8, ::-1], ALU.mult, [B_eb[ei], B_rt[ri]], [B_pt[pi]])
                    if interior:
                        TT("dve", ptx[:, pi, :], eb[0:64, ei, 448:512], rt[0:64, ri, hh, 4, ::-1], ALU.mult,
                           [B_eb[ei], B_rt[ri]], [B_ptx[pi]])
                        MSET("dve", pt[0:64, pi, 448:512], 0.0, [], [B_pt[pi]])
                    pts.append((pi, hh, h, p0))
                return (gt, hp, lt, qb, c4, olist, interior, odb, pts)

            def att_pv(st_):
                (gt, hp, lt, qb, c4, olist, interior, odb, pts) = st_
                OD = psb[odb]
                n = len(olist)
                for (pi, hh, h, p0) in pts:
                    for a, o in enumerate(olist):
                        gk = gt + o
                        krs, klt = rslot_of(gk), gk % 2
                        last = (a == n - 1) and not interior
                        MM(OD[p0:p0 + 64, 0:128], vv[:, krs, klt, h * 64:(h + 1) * 64], pt[:, pi, a * 128:(a + 1) * 128],
                           a == 0, last, [B_vv[krs][klt], B_pt[pi]], [B_ps[odb]], tp=(0, p0))
                    if interior:
                        gk = gt + 2
                        krs, klt = rslot_of(gk), gk % 2
                        MM(OD[p0:p0 + 64, 64:128], vv[0:64, krs, klt, h * 64:(h + 1) * 64], ptx[:, pi, :],
                           False, True, [B_vv[krs][klt], B_ptx[pi]], [B_ps[odb]], tp=(0, p0))
                    for a, o in enumerate(olist):
                        last = (a == n - 1) and not interior
                        MM(OD[p0:p0 + 64, 128:256], ones[:, 0:64], pt[:, pi, a * 128:(a + 1) * 128],
                           a == 0, last, [C_ones, B_pt[pi]], [B_ps[odb]], tp=(0, p0))
                    if interior:
                        MM(OD[p0:p0 + 64, 192:256], ones[0:64, 0:64], ptx[:, pi, :],
                           False, True, [C_ones, B_ptx[pi]], [B_ps[odb]], tp=(0, p0))
                di = rot("rd", 2)
                P.op("dve", lambda e: e.reciprocal(out=rd[:, di, :], in_=OD[:, 128:256]), [B_ps[odb]], [B_rd[di]])
                TT("dve", wgt[:, di, :], rd[:, di, :], sazT[:, qb, c4, lt * 128:(lt + 1) * 128], ALU.mult,
                   [B_rd[di], B_saz[qb]], [B_wgt[di]])
                TT("dve", attn_y[:, hp, lt * 128:(lt + 1) * 128], OD[:, 0:128], wgt[:, di, :], ALU.mult,
                   [B_ps[odb], B_wgt[di]], [B_ay[hp][lt]])

            def lagged(lb):
                hs = lb % 2
                sq = seq_of_tile(lb * 2)
                nb = lb + 2
                rt_next = []

                def rt_load(hp_):
                    ri_ = rot("rt", 2)
                    DMA("pool", rt[:, ri_].rearrange("p a i c -> p (a i c)"), rtab_d.ap()[hp_], S_rt[ri_],
                        [B_rtab_h[2 * hp_], B_rtab_h[2 * hp_ + 1]], [B_rt[ri_]])
                    rt_next.append(ri_)

                rt_load(0)
                for qg in range(2):
                    qb = qg
                    sl = load_w(G_Q + qg)
                    for lt in range(2):
                        bk = proj(hs, lt, sl)
                        headnorm(bk, gqbc, C_gq, qT[:, qb, :, lt * 128:(lt + 1) * 128], [B_qT[qb]], lt, early=(lb < 1))
                    sl = load_w(G_AZ + qg)
                    for lt in range(2):
                        bk = proj(hs, lt, sl)
                        silu2(bk, sazT[:, qb, :, lt * 128:(lt + 1) * 128], [B_saz[qb]], lt + 1)
                flush(True)
                pend = []
                for hp in range(8):
                    ri = rt_next.pop(0)
                    if hp + 1 < 8:
                        rt_load(hp + 1)
                    for lt in range(2):
                        pend.append(att_qk(lb, sq, hp, lt, ri))
                        if len(pend) > 2:
                            att_pv(pend.pop(0))
                while pend:
                    att_pv(pend.pop(0))
                for pg in range(2):
                    sl = load_w(G_PZ + pg)
                    for lt in range(2):
                        bk = proj(hs, lt, sl)
                        silu2(bk, spzT[:, 4 * pg:4 * pg + 4, lt * 128:(lt + 1) * 128], [B_spz], lt)
                flush(True)
                for ch in range(8):
                    g = ch // 2
                    bk = PJ[rot("pj", 2)]
                    for lt in range(2):
                        gt = lb * 2 + lt
                        j = gt - SEQ_T0[sq]
                        J = SEQ_J[sq]
                        srcs = []
                        if j > 0:
                            srcs.append((gt - 1, g * 5 + 0))
                        srcs.append((gt, g * 5 + (3 if j == 0 else (4 if j == J - 1 else 1))))
                        if j < J - 1:
                            srcs.append((gt + 1, g * 5 + 2))
                        for si, (sg, bi) in enumerate(srcs):
                            srs, slt = rslot_of(sg), sg % 2
                            MM(psb[bk][:, lt * 128:(lt + 1) * 128], uu[:, srs, slt, ch * 128:(ch + 1) * 128], band[:, bi, :],
                               si == 0, si == len(srcs) - 1, [B_uu[srs][slt], C_band], [B_ps[bk]])
                    if ch % 2 == 0:
                        ACT(dT[:, ch, :], psb[bk][:, 0:256], AF.Copy, [B_ps[bk]], [B_dT])
                    else:
                        CP("dve", dT[:, ch, :], psb[bk][:, 0:256], [B_ps[bk]], [B_dT])
                for oc in range(8):
                    g = oc // 2
                    bk = PJ[rot("pj", 2)]
                    for ic in range(2):
                        MM(psb[bk][:, 0:256], pw_sb[:, g, ic, (oc % 2) * 128:(oc % 2 + 1) * 128], dT[:, 2 * g + ic, :],
                           ic == 0, ic == 1, [B_dT, C_pw], [B_ps[bk]])
                    STT("dve", pool_y[:, oc, :], psb[bk][:, 0:256], pscT[:, oc:oc + 1], spzT[:, oc, :], ALU.mult, ALU.mult,
                        [B_ps[bk], C_psc, B_spz], [B_py])
                if nb < NBLK:
                    frontend_load(nb)
                ayb = [[B_ay[c][lt] for c in range(8)] for lt in range(2)]
                for mg in range(4):
                    slp = load_w(G_GP + mg)
                    tg = []
                    for lt in range(2):
                        bgp = proj(hs, lt, slp)
                        ia = rot("fa", 3)
                        ACT(fa[:, ia, :], psb[bgp][:], AF.Tanh, [B_ps[bgp]], [B_fa[ia]], scale=0.5)
                        tg.append(ia)
                    slu = load_w(G_UP + mg)
                    t1 = []
                    for lt in range(2):
                        bpu = proj(hs, lt, slu, nk=8, k0=0,
                                   lhs_fn=lambda kc: pool_y[:, kc, lt * 128:(lt + 1) * 128], lhs_bufs=[B_py])
                        ib = rot("fb", 2)
                        STT("dve", fb[:, ib, :], fa[:, tg[lt], :], 1.0, psb[bpu][:], ALU.add, ALU.mult,
                            [B_fa[tg[lt]], B_ps[bpu]], [B_fb[ib]])
                        t1.append(ib)
                    sla = load_w(G_GA + mg)
                    tg2 = []
                    for lt in range(2):
                        bga = proj(hs, lt, sla)
                        ia2 = rot("fa", 3)
                        ACT(fa[:, ia2, :], psb[bga][:], AF.Tanh, [B_ps[bga]], [B_fa[ia2]], scale=0.5)
                        tg2.append(ia2)
                    for lt in range(2):
                        bau = proj(hs, lt, slu, nk=8, k0=8,
                                   lhs_fn=lambda kc: attn_y[:, kc, lt * 128:(lt + 1) * 128], lhs_bufs=ayb[lt])
                        ia2 = tg2[lt]
                        STT("dve", fa[:, ia2, :], fa[:, ia2, :], 1.0, psb[bau][:], ALU.add, ALU.mult,
                            [B_fa[ia2], B_ps[bau]], [B_fa[ia2]])
                        ic = rot("bfs", 3)
                        TT("pool", bfs[:, ic, :], fb[:, t1[lt], :], fa[:, ia2, :], ALU.add, [B_fb[t1[lt]], B_fa[ia2]], [B_bfs[ic]])
                        transpose4(bfs[:, ic, :], B_bfs[ic], mT[:, 4 * mg:4 * mg + 4, lt * 128:(lt + 1) * 128], [B_mT[lt]], lt)
                flush(True)
                fe_items = []
                if nb < NBLK:
                    fe_prep(nb, 0)
                    fe_items = [[lambda: fe_round(nb, 0, 0)], [lambda: fe_round(nb, 0, 1)], [lambda: fe_round(nb, 0, 2)],
                                [lambda: fe_round(nb, 0, 3), lambda: fe_prep(nb, 1)], [],
                                [lambda: fe_round(nb, 1, 0)], [lambda: fe_round(nb, 1, 1)],
                                [lambda: fe_round(nb, 1, 2), lambda: fe_round(nb, 1, 3)]]
                for ngp in range(4):
                    sl = load_w(G_WO + ngp)
                    gi = rot("gq", 2)
                    DMA("sp", gq[:, gi, :], bass.AP(ada_dd, sq * 3 * D + 2 * D + ngp * 512, [[0, 128], [1, 512]]),
                        S_gq[gi], [B_adad], [B_gq[gi]])
                    for lt in range(2):
                        gt = lb * 2 + lt
                        xi = rot("xr", 2)
                        DMA("sp", xr[:, xi, :], x_d.ap()[gt * 128:(gt + 1) * 128, ngp * 512:(ngp + 1) * 512], S_xr[xi],
                            [], [B_xr[xi]])
                        bk = proj(hs, lt, sl, lhs_fn=lambda kc: mT[:, kc, lt * 128:(lt + 1) * 128], lhs_bufs=[B_mT[lt]])
                        ia = rot("fa", 3)
                        STT("dve", fa[:, ia, :], psb[bk][:], 0.25, gq[:, gi, :], ALU.mult, ALU.mult,
                            [B_ps[bk], B_gq[gi]], [B_fa[ia]])
                        TT("pool", xr[:, xi, :], fa[:, ia, :], xr[:, xi, :], ALU.add, [B_fa[ia], B_xr[xi]], [B_xr[xi]])
                        DMA("pool", y_d.ap()[gt * 128:(gt + 1) * 128, ngp * 512:(ngp + 1) * 512], xr[:, xi, :], S_yo[xi],
                            [B_xr[xi]], [B_yo[xi]])
                        if fe_items:
                            for f_ in fe_items.pop(0):
                                f_()

            if KLEVEL < 6:
                raise _Stop
            frontend_load(0)
            frontend_tile(0, 0)
            frontend_tile(0, 1)
            for s in range(NBLK + 1):
                if s < NBLK:
                    if KLEVEL < 10 + 2 * s:
                        raise _Stop
                    ahead_proj(s)
                if s == 0:
                    frontend_load(1)
                    frontend_tile(1, 0)
                    frontend_tile(1, 1)
                    build_rtable()
                flush(True)
                if s >= 1:
                    if KLEVEL < 10 + 2 * s + 1:
                        raise _Stop
                    lagged(s - 1)
        try:
            body()
        except _Stop:
            pass
        P.finish("sp")
        P.finish("pool")
        P.emit()
    return nc


_CACHE = {}


def kernel(x_prompt, x_sample, c_prompt, c_sample, w_ada, b_ada, norm_g, w_in, pool_w, pool_scale,
           q_norm_g, k_norm_g, rpb, w_pool_up, w_attn_up, w_o):
    f32 = np.float32
    xp = np.asarray(x_prompt, f32)
    xs = np.asarray(x_sample, f32)
    cp = np.asarray(c_prompt, f32)
    cs = np.asarray(c_sample, f32)
    if "nc" not in _CACHE:
        _CACHE["nc"] = build_program()
        _CACHE["consts"] = host_consts()
    nc = _CACHE["nc"]
    consts = _CACHE["consts"]
    shared = {
        "w_ada": np.ascontiguousarray(np.asarray(w_ada, f32)[0]),
        "b_ada": np.ascontiguousarray(np.asarray(b_ada, f32)[0].reshape(1, 3 * D)),
        "norm_g": np.ascontiguousarray(np.asarray(norm_g, f32)[0].reshape(16, 128)),
        "w_in": np.ascontiguousarray(np.asarray(w_in, f32)[0]),
        "pool_w": np.ascontiguousarray(np.asarray(pool_w, f32)[0]),
        "pool_scale": np.ascontiguousarray(np.asarray(pool_scale, f32)[0].reshape(8, 128)),
        "q_norm_g": np.ascontiguousarray(np.asarray(q_norm_g, f32)[0].reshape(1, 64)),
        "k_norm_g": np.ascontiguousarray(np.asarray(k_norm_g, f32)[0].reshape(1, 64)),
        "rpb": np.ascontiguousarray(np.asarray(rpb, f32)[0]),
        "w_pool_up": np.ascontiguousarray(np.asarray(w_pool_up, f32)[0]),
        "w_attn_up": np.ascontiguousarray(np.asarray(w_attn_up, f32)[0]),
        "w_o": np.ascontiguousarray(np.asarray(w_o, f32)[0]),
    }
    shared.update(consts)
    in_maps = []
    for c in range(8):
        xc = np.concatenate([xp[2 * c], xp[2 * c + 1], xs[c]], axis=0)
        cc = np.stack([cp[2 * c], cp[2 * c + 1], cs[c]], axis=0)
        m = dict(shared)
        m["x"] = np.ascontiguousarray(xc)
        m["c3"] = np.ascontiguousarray(cc)
        in_maps.append(m)
    res = run_bass_kernel_spmd(nc, in_maps, core_ids=list(range(8)))
    y_prompt = np.empty((16, 2048, D), f32)
    y_sample = np.empty((8, 4096, D), f32)
    for c in range(8):
        y = np.asarray(res.results[c]["y"], f32)
        y_prompt[2 * c] = y[0:2048]
        y_prompt[2 * c + 1] = y[2048:4096]
        y_sample[c] = y[4096:8192]
    return (y_prompt, y_sample)
```

```python
import numpy as np
import ml_dtypes
from contextlib import ExitStack
import concourse.bass as bass
import concourse.mybir as mybir
from concourse.bass_utils import run_bass_kernel_spmd

F32 = mybir.dt.float32
BF16 = mybir.dt.bfloat16
ALU = mybir.AluOpType
AF = mybir.ActivationFunctionType
AX = mybir.AxisListType

D = 2048
NTILE = 64
NBLK = 32
SEQ_T0 = (0, 16, 32)
SEQ_J = (16, 16, 32)
EPS = 1e-6
POOL_WINDOWS = (2, 4, 8, 16)
G_U, G_PZ, G_Q, G_K, G_V, G_AZ, G_GP, G_GA, G_WO, G_UP = 0, 2, 4, 6, 8, 10, 12, 16, 20, 24
NGRP = 28

ENGS = ("pe", "act", "dve", "pool", "sp")


class Buf:
    __slots__ = ("name", "w", "r")

    def __init__(self, name):
        self.name = name
        self.w = None
        self.r = []


class Prog:
    def __init__(self, nc):
        self.nc = nc
        self.ops = {e: [] for e in ENGS}
        self.cnt = {e: 0 for e in ENGS}
        self.seen = {e: {} for e in ENGS}
        self.dma_sems = {}

    def new_sem(self, name):
        self.dma_sems[name] = 0
        return name

    def _collect(self, eng, reads, writes, is_dma):
        seen = self.seen[eng]
        best = {}

        def add(ev):
            sk, val, e2 = ev
            if seen.get(sk, 0) >= val:
                return
            if best.get(sk, 0) < val:
                best[sk] = val

        for b in reads:
            if b.w is not None:
                if is_dma or not (b.w[2] == eng and eng == "pe"):
                    add(b.w)
        strict = is_dma or eng != "pe"
        for b in writes:
            if b.w is not None and (strict or b.w[2] != eng):
                add(b.w)
            for ev in b.r:
                if strict or ev[2] != eng:
                    add(ev)
        waits = []
        for sk, val in best.items():
            seen[sk] = val
            waits.append((sk, val))
        return waits

    def op(self, eng, fn, reads=(), writes=()):
        waits = self._collect(eng, reads, writes, False)
        self.cnt[eng] += 1
        ev = (eng, self.cnt[eng], eng)
        self.ops[eng].append((waits, fn, (eng, 1)))
        for b in reads:
            b.r.append(ev)
        for b in writes:
            b.w = ev
            b.r = []
        return ev

    def dma(self, eng, fn, semkey, reads=(), writes=()):
        waits = self._collect(eng, reads, writes, True)
        self.dma_sems[semkey] += 16
        ev = (semkey, self.dma_sems[semkey], "dma")
        self.ops[eng].append((waits, fn, (semkey, 16)))
        for b in reads:
            b.r.append(ev)
        for b in writes:
            b.w = ev
            b.r = []
        return ev

    def wait_all(self, eng, bufs):
        seen = self.seen[eng]
        best = {}
        for b in bufs:
            evs = list(b.r)
            if b.w is not None:
                evs.append(b.w)
            for (sk, val, e2) in evs:
                if seen.get(sk, 0) >= val:
                    continue
                if best.get(sk, 0) < val:
                    best[sk] = val
        waits = []
        for sk, val in best.items():
            seen[sk] = val
            waits.append((sk, val))
        self.ops[eng].append((waits, None, None))

    def finish(self, eng="sp"):
        waits = []
        for e in ENGS:
            if self.cnt[e] > 0:
                waits.append((e, self.cnt[e]))
        for k, v in self.dma_sems.items():
            if v > 0:
                waits.append((k, v))
        self.ops[eng].append((waits, None, None))

    def emit(self):
        nc = self.nc
        with ExitStack() as st:
            sems = {}
            for e in ENGS:
                sems[e] = st.enter_context(nc.semaphore("s_" + e))
            for k in self.dma_sems:
                sems[k] = st.enter_context(nc.semaphore("s_" + k))
            block = st.enter_context(nc.Block())
            handles = {"pe": block.tensor, "act": block.scalar, "dve": block.vector,
                       "pool": block.gpsimd, "sp": block.sync}
            for e in ENGS:
                ops = self.ops[e]
                if not ops:
                    continue

                def body(h, ops=ops):
                    for (waits, fn, inc) in ops:
                        for (sk, val) in waits:
                            h.wait_ge(sems[sk], val)
                        if fn is None:
                            continue
                        fn(h).then_inc(sems[inc[0]], inc[1])

                handles[e](body)


def bc_last(ap, n):
    return bass.AP(ap.tensor, ap.offset, [list(x) for x in ap.ap] + [[0, n]])


def seq_of_tile(gt):
    return 0 if gt < 16 else (1 if gt < 32 else 2)


def host_consts():
    bf = ml_dtypes.bfloat16
    ident = np.eye(128, dtype=np.float32)
    band = np.zeros((128, 20, 128), np.float64)
    for g, w in enumerate(POOL_WINDOWS):
        h = w // 2
        prev = band[:, g * 5 + 0, :]
        cur = band[:, g * 5 + 1, :]
        nxt = band[:, g * 5 + 2, :]
        first = band[:, g * 5 + 3, :]
        last = band[:, g * 5 + 4, :]
        for t in range(128):
            lo, hi = t - h, t + h - 1
            for tp in range(lo, hi + 1):
                if tp < 0:
                    prev[128 + tp, t] += 1.0 / w
                elif tp > 127:
                    nxt[tp - 128, t] += 1.0 / w
                else:
                    cur[tp, t] += 1.0 / w
            cur[t, t] -= 1.0
            lo2 = max(lo, 0)
            cnt = hi - lo2 + 1
            for tp in range(lo2, min(hi, 127) + 1):
                first[tp, t] += 1.0 / cnt
            first[t, t] -= 1.0
            hi2 = min(hi, 127)
            cnt = hi2 - lo + 1
            for tp in range(max(lo, 0), hi2 + 1):
                last[tp, t] += 1.0 / cnt
            last[t, t] -= 1.0
    maskr = np.zeros((128, 16, 64), np.float32)
    for p in range(128):
        krl, kc = p // 64, p % 64
        for cp in range(64):
            c = 63 - cp
            cs = min(max(c - 8, 0), 48)
            if cs <= kc < cs + 16:
                for i in range(16):
                    ok = (i <= 14) if krl == 0 else (i >= 1)
                    if ok:
                        maskr[p, i, cp] = 1.0
    return {
        "ident_bf": ident.astype(bf),
        "ident_f": ident,
        "band": band.astype(np.float32).astype(bf),
        "maskr": maskr.reshape(128, 1024),
    }


import os
KLEVEL = int(os.environ.get("KLEVEL", "99"))
KSUB = int(os.environ.get("KSUB", "99"))


class _Stop(Exception):
    pass


def build_program():
    nc = bass.Bass("TRN2", target_bir_lowering=False)
    P = Prog(nc)

    def din(name, shape, dt=F32):
        return nc.dram_tensor(name, list(shape), dt, kind="ExternalInput")

    x_d = din("x", [NTILE * 128, D])
    c_d = din("c3", [3, D])
    wada_d = din("w_ada", [D, 3 * D])
    bada_d = din("b_ada", [1, 3 * D])
    ng_d = din("norm_g", [16, 128])
    win_d = din("w_in", [D, 10240])
    pw_d = din("pool_w", [4, 256, 256])
    psc_d = din("pool_scale", [8, 128])
    qg_d = din("q_norm_g", [1, 64])
    kg_d = din("k_norm_g", [1, 64])
    rpb_d = din("rpb", [16, 15, 31])
    wpu_d = din("w_pool_up", [1024, D])
    wau_d = din("w_attn_up", [1024, D])
    wo_d = din("w_o", [D, D])
    identbf_d = din("ident_bf", [128, 128], BF16)
    identf_d = din("ident_f", [128, 128])
    band_d = din("band", [128, 20, 128], BF16)
    maskr_d = din("maskr", [128, 1024])
    y_d = nc.dram_tensor("y", [NTILE * 128, D], F32, kind="ExternalOutput")
    wsc_d = nc.dram_tensor("wsc", [NGRP, 128, 8192], BF16, kind="Internal")
    ada_dd = nc.dram_tensor("ada_s", [3, 3 * D], F32, kind="Internal")
    rp2_d = nc.dram_tensor("rp2", [16, 15, 160], F32, kind="Internal")
    rtab_d = nc.dram_tensor("rtab", [8, 128, 2048], BF16, kind="Internal")

    with ExitStack() as st:
        def sb(name, shape, dt):
            return st.enter_context(nc.sbuf_tensor("sb_" + name, list(shape), dt))

        hT = sb("hT", [128, 2, 16, 256], BF16)
        kT = sb("kT", [128, 3, 8, 256], BF16)
        vv = sb("vv", [128, 3, 2, 1024], BF16)
        uu = sb("uu", [128, 3, 2, 1024], BF16)
        wst = sb("wst", [128, 3, 16, 512], BF16)
        xt = sb("xt", [128, 2, 2048], F32)
        xnb = sb("xnb", [128, 2048], BF16)
        qT = sb("qT", [128, 2, 4, 256], BF16)
        sazT = sb("sazT", [128, 2, 4, 256], BF16)
        rt = sb("rt", [128, 2, 2, 16, 64], BF16)
        attn_y = sb("attn_y", [128, 8, 256], BF16)
        pool_y = sb("pool_y", [128, 8, 256], BF16)
        mT = sb("mT", [128, 16, 256], BF16)
        dT = sb("dT", [128, 8, 256], BF16)
        spzT = sb("spzT", [128, 8, 256], BF16)
        fa = sb("fa", [128, 3, 512], F32)
        fb = sb("fb", [128, 2, 512], F32)
        bfs = sb("bfs", [128, 3, 512], BF16)
        eb = sb("eb", [128, 2, 512], BF16)
        exb = sb("exb", [64, 2, 64], BF16)
        pt = sb("pt", [128, 6, 512], BF16)
        ptx = sb("ptx", [64, 6, 64], BF16)
        rd = sb("rd", [128, 2, 128], F32)
        wgt = sb("wgt", [128, 2, 128], F32)
        xr = sb("xr", [128, 2, 512], F32)
        gq = sb("gq", [128, 2, 512], F32)
        band = sb("band", [128, 20, 128], BF16)
        pw_sb = sb("pw_sb", [128, 4, 2, 256], BF16)
        gqbc = sb("gqbc", [128, 512], F32)
        gkbc = sb("gkbc", [128, 512], F32)
        ident = sb("ident", [128, 128], BF16)
        identf = sb("identf", [128, 128], F32)
        ones = sb("ones", [128, 64], BF16)
        g64 = sb("g64", [128, 2, 64], F32)
        gsT = sb("gsT", [128, 3, 16], F32)
        shT = sb("shT", [128, 3, 16], F32)
        pscT = sb("pscT", [128, 8], F32)
        ngT = sb("ngT", [128, 16], F32)
        ssx = sb("ssx", [128, 4], F32)
        ss8 = sb("ss8", [128, 4, 8], F32)
        nhalf = sb("nhalf", [128, 8], F32)
        sT = sb("sT", [128, 48], F32)
        small48 = gq[0:48, 0, 0:384].rearrange("p (a n) -> p a n", n=128)
        small16 = gq[0:16, 1, 0:256].rearrange("p (a n) -> p a n", n=128)
        adab = fb[0:3, :, :]
        badab = fa[0:3, 2, :]

        psb = [st.enter_context(nc.psum_tensor("ps%d" % i, [128, 512], F32)) for i in range(8)]
        trps = [psb[2].bitcast(BF16), psb[3].bitcast(BF16)]
        PJ = [0, 1]
        S_BANK = [4, 5]
        OD_BANK = [6, 7]

        B_ps = [Buf("ps%d" % i) for i in range(8)]
        B_trp = [B_ps[2], B_ps[3]]
        B_hT = [[Buf("hT%d_%d" % (s, t)) for t in range(2)] for s in range(2)]
        B_kT = [[Buf("kT%d_%d" % (s, t)) for t in range(2)] for s in range(3)]
        B_vv = [[Buf("vv%d_%d" % (s, t)) for t in range(2)] for s in range(3)]
        B_uu = [[Buf("uu%d_%d" % (s, t)) for t in range(2)] for s in range(3)]
        B_wst = [Buf("wst%d" % i) for i in range(3)]
        B_xt = [Buf("xt0"), Buf("xt1")]
        B_xnb = Buf("xnb")
        B_qT = [Buf("qT0"), Buf("qT1")]
        B_saz = [Buf("saz0"), Buf("saz1")]
        B_rt = [Buf("rt0"), Buf("rt1")]
        B_ay = [[Buf("ay%d_%d" % (c, t)) for t in range(2)] for c in range(8)]
        B_py = Buf("py")
        B_mT = [Buf("mT0"), Buf("mT1")]
        B_dT = Buf("dT")
        B_spz = Buf("spz")
        B_fa = [Buf("fa%d" % i) for i in range(3)]
        B_fb = [Buf("fb%d" % i) for i in range(2)]
        B_bfs = [Buf("bfs%d" % i) for i in range(3)]
        B_eb = [Buf("eb0"), Buf("eb1")]
        B_exb = [Buf("exb0"), Buf("exb1")]
        B_pt = [Buf("pt%d" % i) for i in range(6)]
        B_ptx = [Buf("ptx%d" % i) for i in range(6)]
        B_rd = [Buf("rd0"), Buf("rd1")]
        B_wgt = [Buf("wgt0"), Buf("wgt1")]
        B_xr = [Buf("xr0"), Buf("xr1")]
        B_gq = [Buf("gq0"), Buf("gq1")]
        B_const = Buf("const")
        C_ident = Buf("c_ident"); C_identf = Buf("c_identf"); C_band = Buf("c_band"); C_g64 = Buf("c_g64")
        C_ones = Buf("c_ones"); C_nhalf = Buf("c_nhalf"); C_gq = Buf("c_gqbc"); C_gk = Buf("c_gkbc")
        C_pw = Buf("c_pw"); C_ng = Buf("c_ng"); C_psc = Buf("c_psc"); C_gs = Buf("c_gs"); C_sh = Buf("c_sh")
        C_sT = Buf("c_sT"); C_s48 = [Buf("c_s48_%d" % i) for i in range(3)]; C_s16 = [Buf("c_s16_0"), Buf("c_s16_1")]
        C_badab = Buf("c_badab"); C_adab = [Buf("c_adab0"), Buf("c_adab1")]
        B_yo = [Buf("yo0"), Buf("yo1")]
        B_ss = Buf("ss")
        B_ss8 = [Buf("ss8_%d" % i) for i in range(4)]
        B_wsc = [Buf("wsc%d" % g) for g in range(NGRP)]
        B_ada = Buf("ada")
        B_adad = Buf("adad")
        B_rp2 = Buf("rp2")
        B_rtab = Buf("rtab")
        B_y = Buf("y")
        B_misc = Buf("misc")

        ctr = {"fa": 0, "fb": 0, "bfs": 0, "pj": 0, "trp": 0, "eb": 0, "pt": 0, "rd": 0,
               "xr": 0, "gq": 0, "S": 0, "OD": 0, "wst": 0, "ss8": 0, "xt": 0, "rt": 0}

        def rot(name, n):
            i = ctr[name] % n
            ctr[name] += 1
            return i

        S_wst = [P.new_sem("wst%d" % i) for i in range(3)]
        S_xt = [P.new_sem("xt%d" % i) for i in range(2)]
        S_xr = [P.new_sem("xr%d" % i) for i in range(2)]
        S_gq = [P.new_sem("gq%d" % i) for i in range(2)]
        S_rt = [P.new_sem("rt%d" % i) for i in range(2)]
        S_yo = [P.new_sem("yo0"), P.new_sem("yo1")]

        def MM(out, lhsT, rhs, start, stop, r, w, tp=None):
            if tp is None:
                P.op("pe", lambda e: e.matmul(out, lhsT=lhsT, rhs=rhs, start=start, stop=stop), r, w)
            else:
                P.op("pe", lambda e: e.matmul(out, lhsT=lhsT, rhs=rhs, start=start, stop=stop,
                                              tile_position=tp), r, w)

        def TR(out, in_, idn, r, w):
            P.op("pe", lambda e: e.transpose(out, in_, idn), r, w)

        def ACT(out, in_, func, r, w, scale=1.0, bias=None, accum=None):
            kw = {}
            if bias is not None:
                kw["bias"] = bias
            if accum is not None:
                kw["accum_out"] = accum
            P.op("act", lambda e: e.activation(out=out, in_=in_, func=func, scale=scale, **kw), r, w)

        def TS(eng, out, in0, s1, s2, op0, op1, r, w):
            if s2 is None:
                P.op(eng, lambda e: e.tensor_scalar(out=out, in0=in0, scalar1=s1, scalar2=None, op0=op0), r, w)
            else:
                P.op(eng, lambda e: e.tensor_scalar(out=out, in0=in0, scalar1=s1, scalar2=s2, op0=op0, op1=op1), r, w)

        def TT(eng, out, in0, in1, op, r, w):
            P.op(eng, lambda e: e.tensor_tensor(out=out, in0=in0, in1=in1, op=op), r, w)

        def STT(eng, out, in0, scalar, in1, op0, op1, r, w):
            P.op(eng, lambda e: e.scalar_tensor_tensor(out=out, in0=in0, scalar=scalar, in1=in1, op0=op0, op1=op1), r, w)

        def CP(eng, out, in_, r, w):
            P.op(eng, lambda e: e.tensor_copy(out=out, in_=in_), r, w)

        def MSET(eng, ap, val, r, w):
            P.op(eng, lambda e: e.memset(ap, val), r, w)

        nsem = [0]

        def fresh():
            nsem[0] += 1
            return P.new_sem("one%d" % nsem[0])

        def DMAn(q, out, in_, r, w, sem=None):
            if sem is None:
                sem = fresh()
            return P.dma(q, lambda e: e.dma_start(out=out, in_=in_), sem, r, w)

        def DMAG(items):
            sem = fresh()
            ev = None
            ws = []
            for (q, out, in_, r, w) in items:
                ev = DMAn(q, out, in_, r, w, sem=sem)
                ws.extend(w)
            for b in ws:
                b.w = ev

        def DMA(q, out, in_, sem, r, w, slow=False):
            if slow:
                P.dma(q, lambda e: e.dma_start(out=out, in_=in_, allow_slow_non_contiguous=True), sem, r, w)
            else:
                P.dma(q, lambda e: e.dma_start(out=out, in_=in_), sem, r, w)

        def body():
            cast_gate = []

            def cast_group(g, src_ap_2d, kchunks, dst_k0, sem):
                src = src_ap_2d.rearrange("(kc p) n -> p kc n", p=128)
                dst = wsc_d.ap()[g].rearrange("p (kc n) -> p kc n", n=512)
                step = 8
                ev = None
                for k0 in range(0, kchunks, step):
                    ev = DMAn("pool", dst[:, dst_k0 + k0:dst_k0 + k0 + step, :], src[:, k0:k0 + step, :], list(cast_gate), [Buf("dmy")], sem=sem)
                return ev

            def win_cols(g):
                return win_d.ap()[:, g * 512:(g + 1) * 512]

            def cast_phase(groups):
                sem = fresh()
                ev = None
                for g in groups:
                    ev = cast_group(g, win_cols(g), 16, 0, sem)
                for g in groups:
                    B_wsc[g].w = ev

            cast_phase([G_K])
            cast_phase([G_K + 1])

            def late_casts():
                for g_ in (G_V, G_V + 1, G_U, G_U + 1, G_Q, G_AZ, G_Q + 1, G_AZ + 1, G_PZ, G_PZ + 1):
                    cast_phase([g_])
                for mg in range(4):
                    cast_phase([G_GP + mg])
                    sem = fresh()
                    cast_group(G_UP + mg, wpu_d.ap()[:, mg * 512:(mg + 1) * 512], 8, 0, sem)
                    ev = cast_group(G_UP + mg, wau_d.ap()[:, mg * 512:(mg + 1) * 512], 8, 8, sem)
                    B_wsc[G_UP + mg].w = ev
                    cast_phase([G_GA + mg])
                for ngp in range(4):
                    sem = fresh()
                    ev = cast_group(G_WO + ngp, wo_d.ap()[:, ngp * 512:(ngp + 1) * 512], 16, 0, sem)
                    B_wsc[G_WO + ngp].w = ev
            DMAn("pool", pw_sb[:], pw_d.ap().rearrange("g (ic p) n -> p g ic n", p=128), [], [C_pw])

            if KLEVEL < 1:
                raise _Stop
            DMAG([("sp", ident[:], identbf_d.ap(), [], [C_ident]),
                  ("sp", identf[:], identf_d.ap(), [], [C_identf]),
                  ("sp", band[:], band_d.ap(), [], [C_band]),
                  ("sp", g64[:, 0, :], bass.AP(qg_d, 0, [[0, 128], [1, 64]]), [], [C_g64]),
                  ("sp", g64[:, 1, :], bass.AP(kg_d, 0, [[0, 128], [1, 64]]), [], [C_g64])])
            MSET("dve", ones[:], 1.0, [], [C_ones])
            MSET("dve", nhalf[:], -0.5, [], [C_nhalf])
            a0 = g64[:, 0, :]
            a1 = g64[:, 1, :]
            TS("dve", gqbc[:].rearrange("p (h d) -> p h d", d=64),
               bass.AP(a0.tensor, a0.offset, [list(a0.ap[0]), [0, 8], [1, 64]]), 1.0, None, ALU.mult, None, [C_g64], [C_gq])
            TS("dve", gkbc[:].rearrange("p (h d) -> p h d", d=64),
               bass.AP(a1.tensor, a1.offset, [list(a1.ap[0]), [0, 8], [1, 64]]), 8.0, None, ALU.mult, None, [C_g64], [C_gk])

            if KLEVEL < 2:
                raise _Stop
            DMAG([("sp", small48[:, 0, :], c_d.ap().rearrange("b (kc p) -> (b kc) p", p=128), [], [C_s48[0]]),
                  ("sp", small16[:, 0, :], ng_d.ap(), [], [C_s16[0]]),
                  ("sp", small16[0:8, 1, :], psc_d.ap(), [], [C_s16[1]])])
            pT = psb[0]
            TR(pT[:, 0:48], small48[:, 0, :], identf[0:48, 0:48], [C_s48[0], C_identf], [B_ps[0]])
            TR(pT[:, 48:64], small16[:, 0, :], identf[0:16, 0:16], [C_s16[0], C_identf], [B_ps[0]])
            TR(pT[:, 64:72], small16[0:8, 1, :], identf[0:8, 0:8], [C_s16[1], C_identf], [B_ps[0]])
            cTt = fa[:, 0, 0:48]
            CP("dve", cTt, pT[:, 0:48], [B_ps[0]], [B_fa[0]])
            CP("dve", ngT[:], pT[:, 48:64], [B_ps[0]], [C_ng])
            CP("dve", pscT[:], pT[:, 64:72], [B_ps[0]], [C_psc])
            tht = fa[:, 1, 0:48]
            ACT(tht, cTt, AF.Tanh, [B_fa[0]], [B_fa[1]], scale=0.5)
            STT("dve", sT[:], tht, 1.0, cTt, ALU.add, ALU.mult, [B_fa[0], B_fa[1]], [C_sT])

            if KLEVEL < 3:
                raise _Stop
            wst_f = [wst[:, i].rearrange("p a n -> p (a n)").bitcast(F32) for i in range(3)]
            sT3 = sT[:].rearrange("p (b k) -> p b k", k=16)
            B_adad_p = []
            S_badab = fresh()
            S_adst = [fresh(), fresh()]
            for half in range(2):
                banks = [0, 1, 2, 4, 5, 6]
                for kc in range(16):
                    sl = rot("wst", 3)
                    DMA("sp", wst_f[sl][:, 0:3072], wada_d.ap()[kc * 128:(kc + 1) * 128, half * 3072:(half + 1) * 3072],
                        S_wst[sl], [], [B_wst[sl]])
                    for n in range(6):
                        MM(psb[banks[n]][0:3, :], sT3[:, :, kc], wst_f[sl][:, n * 512:(n + 1) * 512], kc == 0, kc == 15,
                           [C_sT, B_wst[sl]], [B_ps[banks[n]]])
                for n in range(6):
                    col = half * 3072 + n * 512
                    ab = n % 2
                    DMAn("sp", badab, bass.AP(bada_d, col, [[0, 3], [1, 512]]), [], [C_badab], sem=S_badab)
                    STT("dve", adab[:, ab, :], psb[banks[n]][0:3, :], 0.5, badab, ALU.mult, ALU.add,
                        [B_ps[banks[n]], C_badab], [C_adab[ab]])
                    bp = Buf("adad%d" % col)
                    B_adad_p.append(bp)
                    DMAn("sp", ada_dd.ap()[:, col:col + 512], adab[:, ab, :], [C_adab[ab]], [bp], sem=S_adst[ab])
            gate_b = Buf("castgate")
            gate_b.w = B_wst[(ctr["wst"] - 1) % 3].w
            cast_gate.append(gate_b)
            late_casts()
            del cast_gate[:]
            for bp in B_adad_p:
                sk = bp.w[0]
                bp.w = (sk, P.dma_sems[sk], "dma")
            DMAG([("sp", small48[b * 16:(b + 1) * 16, 1, :], bass.AP(ada_dd, b * 3 * D, [[128, 16], [1, 128]]), B_adad_p, [C_s48[1]])
                  for b in range(3)])
            DMAG([("sp", small48[b * 16:(b + 1) * 16, 2, :], bass.AP(ada_dd, b * 3 * D + D, [[128, 16], [1, 128]]), B_adad_p, [C_s48[2]])
                  for b in range(3)])
            TR(pT[:, 0:48], small48[:, 1, :], identf[0:48, 0:48], [C_s48[1], C_identf], [B_ps[0]])
            TR(pT[:, 48:96], small48[:, 2, :], identf[0:48, 0:48], [C_s48[2], C_identf], [B_ps[0]])
            CP("dve", shT[:].rearrange("p b k -> p (b k)"), pT[:, 0:48], [B_ps[0]], [C_sh])
            for b in range(3):
                STT("dve", gsT[:, b, :], pT[:, 48 + 16 * b:64 + 16 * b], 1.0, ngT[:], ALU.add, ALU.mult,
                    [B_ps[0], C_ng], [C_gs])
            B_adad = Buf("adad_all")
            for bp in B_adad_p:
                if bp.w is not None:
                    B_adad.r.append(bp.w)

            B_rtab_h = [Buf("rtab%d" % h) for h in range(16)]

            def build_rtable():
                if KLEVEL < 5:
                    raise _Stop
                xt_flat = xt[:].rearrange("p a n -> p (a n)")
                rpt = xt_flat[0:16, 0:2400].rearrange("p (a b) -> p a b", b=160)
                rtmp = xt_flat[0:16, 2400:2400 + 465].rearrange("p (a b) -> p a b", b=31)
                MSET("pool", rpt, 0.0, [], [B_xt[0], B_xt[1]])
                DMAn("sp", rtmp, rpb_d.ap(), [], [B_xt[0], B_xt[1]])
                CP("pool", rpt[:, :, 64:95], rtmp[:, ::-1, :], [B_xt[0], B_xt[1]], [B_xt[0], B_xt[1]])
                DMAn("sp", rp2_d.ap(), rpt, [B_xt[0], B_xt[1]], [B_rp2])
                mT_f = mT[:].rearrange("p a b -> p (a b)").bitcast(F32)
                ay_f = attn_y[:].rearrange("p a b -> p (a b)").bitcast(F32)
                py_f = pool_y[:].rearrange("p a b -> p (a b)").bitcast(F32)
                qT_f = qT[:].rearrange("p a b c -> p (a b c)").bitcast(F32)
                dT_fl = dT[:].rearrange("p a b -> p (a b)")
                sp_fl = spzT[:].rearrange("p a b -> p (a b)")
                maskt = qT_f[:, 0:1024]
                DMAn("sp", maskt, maskr_d.ap(), [], [B_qT[0], B_qT[1]])
                MSET("pool", mT_f[:, 0:2048], 0.0, [], [B_mT[0], B_mT[1]])
                S_rb = [fresh(), fresh()]
                B_est = [Buf("est0"), B_py]
                B_rb = [B_dT, B_spz]
                S_stg = [P.new_sem("stg0"), P.new_sem("stg1")]
                ests = [ay_f[:, 0:1024], py_f[:, 0:1024]]
                rbufs = [dT_fl[:, 0:1024], sp_fl[:, 0:1024]]
                for h in range(16):
                    s_ = h % 2
                    stg = mT_f[:, s_ * 1024:(s_ + 1) * 1024]
                    stg3 = stg.rearrange("p (i c) -> p i c", c=64)
                    src = bass.AP(rp2_d, h * 15 * 160 + 16, [[1, 64], [160, 15], [1, 64]])
                    DMA("sp", stg3[0:64, 0:15, :], src, S_stg[s_], [B_rp2], [B_mT[s_]])
                    dmy = Buf("dmy")
                    ev_ = P.dma("sp", (lambda o_, i_: (lambda e: e.dma_start(out=o_, in_=i_)))(stg3[64:128, 1:16, :], src),
                                S_stg[s_], [B_rp2], [dmy])
                    B_mT[s_].w = ev_
                    ACT(ests[s_], stg, AF.Exp, [B_mT[s_]], [B_est[s_]])
                    TT("dve", rbufs[s_], ests[s_], maskt, ALU.mult, [B_est[s_], B_qT[0], B_qT[1]], [B_rb[s_]])
                    DMAn("sp", rtab_d.ap()[h // 2][:, (h % 2) * 1024:(h % 2 + 1) * 1024], rbufs[s_], [B_rb[s_]], [B_rtab_h[h]], sem=S_rb[s_])
                for h in range(16):
                    B_rtab_h[h].w = (S_rb[h % 2], P.dma_sems[S_rb[h % 2]], "dma")
                for c_ in range(8):
                    for t_ in range(2):
                        B_ay[c_][t_].r.extend(B_est[0].r)
                        if B_est[0].w is not None:
                            B_ay[c_][t_].r.append(B_est[0].w)

            for cb in C_s48 + C_s16:
                for gb_ in B_gq:
                    gb_.r.extend(cb.r)
                    if cb.w is not None:
                        gb_.r.append(cb.w)
            def load_w(g):
                sl = rot("wst", 3)
                DMA("sp", wst[:, sl].rearrange("p a n -> p (a n)"), wsc_d.ap()[g], S_wst[sl], [B_wsc[g]], [B_wst[sl]])
                return sl

            def proj(hslot, lt, sl, nk=16, k0=0, lhs_fn=None, lhs_bufs=None):
                flush()
                gen[0] += 1
                bk = PJ[rot("pj", 2)]
                for kc in range(nk):
                    if lhs_fn is None:
                        lhsT = hT[:, hslot, kc, lt * 128:(lt + 1) * 128]
                        rb = [B_hT[hslot][lt]]
                    else:
                        lhsT = lhs_fn(kc)
                        rb = lhs_bufs
                    MM(psb[bk][:], lhsT, wst[:, sl, k0 + kc, :], kc == 0, kc == nk - 1, rb + [B_wst[sl]], [B_ps[bk]])
                return bk

            deferred = []
            gen = [0]

            def flush(all_=False):
                keep = []
                for (g_, f_) in deferred:
                    if all_ or g_ <= gen[0] - 2:
                        f_()
                    else:
                        keep.append((g_, f_))
                deferred[:] = keep

            def transpose4(src_bf, src_buf, dst_ap, dst_bufs, eng_i):
                deferred.append((gen[0], lambda: transpose4_now(src_bf, src_buf, dst_ap, dst_bufs, eng_i)))

            def transpose4_now(src_bf, src_buf, dst_ap, dst_bufs, eng_i):
                th = rot("trp", 2)
                trp = trps[th]
                for c in range(4):
                    TR(trp[:, c * 128:(c + 1) * 128], src_bf[:, c * 128:(c + 1) * 128], ident[:],
                       [src_buf, C_ident], [B_trp[th]])
                src = trp[:, 0:512].rearrange("p (c t) -> p c t", t=128)
                if eng_i % 2 == 0:
                    ACT(dst_ap, src, AF.Copy, [B_trp[th]], dst_bufs)
                else:
                    CP("dve", dst_ap, src, [B_trp[th]], dst_bufs)

            def headnorm(bk, gbc, gbuf, dst_ap, dst_bufs, eng_i, early=False):
                ia = rot("fa", 3)
                ib = rot("fb", 2)
                ic = rot("bfs", 3)
                i8 = rot("ss8", 4)
                ACT(fa[:, ia, :], psb[bk][:], AF.Copy, [B_ps[bk]], [B_fa[ia]])
                ACT(fb[:, ib, :], fa[:, ia, :], AF.Square, [B_fa[ia]], [B_fb[ib]])
                P.op("dve", lambda e: e.tensor_reduce(out=ss8[:, i8, :], in_=fb[:, ib, :].rearrange("p (h d) -> p h d", d=64),
                                                      axis=AX.X, op=ALU.add), [B_fb[ib]], [B_ss8[i8]])
                if early:
                    TS("dve", ss8[:, i8, :], ss8[:, i8, :], 64.0 * EPS, None, ALU.add, None, [B_ss8[i8]], [B_ss8[i8]])
                    ACT(ss8[:, i8, :], ss8[:, i8, :], AF.Ln, [B_ss8[i8]], [B_ss8[i8]])
                    ACT(ss8[:, i8, :], ss8[:, i8, :], AF.Exp, [B_ss8[i8]], [B_ss8[i8]], scale=-0.5)
                else:
                    TS("dve", ss8[:, i8, :], ss8[:, i8, :], 64.0 * EPS, None, ALU.add, None, [B_ss8[i8]], [B_ss8[i8]])
                    TT("pool", ss8[:, i8, :], ss8[:, i8, :], nhalf[:], ALU.pow, [B_ss8[i8], C_nhalf], [B_ss8[i8]])
                TT("dve", fb[:, ib, :].rearrange("p (h d) -> p h d", d=64),
                   fa[:, ia, :].rearrange("p (h d) -> p h d", d=64), bc_last(ss8[:, i8, :], 64), ALU.mult,
                   [B_fa[ia], B_ss8[i8]], [B_fb[ib]])
                TT("dve" if early else "pool", bfs[:, ic, :], fb[:, ib, :], gbc[:], ALU.mult, [B_fb[ib], gbuf], [B_bfs[ic]])
                transpose4(bfs[:, ic, :], B_bfs[ic], dst_ap, dst_bufs, eng_i)

            def silu2(bk, dst_ap, dst_bufs, eng_i):
                ia = rot("fa", 3)
                ic = rot("bfs", 3)
                ACT(fa[:, ia, :], psb[bk][:], AF.Tanh, [B_ps[bk]], [B_fa[ia]], scale=0.5)
                STT("dve", bfs[:, ic, :], fa[:, ia, :], 1.0, psb[bk][:], ALU.add, ALU.mult, [B_fa[ia], B_ps[bk]], [B_bfs[ic]])
                transpose4(bfs[:, ic, :], B_bfs[ic], dst_ap, dst_bufs, eng_i)

            def rslot_of(gt):
                return (gt // 2) % 3

            xt_of = {}

            def frontend_load(gb):
                for lt in range(2):
                    gt = gb * 2 + lt
                    xi = rot("xt", 2)
                    xt_of[gt] = xi
                    DMA("sp", xt[:, xi, :], x_d.ap()[gt * 128:(gt + 1) * 128, :], S_xt[xi], [], [B_xt[xi]])

            def fe_prep(gb, lt):
                gt = gb * 2 + lt
                xi = xt_of[gt]
                col = gt % 4
                MSET("dve", ssx[:, col:col + 1], 0.0, [], [B_ss])
                ACT(xnb[:], xt[:, xi, :], AF.Square, [B_xt[xi], B_ss], [B_xnb, B_ss], accum=ssx[:, col:col + 1])
                TS("dve", ssx[:, col:col + 1], ssx[:, col:col + 1], 1.0 / D, EPS, ALU.mult, ALU.add, [B_ss], [B_ss])
                if gb < 2:
                    ACT(ssx[:, col:col + 1], ssx[:, col:col + 1], AF.Ln, [B_ss], [B_ss])
                    ACT(ssx[:, col:col + 1], ssx[:, col:col + 1], AF.Exp, [B_ss], [B_ss], scale=-0.5)
                else:
                    TT("pool", ssx[:, col:col + 1], ssx[:, col:col + 1], nhalf[:, 0:1], ALU.pow, [B_ss, C_nhalf], [B_ss])
                ACT(xnb[:], xt[:, xi, :], AF.Copy, [B_xt[xi], B_ss], [B_xnb], scale=ssx[:, col:col + 1])

            def fe_round(gb, lt, q4):
                hs = gb % 2
                gt = gb * 2 + lt
                sq = seq_of_tile(gt)
                th = rot("trp", 2)
                for c in range(4):
                    kc = q4 * 4 + c
                    TR(trps[th][:, c * 128:(c + 1) * 128], xnb[:, kc * 128:(kc + 1) * 128], ident[:],
                       [B_xnb, C_ident], [B_trp[th]])
                for c in range(4):
                    kc = q4 * 4 + c
                    src = trps[th][:, c * 128:(c + 1) * 128]
                    dst = hT[:, hs, kc, lt * 128:(lt + 1) * 128]
                    if c % 2 == 0:
                        ACT(dst, src, AF.Identity, [B_trp[th], C_gs, C_sh], [B_hT[hs][lt]],
                            scale=gsT[:, sq, kc:kc + 1], bias=shT[:, sq, kc:kc + 1])
                    else:
                        TS("dve", dst, src, gsT[:, sq, kc:kc + 1], shT[:, sq, kc:kc + 1], ALU.mult, ALU.add,
                           [B_trp[th], C_gs, C_sh], [B_hT[hs][lt]])

            def frontend_tile(gb, lt):
                fe_prep(gb, lt)
                for q4 in range(4):
                    fe_round(gb, lt, q4)

            def ahead_proj(gb):
                hs = gb % 2
                rs = gb % 3
                for kg in range(2):
                    sl = load_w(G_K + kg)
                    for lt in range(2):
                        bk = proj(hs, lt, sl)
                        dst = kT[:, rs, 4 * kg:4 * kg + 4, lt * 128:(lt + 1) * 128]
                        headnorm(bk, gkbc, C_gk, dst, [B_kT[rs][lt]], lt, early=(gb < 2))
                for (gbase, ring, Bring) in ((G_V, vv, B_vv), (G_U, uu, B_uu)):
                    for g2 in range(2):
                        sl = load_w(gbase + g2)
                        for lt in range(2):
                            bk = proj(hs, lt, sl)
                            dst = ring[:, rs, lt, g2 * 512:(g2 + 1) * 512]
                            if lt == 0:
                                ACT(dst, psb[bk][:], AF.Copy, [B_ps[bk]], [Bring[rs][lt]])
                            else:
                                CP("dve", dst, psb[bk][:], [B_ps[bk]], [Bring[rs][lt]])

            S4 = [0, 1, 2, 3, 4, 5]

            def att_qk(lb, sq, hp, lt, ri):
                qb, c4 = hp // 4, hp % 4
                gt = lb * 2 + lt
                j = gt - SEQ_T0[sq]
                J = SEQ_J[sq]
                if j == 0:
                    olist, i0, interior = [3, 2, 1, 0], 1, False
                elif j == 1:
                    olist, i0, interior = [2, 1, 0, -1], 3, False
                elif j == J - 2:
                    olist, i0, interior = [1, 0, -1, -2], 5, False
                elif j == J - 1:
                    olist, i0, interior = [0, -1, -2, -3], 7, False
                else:
                    olist, i0, interior = [1, 0, -1, -2], 5, True
                odb = OD_BANK[rot("OD", 2)]
                OD = psb[odb]
                sbis = [S4[rot("S", 6)], S4[rot("S", 6)]]
                for a, o in enumerate(olist):
                    gk = gt + o
                    krs, klt = rslot_of(gk), gk % 2
                    for hh in range(2):
                        p0 = 64 * hh
                        MM(psb[sbis[hh]][:, a * 128:(a + 1) * 128], kT[p0:p0 + 64, krs, hp, klt * 128:(klt + 1) * 128],
                           qT[p0:p0 + 64, qb, c4, lt * 128:(lt + 1) * 128], True, True,
                           [B_kT[krs][klt], B_qT[qb]], [B_ps[sbis[hh]]], tp=(p0, 0))
                if interior:
                    gk2 = gt + 2
                    krs2, klt2 = rslot_of(gk2), gk2 % 2
                    for hh in range(2):
                        p0 = 64 * hh
                        MM(psb[sbis[hh]][0:64, 448:512], kT[p0:p0 + 64, krs2, hp, klt2 * 128:klt2 * 128 + 64],
                           qT[p0:p0 + 64, qb, c4, lt * 128 + 64:(lt + 1) * 128], True, True,
                           [B_kT[krs2][klt2], B_qT[qb]], [B_ps[sbis[hh]]], tp=(p0, 0))
                pts = []
                for hh in range(2):
                    h = 2 * hp + hh
                    p0 = 64 * hh
                    sbi = sbis[hh]
                    S = psb[sbi]
                    ei = rot("eb", 2)
                    ACT(eb[:, ei, :], S[:], AF.Exp, [B_ps[sbi]], [B_eb[ei]])
                    pi = rot("pt", 6)
                    TT("dve", pt[:, pi, :].rearrange("p (i c) -> p i c", c=64),
                       eb[:, ei, :].rearrange("p (i c) -> p i c", c=64),
                       rt[:, ri, hh, i0:i0 + 8, ::-1], ALU.mult, [B_eb[ei], B_rt[ri]], [B_pt[pi]])
                    if interior:
                        TT("dve", ptx[:, pi, :], eb[0:64, ei, 448:512], rt[0:64, ri, hh, 4, ::-1], ALU.mult,
                           [B_eb[ei], B_rt[ri]], [B_ptx[pi]])
                        MSET("dve", pt[0:64, pi, 448:512], 0.0, [], [B_pt[pi]])
                    pts.append((pi, hh, h, p0))
                return (gt, hp, lt, qb, c4, olist, interior, odb, pts)

            def att_pv(st_):
                (gt, hp, lt, qb, c4, olist, interior, odb, pts) = st_
                OD = psb[odb]
                n = len(olist)
                for (pi, hh, h, p0) in pts:
                    for a, o in enumerate(olist):
                        gk = gt + o
                        krs, klt = rslot_of(gk), gk % 2
                        last = (a == n - 1) and not interior
                        MM(OD[p0:p0 + 64, 0:128], vv[:, krs, klt, h * 64:(h + 1) * 64], pt[:, pi, a * 128:(a + 1) * 128],
                           a == 0, last, [B_vv[krs][klt], B_pt[pi]], [B_ps[odb]], tp=(0, p0))
                    if interior:
                        gk = gt + 2
                        krs, klt = rslot_of(gk), gk % 2
                        MM(OD[p0:p0 + 64, 64:128], vv[0:64, krs, klt, h * 64:(h + 1) * 64], ptx[:, pi, :],
                           False, True, [B_vv[krs][klt], B_ptx[pi]], [B_ps[odb]], tp=(0, p0))
                    for a, o in enumerate(olist):
                        last = (a == n - 1) and not interior
                        MM(OD[p0:p0 + 64, 128:256], ones[:, 0:64], pt[:, pi, a * 128:(a + 1) * 128],
                           a == 0, last, [C_ones, B_pt[pi]], [B_ps[odb]], tp=(0, p0))
                    if interior:
                        MM(OD[p0:p0 + 64, 192:256], ones[0:64, 0:64], ptx[:, pi, :],
                           False, True, [C_ones, B_ptx[pi]], [B_ps[odb]], tp=(0, p0))
                di = rot("rd", 2)
                P.op("dve", lambda e: e.reciprocal(out=rd[:, di, :], in_=OD[:, 128:256]), [B_ps[odb]], [B_rd[di]])
                TT("dve", wgt[:, di, :], rd[:, di, :], sazT[:, qb, c4, lt * 128:(lt + 1) * 128], ALU.mult,
                   [B_rd[di], B_saz[qb]], [B_wgt[di]])
                TT("dve", attn_y[:, hp, lt * 128:(lt + 1) * 128], OD[:, 0:128], wgt[:, di, :], ALU.mult,
                   [B_ps[odb], B_wgt[di]], [B_ay[hp][lt]])

            def lagged(lb):
                hs = lb % 2
                sq = seq_of_tile(lb * 2)
                nb = lb + 2
                rt_next = []

                def rt_load(hp_):
                    ri_ = rot("rt", 2)
                    DMA("pool", rt[:, ri_].rearrange("p a i c -> p (a i c)"), rtab_d.ap()[hp_], S_rt[ri_],
                        [B_rtab_h[2 * hp_], B_rtab_h[2 * hp_ + 1]], [B_rt[ri_]])
                    rt_next.append(ri_)

                rt_load(0)
                for qg in range(2):
                    qb = qg
                    sl = load_w(G_Q + qg)
                    for lt in range(2):
                        bk = proj(hs, lt, sl)
                        headnorm(bk, gqbc, C_gq, qT[:, qb, :, lt * 128:(lt + 1) * 128], [B_qT[qb]], lt, early=(lb < 1))
                    sl = load_w(G_AZ + qg)
                    for lt in range(2):
                        bk = proj(hs, lt, sl)
                        silu2(bk, sazT[:, qb, :, lt * 128:(lt + 1) * 128], [B_saz[qb]], lt + 1)
                flush(True)
                pend = []
                for hp in range(8):
                    ri = rt_next.pop(0)
                    if hp + 1 < 8:
                        rt_load(hp + 1)
                    for lt in range(2):
                        pend.append(att_qk(lb, sq, hp, lt, ri))
                        if len(pend) > 2:
                            att_pv(pend.pop(0))
                while pend:
                    att_pv(pend.pop(0))
                for pg in range(2):
                    sl = load_w(G_PZ + pg)
                    for lt in range(2):
                        bk = proj(hs, lt, sl)
                        silu2(bk, spzT[:, 4 * pg:4 * pg + 4, lt * 128:(lt + 1) * 128], [B_spz], lt)
                flush(True)
                for ch in range(8):
                    g = ch // 2
                    bk = PJ[rot("pj", 2)]
                    for lt in range(2):
                        gt = lb * 2 + lt
                        j = gt - SEQ_T0[sq]
                        J = SEQ_J[sq]
                        srcs = []
                        if j > 0:
                            srcs.append((gt - 1, g * 5 + 0))
                        srcs.append((gt, g * 5 + (3 if j == 0 else (4 if j == J - 1 else 1))))
                        if j < J - 1:
                            srcs.append((gt + 1, g * 5 + 2))
                        for si, (sg, bi) in enumerate(srcs):
                            srs, slt = rslot_of(sg), sg % 2
                            MM(psb[bk][:, lt * 128:(lt + 1) * 128], uu[:, srs, slt, ch * 128:(ch + 1) * 128], band[:, bi, :],
                               si == 0, si == len(srcs) - 1, [B_uu[srs][slt], C_band], [B_ps[bk]])
                    if ch % 2 == 0:
                        ACT(dT[:, ch, :], psb[bk][:, 0:256], AF.Copy, [B_ps[bk]], [B_dT])
                    else:
                        CP("dve", dT[:, ch, :], psb[bk][:, 0:256], [B_ps[bk]], [B_dT])
                for oc in range(8):
                    g = oc // 2
                    bk = PJ[rot("pj", 2)]
                    for ic in range(2):
                        MM(psb[bk][:, 0:256], pw_sb[:, g, ic, (oc % 2) * 128:(oc % 2 + 1) * 128], dT[:, 2 * g + ic, :],
                           ic == 0, ic == 1, [B_dT, C_pw], [B_ps[bk]])
                    STT("dve", pool_y[:, oc, :], psb[bk][:, 0:256], pscT[:, oc:oc + 1], spzT[:, oc, :], ALU.mult, ALU.mult,
                        [B_ps[bk], C_psc, B_spz], [B_py])
                if nb < NBLK:
                    frontend_load(nb)
                ayb = [[B_ay[c][lt] for c in range(8)] for lt in range(2)]
                for mg in range(4):
                    slp = load_w(G_GP + mg)
                    tg = []
                    for lt in range(2):
                        bgp = proj(hs, lt, slp)
                        ia = rot("fa", 3)
                        ACT(fa[:, ia, :], psb[bgp][:], AF.Tanh, [B_ps[bgp]], [B_fa[ia]], scale=0.5)
                        tg.append(ia)
                    slu = load_w(G_UP + mg)
                    t1 = []
                    for lt in range(2):
                        bpu = proj(hs, lt, slu, nk=8, k0=0,
                                   lhs_fn=lambda kc: pool_y[:, kc, lt * 128:(lt + 1) * 128], lhs_bufs=[B_py])
                        ib = rot("fb", 2)
                        STT("dve", fb[:, ib, :], fa[:, tg[lt], :], 1.0, psb[bpu][:], ALU.add, ALU.mult,
                            [B_fa[tg[lt]], B_ps[bpu]], [B_fb[ib]])
                        t1.append(ib)
                    sla = load_w(G_GA + mg)
                    tg2 = []
                    for lt in range(2):
                        bga = proj(hs, lt, sla)
                        ia2 = rot("fa", 3)
                        ACT(fa[:, ia2, :], psb[bga][:], AF.Tanh, [B_ps[bga]], [B_fa[ia2]], scale=0.5)
                        tg2.append(ia2)
                    for lt in range(2):
                        bau = proj(hs, lt, slu, nk=8, k0=8,
                                   lhs_fn=lambda kc: attn_y[:, kc, lt * 128:(lt + 1) * 128], lhs_bufs=ayb[lt])
                        ia2 = tg2[lt]
                        STT("dve", fa[:, ia2, :], fa[:, ia2, :], 1.0, psb[bau][:], ALU.add, ALU.mult,
                            [B_fa[ia2], B_ps[bau]], [B_fa[ia2]])
                        ic = rot("bfs", 3)
                        TT("pool", bfs[:, ic, :], fb[:, t1[lt], :], fa[:, ia2, :], ALU.add, [B_fb[t1[lt]], B_fa[ia2]], [B_bfs[ic]])
                        transpose4(bfs[:, ic, :], B_bfs[ic], mT[:, 4 * mg:4 * mg + 4, lt * 128:(lt + 1) * 128], [B_mT[lt]], lt)
                flush(True)
                fe_items = []
                if nb < NBLK:
                    fe_prep(nb, 0)
                    fe_items = [[lambda: fe_round(nb, 0, 0)], [lambda: fe_round(nb, 0, 1)], [lambda: fe_round(nb, 0, 2)],
                                [lambda: fe_round(nb, 0, 3), lambda: fe_prep(nb, 1)], [],
                                [lambda: fe_round(nb, 1, 0)], [lambda: fe_round(nb, 1, 1)],
                                [lambda: fe_round(nb, 1, 2), lambda: fe_round(nb, 1, 3)]]
                for ngp in range(4):
                    sl = load_w(G_WO + ngp)
                    gi = rot("gq", 2)
                    DMA("sp", gq[:, gi, :], bass.AP(ada_dd, sq * 3 * D + 2 * D + ngp * 512, [[0, 128], [1, 512]]),
                        S_gq[gi], [B_adad], [B_gq[gi]])
                    for lt in range(2):
                        gt = lb * 2 + lt
                        xi = rot("xr", 2)
                        DMA("sp", xr[:, xi, :], x_d.ap()[gt * 128:(gt + 1) * 128, ngp * 512:(ngp + 1) * 512], S_xr[xi],
                            [], [B_xr[xi]])
                        bk = proj(hs, lt, sl, lhs_fn=lambda kc: mT[:, kc, lt * 128:(lt + 1) * 128], lhs_bufs=[B_mT[lt]])
                        ia = rot("fa", 3)
                        STT("dve", fa[:, ia, :], psb[bk][:], 0.25, gq[:, gi, :], ALU.mult, ALU.mult,
                            [B_ps[bk], B_gq[gi]], [B_fa[ia]])
                        TT("pool", xr[:, xi, :], fa[:, ia, :], xr[:, xi, :], ALU.add, [B_fa[ia], B_xr[xi]], [B_xr[xi]])
                        DMA("pool", y_d.ap()[gt * 128:(gt + 1) * 128, ngp * 512:(ngp + 1) * 512], xr[:, xi, :], S_yo[xi],
                            [B_xr[xi]], [B_yo[xi]])
                        if fe_items:
                            for f_ in fe_items.pop(0):
                                f_()

            if KLEVEL < 6:
                raise _Stop
            frontend_load(0)
            frontend_tile(0, 0)
            frontend_tile(0, 1)
            for s in range(NBLK + 1):
                if s < NBLK:
                    if KLEVEL < 10 + 2 * s:
                        raise _Stop
                    ahead_proj(s)
                if s == 0:
                    frontend_load(1)
                    frontend_tile(1, 0)
                    frontend_tile(1, 1)
                    build_rtable()
                flush(True)
                if s >= 1:
                    if KLEVEL < 10 + 2 * s + 1:
                        raise _Stop
                    lagged(s - 1)
        try:
            body()
        except _Stop:
            pass
        P.finish("sp")
        P.finish("pool")
        P.emit()
    return nc


_CACHE = {}


def kernel(x_prompt, x_sample, c_prompt, c_sample, w_ada, b_ada, norm_g, w_in, pool_w, pool_scale,
           q_norm_g, k_norm_g, rpb, w_pool_up, w_attn_up, w_o):
    f32 = np.float32
    xp = np.asarray(x_prompt, f32)
    xs = np.asarray(x_sample, f32)
    cp = np.asarray(c_prompt, f32)
    cs = np.asarray(c_sample, f32)
    if "nc" not in _CACHE:
        _CACHE["nc"] = build_program()
        _CACHE["consts"] = host_consts()
    nc = _CACHE["nc"]
    consts = _CACHE["consts"]
    shared = {
        "w_ada": np.ascontiguousarray(np.asarray(w_ada, f32)[0]),
        "b_ada": np.ascontiguousarray(np.asarray(b_ada, f32)[0].reshape(1, 3 * D)),
        "norm_g": np.ascontiguousarray(np.asarray(norm_g, f32)[0].reshape(16, 128)),
        "w_in": np.ascontiguousarray(np.asarray(w_in, f32)[0]),
        "pool_w": np.ascontiguousarray(np.asarray(pool_w, f32)[0]),
        "pool_scale": np.ascontiguousarray(np.asarray(pool_scale, f32)[0].reshape(8, 128)),
        "q_norm_g": np.ascontiguousarray(np.asarray(q_norm_g, f32)[0].reshape(1, 64)),
        "k_norm_g": np.ascontiguousarray(np.asarray(k_norm_g, f32)[0].reshape(1, 64)),
        "rpb": np.ascontiguousarray(np.asarray(rpb, f32)[0]),
        "w_pool_up": np.ascontiguousarray(np.asarray(w_pool_up, f32)[0]),
        "w_attn_up": np.ascontiguousarray(np.asarray(w_attn_up, f32)[0]),
        "w_o": np.ascontiguousarray(np.asarray(w_o, f32)[0]),
    }
    shared.update(consts)
    in_maps = []
    for c in range(8):
        xc = np.concatenate([xp[2 * c], xp[2 * c + 1], xs[c]], axis=0)
        cc = np.stack([cp[2 * c], cp[2 * c + 1], cs[c]], axis=0)
        m = dict(shared)
        m["x"] = np.ascontiguousarray(xc)
        m["c3"] = np.ascontiguousarray(cc)
        in_maps.append(m)
    res = run_bass_kernel_spmd(nc, in_maps, core_ids=list(range(8)))
    y_prompt = np.empty((16, 2048, D), f32)
    y_sample = np.empty((8, 4096, D), f32)
    for c in range(8):
        y = np.asarray(res.results[c]["y"], f32)
        y_prompt[2 * c] = y[0:2048]
        y_prompt[2 * c + 1] = y[2048:4096]
        y_sample[c] = y[4096:8192]
    return (y_prompt, y_sample)
```

```python
import numpy as np
import ml_dtypes
from contextlib import ExitStack
import concourse.bass as bass
import concourse.mybir as mybir
from concourse.bass_utils import run_bass_kernel_spmd

F32 = mybir.dt.float32
BF16 = mybir.dt.bfloat16
ALU = mybir.AluOpType
AF = mybir.ActivationFunctionType
AX = mybir.AxisListType

D = 2048
NTILE = 64
NBLK = 32
SEQ_T0 = (0, 16, 32)
SEQ_J = (16, 16, 32)
EPS = 1e-6
POOL_WINDOWS = (2, 4, 8, 16)
G_U, G_PZ, G_Q, G_K, G_V, G_AZ, G_GP, G_GA, G_WO, G_UP = 0, 2, 4, 6, 8, 10, 12, 16, 20, 24
NGRP = 28

ENGS = ("pe", "act", "dve", "pool", "sp")


class Buf:
    __slots__ = ("name", "w", "r")

    def __init__(self, name):
        self.name = name
        self.w = None
        self.r = []


class Prog:
    def __init__(self, nc):
        self.nc = nc
        self.ops = {e: [] for e in ENGS}
        self.cnt = {e: 0 for e in ENGS}
        self.seen = {e: {} for e in ENGS}
        self.dma_sems = {}

    def new_sem(self, name):
        self.dma_sems[name] = 0
        return name

    def _collect(self, eng, reads, writes, is_dma):
        seen = self.seen[eng]
        best = {}

        def add(ev):
            sk, val, e2 = ev
            if seen.get(sk, 0) >= val:
                return
            if best.get(sk, 0) < val:
                best[sk] = val

        for b in reads:
            if b.w is not None:
                if is_dma or not (b.w[2] == eng and eng == "pe"):
                    add(b.w)
        strict = is_dma or eng != "pe"
        for b in writes:
            if b.w is not None and (strict or b.w[2] != eng):
                add(b.w)
            for ev in b.r:
                if strict or ev[2] != eng:
                    add(ev)
        waits = []
        for sk, val in best.items():
            seen[sk] = val
            waits.append((sk, val))
        return waits

    def op(self, eng, fn, reads=(), writes=()):
        waits = self._collect(eng, reads, writes, False)
        self.cnt[eng] += 1
        ev = (eng, self.cnt[eng], eng)
        self.ops[eng].append((waits, fn, (eng, 1)))
        for b in reads:
            b.r.append(ev)
        for b in writes:
            b.w = ev
            b.r = []
        return ev

    def dma(self, eng, fn, semkey, reads=(), writes=()):
        waits = self._collect(eng, reads, writes, True)
        self.dma_sems[semkey] += 16
        ev = (semkey, self.dma_sems[semkey], "dma")
        self.ops[eng].append((waits, fn, (semkey, 16)))
        for b in reads:
            b.r.append(ev)
        for b in writes:
            b.w = ev
            b.r = []
        return ev

    def wait_all(self, eng, bufs):
        seen = self.seen[eng]
        best = {}
        for b in bufs:
            evs = list(b.r)
            if b.w is not None:
                evs.append(b.w)
            for (sk, val, e2) in evs:
                if seen.get(sk, 0) >= val:
                    continue
                if best.get(sk, 0) < val:
                    best[sk] = val
        waits = []
        for sk, val in best.items():
            seen[sk] = val
            waits.append((sk, val))
        self.ops[eng].append((waits, None, None))

    def finish(self, eng="sp"):
        waits = []
        for e in ENGS:
            if self.cnt[e] > 0:
                waits.append((e, self.cnt[e]))
        for k, v in self.dma_sems.items():
            if v > 0:
                waits.append((k, v))
        self.ops[eng].append((waits, None, None))

    def emit(self):
        nc = self.nc
        with ExitStack() as st:
            sems = {}
            for e in ENGS:
                sems[e] = st.enter_context(nc.semaphore("s_" + e))
            for k in self.dma_sems:
                sems[k] = st.enter_context(nc.semaphore("s_" + k))
            block = st.enter_context(nc.Block())
            handles = {"pe": block.tensor, "act": block.scalar, "dve": block.vector,
                       "pool": block.gpsimd, "sp": block.sync}
            for e in ENGS:
                ops = self.ops[e]
                if not ops:
                    continue

                def body(h, ops=ops):
                    for (waits, fn, inc) in ops:
                        for (sk, val) in waits:
                            h.wait_ge(sems[sk], val)
                        if fn is None:
                            continue
                        fn(h).then_inc(sems[inc[0]], inc[1])

                handles[e](body)


def bc_last(ap, n):
    return bass.AP(ap.tensor, ap.offset, [list(x) for x in ap.ap] + [[0, n]])


def seq_of_tile(gt):
    return 0 if gt < 16 else (1 if gt < 32 else 2)


def host_consts():
    bf = ml_dtypes.bfloat16
    ident = np.eye(128, dtype=np.float32)
    band = np.zeros((128, 20, 128), np.float64)
    for g, w in enumerate(POOL_WINDOWS):
        h = w // 2
        prev = band[:, g * 5 + 0, :]
        cur = band[:, g * 5 + 1, :]
        nxt = band[:, g * 5 + 2, :]
        first = band[:, g * 5 + 3, :]
        last = band[:, g * 5 + 4, :]
        for t in range(128):
            lo, hi = t - h, t + h - 1
            for tp in range(lo, hi + 1):
                if tp < 0:
                    prev[128 + tp, t] += 1.0 / w
                elif tp > 127:
                    nxt[tp - 128, t] += 1.0 / w
                else:
                    cur[tp, t] += 1.0 / w
            cur[t, t] -= 1.0
            lo2 = max(lo, 0)
            cnt = hi - lo2 + 1
            for tp in range(lo2, min(hi, 127) + 1):
                first[tp, t] += 1.0 / cnt
            first[t, t] -= 1.0
            hi2 = min(hi, 127)
            cnt = hi2 - lo + 1
            for tp in range(max(lo, 0), hi2 + 1):
                last[tp, t] += 1.0 / cnt
            last[t, t] -= 1.0
    maskr = np.zeros((128, 16, 64), np.float32)
    for p in range(128):
        krl, kc = p // 64, p % 64
        for cp in range(64):
            c = 63 - cp
            cs = min(max(c - 8, 0), 48)
            if cs <= kc < cs + 16:
                for i in range(16):
                    ok = (i <= 14) if krl == 0 else (i >= 1)
                    if ok:
                        maskr[p, i, cp] = 1.0
    return {
        "ident_bf": ident.astype(bf),
        "ident_f": ident,
        "band": band.astype(np.float32).astype(bf),
        "maskr": maskr.reshape(128, 1024),
    }


import os
KLEVEL = int(os.environ.get("KLEVEL", "99"))
KSUB = int(os.environ.get("KSUB", "99"))


class _Stop(Exception):
    pass


def build_program():
    nc = bass.Bass("TRN2", target_bir_lowering=False)
    P = Prog(nc)

    def din(name, shape, dt=F32):
        return nc.dram_tensor(name, list(shape), dt, kind="ExternalInput")

    x_d = din("x", [NTILE * 128, D])
    c_d = din("c3", [3, D])
    wada_d = din("w_ada", [D, 3 * D])
    bada_d = din("b_ada", [1, 3 * D])
    ng_d = din("norm_g", [16, 128])
    win_d = din("w_in", [D, 10240])
    pw_d = din("pool_w", [4, 256, 256])
    psc_d = din("pool_scale", [8, 128])
    qg_d = din("q_norm_g", [1, 64])
    kg_d = din("k_norm_g", [1, 64])
    rpb_d = din("rpb", [16, 15, 31])
    wpu_d = din("w_pool_up", [1024, D])
    wau_d = din("w_attn_up", [1024, D])
    wo_d = din("w_o", [D, D])
    identbf_d = din("ident_bf", [128, 128], BF16)
    identf_d = din("ident_f", [128, 128])
    band_d = din("band", [128, 20, 128], BF16)
    maskr_d = din("maskr", [128, 1024])
    y_d = nc.dram_tensor("y", [NTILE * 128, D], F32, kind="ExternalOutput")
    wsc_d = nc.dram_tensor("wsc", [NGRP, 128, 8192], BF16, kind="Internal")
    ada_dd = nc.dram_tensor("ada_s", [3, 3 * D], F32, kind="Internal")
    rp2_d = nc.dram_tensor("rp2", [16, 15, 160], F32, kind="Internal")
    rtab_d = nc.dram_tensor("rtab", [8, 128, 2048], BF16, kind="Internal")

    with ExitStack() as st:
        def sb(name, shape, dt):
            return st.enter_context(nc.sbuf_tensor("sb_" + name, list(shape), dt))

        hT = sb("hT", [128, 2, 16, 256], BF16)
        kT = sb("kT", [128, 3, 8, 256], BF16)
        vv = sb("vv", [128, 3, 2, 1024], BF16)
        uu = sb("uu", [128, 3, 2, 1024], BF16)
        wst = sb("wst", [128, 3, 16, 512], BF16)
        xt = sb("xt", [128, 2, 2048], F32)
        xnb = sb("xnb", [128, 2048], BF16)
        qT = sb("qT", [128, 2, 4, 256], BF16)
        sazT = sb("sazT", [128, 2, 4, 256], BF16)
        rt = sb("rt", [128, 2, 2, 16, 64], BF16)
        attn_y = sb("attn_y", [128, 8, 256], BF16)
        pool_y = sb("pool_y", [128, 8, 256], BF16)
        mT = sb("mT", [128, 16, 256], BF16)
        dT = sb("dT", [128, 8, 256], BF16)
        spzT = sb("spzT", [128, 8, 256], BF16)
        fa = sb("fa", [128, 3, 512], F32)
        fb = sb("fb", [128, 2, 512], F32)
        bfs = sb("bfs", [128, 3, 512], BF16)
        eb = sb("eb", [128, 2, 512], BF16)
        exb = sb("exb", [64, 2, 64], BF16)
        pt = sb("pt", [128, 6, 512], BF16)
        ptx = sb("ptx", [64, 6, 64], BF16)
        rd = sb("rd", [128, 2, 128], F32)
        wgt = sb("wgt", [128, 2, 128], F32)
        xr = sb("xr", [128, 2, 512], F32)
        gq = sb("gq", [128, 2, 512], F32)
        band = sb("band", [128, 20, 128], BF16)
        pw_sb = sb("pw_sb", [128, 4, 2, 256], BF16)
        gqbc = sb("gqbc", [128, 512], F32)
        gkbc = sb("gkbc", [128, 512], F32)
        ident = sb("ident", [128, 128], BF16)
        identf = sb("identf", [128, 128], F32)
        ones = sb("ones", [128, 64], BF16)
        g64 = sb("g64", [128, 2, 64], F32)
        gsT = sb("gsT", [128, 3, 16], F32)
        shT = sb("shT", [128, 3, 16], F32)
        pscT = sb("pscT", [128, 8], F32)
        ngT = sb("ngT", [128, 16], F32)
        ssx = sb("ssx", [128, 4], F32)
        ss8 = sb("ss8", [128, 4, 8], F32)
        nhalf = sb("nhalf", [128, 8], F32)
        sT = sb("sT", [128, 48], F32)
        small48 = gq[0:48, 0, 0:384].rearrange("p (a n) -> p a n", n=128)
        small16 = gq[0:16, 1, 0:256].rearrange("p (a n) -> p a n", n=128)
        adab = fb[0:3, :, :]
        badab = fa[0:3, 2, :]

        psb = [st.enter_context(nc.psum_tensor("ps%d" % i, [128, 512], F32)) for i in range(8)]
        trps = [psb[2].bitcast(BF16), psb[3].bitcast(BF16)]
        PJ = [0, 1]
        S_BANK = [4, 5]
        OD_BANK = [6, 7]

        B_ps = [Buf("ps%d" % i) for i in range(8)]
        B_trp = [B_ps[2], B_ps[3]]
        B_hT = [[Buf("hT%d_%d" % (s, t)) for t in range(2)] for s in range(2)]
        B_kT = [[Buf("kT%d_%d" % (s, t)) for t in range(2)] for s in range(3)]
        B_vv = [[Buf("vv%d_%d" % (s, t)) for t in range(2)] for s in range(3)]
        B_uu = [[Buf("uu%d_%d" % (s, t)) for t in range(2)] for s in range(3)]
        B_wst = [Buf("wst%d" % i) for i in range(3)]
        B_xt = [Buf("xt0"), Buf("xt1")]
        B_xnb = Buf("xnb")
        B_qT = [Buf("qT0"), Buf("qT1")]
        B_saz = [Buf("saz0"), Buf("saz1")]
        B_rt = [Buf("rt0"), Buf("rt1")]
        B_ay = [[Buf("ay%d_%d" % (c, t)) for t in range(2)] for c in range(8)]
        B_py = Buf("py")
        B_mT = [Buf("mT0"), Buf("mT1")]
        B_dT = Buf("dT")
        B_spz = Buf("spz")
        B_fa = [Buf("fa%d" % i) for i in range(3)]
        B_fb = [Buf("fb%d" % i) for i in range(2)]
        B_bfs = [Buf("bfs%d" % i) for i in range(3)]
        B_eb = [Buf("eb0"), Buf("eb1")]
        B_exb = [Buf("exb0"), Buf("exb1")]
        B_pt = [Buf("pt%d" % i) for i in range(6)]
        B_ptx = [Buf("ptx%d" % i) for i in range(6)]
        B_rd = [Buf("rd0"), Buf("rd1")]
        B_wgt = [Buf("wgt0"), Buf("wgt1")]
        B_xr = [Buf("xr0"), Buf("xr1")]
        B_gq = [Buf("gq0"), Buf("gq1")]
        B_const = Buf("const")
        C_ident = Buf("c_ident"); C_identf = Buf("c_identf"); C_band = Buf("c_band"); C_g64 = Buf("c_g64")
        C_ones = Buf("c_ones"); C_nhalf = Buf("c_nhalf"); C_gq = Buf("c_gqbc"); C_gk = Buf("c_gkbc")
        C_pw = Buf("c_pw"); C_ng = Buf("c_ng"); C_psc = Buf("c_psc"); C_gs = Buf("c_gs"); C_sh = Buf("c_sh")
        C_sT = Buf("c_sT"); C_s48 = [Buf("c_s48_%d" % i) for i in range(3)]; C_s16 = [Buf("c_s16_0"), Buf("c_s16_1")]
        C_badab = Buf("c_badab"); C_adab = [Buf("c_adab0"), Buf("c_adab1")]
        B_yo = [Buf("yo0"), Buf("yo1")]
        B_ss = Buf("ss")
        B_ss8 = [Buf("ss8_%d" % i) for i in range(4)]
        B_wsc = [Buf("wsc%d" % g) for g in range(NGRP)]
        B_ada = Buf("ada")
        B_adad = Buf("adad")
        B_rp2 = Buf("rp2")
        B_rtab = Buf("rtab")
        B_y = Buf("y")
        B_misc = Buf("misc")

        ctr = {"fa": 0, "fb": 0, "bfs": 0, "pj": 0, "trp": 0, "eb": 0, "pt": 0, "rd": 0,
               "xr": 0, "gq": 0, "S": 0, "OD": 0, "wst": 0, "ss8": 0, "xt": 0, "rt": 0}

        def rot(name, n):
            i = ctr[name] % n
            ctr[name] += 1
            return i

        S_wst = [P.new_sem("wst%d" % i) for i in range(3)]
        S_xt = [P.new_sem("xt%d" % i) for i in range(2)]
        S_xr = [P.new_sem("xr%d" % i) for i in range(2)]
        S_gq = [P.new_sem("gq%d" % i) for i in range(2)]
        S_rt = [P.new_sem("rt%d" % i) for i in range(2)]
        S_yo = [P.new_sem("yo0"), P.new_sem("yo1")]

        def MM(out, lhsT, rhs, start, stop, r, w, tp=None):
            if tp is None:
                P.op("pe", lambda e: e.matmul(out, lhsT=lhsT, rhs=rhs, start=start, stop=stop), r, w)
            else:
                P.op("pe", lambda e: e.matmul(out, lhsT=lhsT, rhs=rhs, start=start, stop=stop,
                                              tile_position=tp), r, w)

        def TR(out, in_, idn, r, w):
            P.op("pe", lambda e: e.transpose(out, in_, idn), r, w)

        def ACT(out, in_, func, r, w, scale=1.0, bias=None, accum=None):
            kw = {}
            if bias is not None:
                kw["bias"] = bias
            if accum is not None:
                kw["accum_out"] = accum
            P.op("act", lambda e: e.activation(out=out, in_=in_, func=func, scale=scale, **kw), r, w)

        def TS(eng, out, in0, s1, s2, op0, op1, r, w):
            if s2 is None:
                P.op(eng, lambda e: e.tensor_scalar(out=out, in0=in0, scalar1=s1, scalar2=None, op0=op0), r, w)
            else:
                P.op(eng, lambda e: e.tensor_scalar(out=out, in0=in0, scalar1=s1, scalar2=s2, op0=op0, op1=op1), r, w)

        def TT(eng, out, in0, in1, op, r, w):
            P.op(eng, lambda e: e.tensor_tensor(out=out, in0=in0, in1=in1, op=op), r, w)

        def STT(eng, out, in0, scalar, in1, op0, op1, r, w):
            P.op(eng, lambda e: e.scalar_tensor_tensor(out=out, in0=in0, scalar=scalar, in1=in1, op0=op0, op1=op1), r, w)

        def CP(eng, out, in_, r, w):
            P.op(eng, lambda e: e.tensor_copy(out=out, in_=in_), r, w)

        def MSET(eng, ap, val, r, w):
            P.op(eng, lambda e: e.memset(ap, val), r, w)

        nsem = [0]

        def fresh():
            nsem[0] += 1
            return P.new_sem("one%d" % nsem[0])

        def DMAn(q, out, in_, r, w, sem=None):
            if sem is None:
                sem = fresh()
            return P.dma(q, lambda e: e.dma_start(out=out, in_=in_), sem, r, w)

        def DMAG(items):
            sem = fresh()
            ev = None
            ws = []
            for (q, out, in_, r, w) in items:
                ev = DMAn(q, out, in_, r, w, sem=sem)
                ws.extend(w)
            for b in ws:
                b.w = ev

        def DMA(q, out, in_, sem, r, w, slow=False):
            if slow:
                P.dma(q, lambda e: e.dma_start(out=out, in_=in_, allow_slow_non_contiguous=True), sem, r, w)
            else:
                P.dma(q, lambda e: e.dma_start(out=out, in_=in_), sem, r, w)

        def body():
            cast_gate = []

            def cast_group(g, src_ap_2d, kchunks, dst_k0, sem):
                src = src_ap_2d.rearrange("(kc p) n -> p kc n", p=128)
                dst = wsc_d.ap()[g].rearrange("p (kc n) -> p kc n", n=512)
                step = 8
                ev = None
                for k0 in range(0, kchunks, step):
                    ev = DMAn("pool", dst[:, dst_k0 + k0:dst_k0 + k0 + step, :], src[:, k0:k0 + step, :], list(cast_gate), [Buf("dmy")], sem=sem)
                return ev

            def win_cols(g):
                return win_d.ap()[:, g * 512:(g + 1) * 512]

            def cast_phase(groups):
                sem = fresh()
                ev = None
                for g in groups:
                    ev = cast_group(g, win_cols(g), 16, 0, sem)
                for g in groups:
                    B_wsc[g].w = ev

            cast_phase([G_K, G_K + 1])

            def late_casts():
                cast_phase([G_V, G_V + 1, G_U, G_U + 1])
                cast_phase([G_Q, G_AZ])
                cast_phase([G_Q + 1, G_AZ + 1])
                cast_phase([G_PZ, G_PZ + 1])
                for mg in range(4):
                    sem = fresh()
                    cast_group(G_GP + mg, win_cols(G_GP + mg), 16, 0, sem)
                    cast_group(G_UP + mg, wpu_d.ap()[:, mg * 512:(mg + 1) * 512], 8, 0, sem)
                    cast_group(G_UP + mg, wau_d.ap()[:, mg * 512:(mg + 1) * 512], 8, 8, sem)
                    ev = cast_group(G_GA + mg, win_cols(G_GA + mg), 16, 0, sem)
                    for g in (G_GP + mg, G_UP + mg, G_GA + mg):
                        B_wsc[g].w = ev
                sem = fresh()
                ev = None
                for ngp in range(4):
                    ev = cast_group(G_WO + ngp, wo_d.ap()[:, ngp * 512:(ngp + 1) * 512], 16, 0, sem)
                for ngp in range(4):
                    B_wsc[G_WO + ngp].w = ev
            DMAn("pool", pw_sb[:], pw_d.ap().rearrange("g (ic p) n -> p g ic n", p=128), [], [C_pw])

            if KLEVEL < 1:
                raise _Stop
            DMAG([("sp", ident[:], identbf_d.ap(), [], [C_ident]),
                  ("sp", identf[:], identf_d.ap(), [], [C_identf]),
                  ("sp", band[:], band_d.ap(), [], [C_band]),
                  ("sp", g64[:, 0, :], bass.AP(qg_d, 0, [[0, 128], [1, 64]]), [], [C_g64]),
                  ("sp", g64[:, 1, :], bass.AP(kg_d, 0, [[0, 128], [1, 64]]), [], [C_g64])])
            MSET("dve", ones[:], 1.0, [], [C_ones])
            MSET("dve", nhalf[:], -0.5, [], [C_nhalf])
            a0 = g64[:, 0, :]
            a1 = g64[:, 1, :]
            TS("dve", gqbc[:].rearrange("p (h d) -> p h d", d=64),
               bass.AP(a0.tensor, a0.offset, [list(a0.ap[0]), [0, 8], [1, 64]]), 1.0, None, ALU.mult, None, [C_g64], [C_gq])
            TS("dve", gkbc[:].rearrange("p (h d) -> p h d", d=64),
               bass.AP(a1.tensor, a1.offset, [list(a1.ap[0]), [0, 8], [1, 64]]), 8.0, None, ALU.mult, None, [C_g64], [C_gk])

            if KLEVEL < 2:
                raise _Stop
            DMAG([("sp", small48[:, 0, :], c_d.ap().rearrange("b (kc p) -> (b kc) p", p=128), [], [C_s48[0]]),
                  ("sp", small16[:, 0, :], ng_d.ap(), [], [C_s16[0]]),
                  ("sp", small16[0:8, 1, :], psc_d.ap(), [], [C_s16[1]])])
            pT = psb[0]
            TR(pT[:, 0:48], small48[:, 0, :], identf[0:48, 0:48], [C_s48[0], C_identf], [B_ps[0]])
            TR(pT[:, 48:64], small16[:, 0, :], identf[0:16, 0:16], [C_s16[0], C_identf], [B_ps[0]])
            TR(pT[:, 64:72], small16[0:8, 1, :], identf[0:8, 0:8], [C_s16[1], C_identf], [B_ps[0]])
            cTt = fa[:, 0, 0:48]
            CP("dve", cTt, pT[:, 0:48], [B_ps[0]], [B_fa[0]])
            CP("dve", ngT[:], pT[:, 48:64], [B_ps[0]], [C_ng])
            CP("dve", pscT[:], pT[:, 64:72], [B_ps[0]], [C_psc])
            tht = fa[:, 1, 0:48]
            ACT(tht, cTt, AF.Tanh, [B_fa[0]], [B_fa[1]], scale=0.5)
            STT("dve", sT[:], tht, 1.0, cTt, ALU.add, ALU.mult, [B_fa[0], B_fa[1]], [C_sT])

            if KLEVEL < 3:
                raise _Stop
            wst_f = [wst[:, i].rearrange("p a n -> p (a n)").bitcast(F32) for i in range(3)]
            sT3 = sT[:].rearrange("p (b k) -> p b k", k=16)
            B_adad_p = []
            S_badab = fresh()
            S_adst = [fresh(), fresh()]
            for half in range(2):
                banks = [0, 1, 2, 4, 5, 6]
                for kc in range(16):
                    sl = rot("wst", 3)
                    DMA("sp", wst_f[sl][:, 0:3072], wada_d.ap()[kc * 128:(kc + 1) * 128, half * 3072:(half + 1) * 3072],
                        S_wst[sl], [], [B_wst[sl]])
                    for n in range(6):
                        MM(psb[banks[n]][0:3, :], sT3[:, :, kc], wst_f[sl][:, n * 512:(n + 1) * 512], kc == 0, kc == 15,
                           [C_sT, B_wst[sl]], [B_ps[banks[n]]])
                for n in range(6):
                    col = half * 3072 + n * 512
                    ab = n % 2
                    DMAn("sp", badab, bass.AP(bada_d, col, [[0, 3], [1, 512]]), [], [C_badab], sem=S_badab)
                    STT("dve", adab[:, ab, :], psb[banks[n]][0:3, :], 0.5, badab, ALU.mult, ALU.add,
                        [B_ps[banks[n]], C_badab], [C_adab[ab]])
                    bp = Buf("adad%d" % col)
                    B_adad_p.append(bp)
                    DMAn("sp", ada_dd.ap()[:, col:col + 512], adab[:, ab, :], [C_adab[ab]], [bp], sem=S_adst[ab])
            gate_b = Buf("castgate")
            gate_b.w = B_wst[(ctr["wst"] - 1) % 3].w
            cast_gate.append(gate_b)
            late_casts()
            del cast_gate[:]
            for bp in B_adad_p:
                sk = bp.w[0]
                bp.w = (sk, P.dma_sems[sk], "dma")
            DMAG([("sp", small48[b * 16:(b + 1) * 16, 1, :], bass.AP(ada_dd, b * 3 * D, [[128, 16], [1, 128]]), B_adad_p, [C_s48[1]])
                  for b in range(3)])
            DMAG([("sp", small48[b * 16:(b + 1) * 16, 2, :], bass.AP(ada_dd, b * 3 * D + D, [[128, 16], [1, 128]]), B_adad_p, [C_s48[2]])
                  for b in range(3)])
            TR(pT[:, 0:48], small48[:, 1, :], identf[0:48, 0:48], [C_s48[1], C_identf], [B_ps[0]])
            TR(pT[:, 48:96], small48[:, 2, :], identf[0:48, 0:48], [C_s48[2], C_identf], [B_ps[0]])
            CP("dve", shT[:].rearrange("p b k -> p (b k)"), pT[:, 0:48], [B_ps[0]], [C_sh])
            for b in range(3):
                STT("dve", gsT[:, b, :], pT[:, 48 + 16 * b:64 + 16 * b], 1.0, ngT[:], ALU.add, ALU.mult,
                    [B_ps[0], C_ng], [C_gs])
            B_adad = Buf("adad_all")
            for bp in B_adad_p:
                if bp.w is not None:
                    B_adad.r.append(bp.w)

            B_rtab_h = [Buf("rtab%d" % h) for h in range(16)]

            def build_rtable():
                if KLEVEL < 5:
                    raise _Stop
                xt_flat = xt[:].rearrange("p a n -> p (a n)")
                rpt = xt_flat[0:16, 0:2400].rearrange("p (a b) -> p a b", b=160)
                rtmp = xt_flat[0:16, 2400:2400 + 465].rearrange("p (a b) -> p a b", b=31)
                MSET("pool", rpt, 0.0, [], [B_xt[0], B_xt[1]])
                DMAn("sp", rtmp, rpb_d.ap(), [], [B_xt[0], B_xt[1]])
                CP("pool", rpt[:, :, 64:95], rtmp[:, ::-1, :], [B_xt[0], B_xt[1]], [B_xt[0], B_xt[1]])
                DMAn("sp", rp2_d.ap(), rpt, [B_xt[0], B_xt[1]], [B_rp2])
                mT_f = mT[:].rearrange("p a b -> p (a b)").bitcast(F32)
                ay_f = attn_y[:].rearrange("p a b -> p (a b)").bitcast(F32)
                py_f = pool_y[:].rearrange("p a b -> p (a b)").bitcast(F32)
                qT_f = qT[:].rearrange("p a b c -> p (a b c)").bitcast(F32)
                dT_fl = dT[:].rearrange("p a b -> p (a b)")
                sp_fl = spzT[:].rearrange("p a b -> p (a b)")
                maskt = qT_f[:, 0:1024]
                DMAn("sp", maskt, maskr_d.ap(), [], [B_qT[0], B_qT[1]])
                MSET("pool", mT_f[:, 0:2048], 0.0, [], [B_mT[0], B_mT[1]])
                S_rb = [fresh(), fresh()]
                B_est = [Buf("est0"), B_py]
                B_rb = [B_dT, B_spz]
                S_stg = [P.new_sem("stg0"), P.new_sem("stg1")]
                ests = [ay_f[:, 0:1024], py_f[:, 0:1024]]
                rbufs = [dT_fl[:, 0:1024], sp_fl[:, 0:1024]]
                for h in range(16):
                    s_ = h % 2
                    stg = mT_f[:, s_ * 1024:(s_ + 1) * 1024]
                    stg3 = stg.rearrange("p (i c) -> p i c", c=64)
                    src = bass.AP(rp2_d, h * 15 * 160 + 16, [[1, 64], [160, 15], [1, 64]])
                    DMA("sp", stg3[0:64, 0:15, :], src, S_stg[s_], [B_rp2], [B_mT[s_]])
                    dmy = Buf("dmy")
                    ev_ = P.dma("sp", (lambda o_, i_: (lambda e: e.dma_start(out=o_, in_=i_)))(stg3[64:128, 1:16, :], src),
                                S_stg[s_], [B_rp2], [dmy])
                    B_mT[s_].w = ev_
                    ACT(ests[s_], stg, AF.Exp, [B_mT[s_]], [B_est[s_]])
                    TT("dve", rbufs[s_], ests[s_], maskt, ALU.mult, [B_est[s_], B_qT[0], B_qT[1]], [B_rb[s_]])
                    DMAn("sp", rtab_d.ap()[h // 2][:, (h % 2) * 1024:(h % 2 + 1) * 1024], rbufs[s_], [B_rb[s_]], [B_rtab_h[h]], sem=S_rb[s_])
                for h in range(16):
                    B_rtab_h[h].w = (S_rb[h % 2], P.dma_sems[S_rb[h % 2]], "dma")
                for c_ in range(8):
                    for t_ in range(2):
                        B_ay[c_][t_].r.extend(B_est[0].r)
                        if B_est[0].w is not None:
                            B_ay[c_][t_].r.append(B_est[0].w)

            for cb in C_s48 + C_s16:
                for gb_ in B_gq:
                    gb_.r.extend(cb.r)
                    if cb.w is not None:
                        gb_.r.append(cb.w)
            def load_w(g):
                sl = rot("wst", 3)
                DMA("sp", wst[:, sl].rearrange("p a n -> p (a n)"), wsc_d.ap()[g], S_wst[sl], [B_wsc[g]], [B_wst[sl]])
                return sl

            def proj(hslot, lt, sl, nk=16, k0=0, lhs_fn=None, lhs_bufs=None):
                flush()
                gen[0] += 1
                bk = PJ[rot("pj", 2)]
                for kc in range(nk):
                    if lhs_fn is None:
                        lhsT = hT[:, hslot, kc, lt * 128:(lt + 1) * 128]
                        rb = [B_hT[hslot][lt]]
                    else:
                        lhsT = lhs_fn(kc)
                        rb = lhs_bufs
                    MM(psb[bk][:], lhsT, wst[:, sl, k0 + kc, :], kc == 0, kc == nk - 1, rb + [B_wst[sl]], [B_ps[bk]])
                return bk

            deferred = []
            gen = [0]

            def flush(all_=False):
                keep = []
                for (g_, f_) in deferred:
                    if all_ or g_ <= gen[0] - 2:
                        f_()
                    else:
                        keep.append((g_, f_))
                deferred[:] = keep

            def transpose4(src_bf, src_buf, dst_ap, dst_bufs, eng_i):
                deferred.append((gen[0], lambda: transpose4_now(src_bf, src_buf, dst_ap, dst_bufs, eng_i)))

            def transpose4_now(src_bf, src_buf, dst_ap, dst_bufs, eng_i):
                th = rot("trp", 2)
                trp = trps[th]
                for c in range(4):
                    TR(trp[:, c * 128:(c + 1) * 128], src_bf[:, c * 128:(c + 1) * 128], ident[:],
                       [src_buf, C_ident], [B_trp[th]])
                src = trp[:, 0:512].rearrange("p (c t) -> p c t", t=128)
                if eng_i % 2 == 0:
                    ACT(dst_ap, src, AF.Copy, [B_trp[th]], dst_bufs)
                else:
                    CP("dve", dst_ap, src, [B_trp[th]], dst_bufs)

            def headnorm(bk, gbc, gbuf, dst_ap, dst_bufs, eng_i, early=False):
                ia = rot("fa", 3)
                ib = rot("fb", 2)
                ic = rot("bfs", 3)
                i8 = rot("ss8", 4)
                ACT(fa[:, ia, :], psb[bk][:], AF.Copy, [B_ps[bk]], [B_fa[ia]])
                ACT(fb[:, ib, :], fa[:, ia, :], AF.Square, [B_fa[ia]], [B_fb[ib]])
                P.op("dve", lambda e: e.tensor_reduce(out=ss8[:, i8, :], in_=fb[:, ib, :].rearrange("p (h d) -> p h d", d=64),
                                                      axis=AX.X, op=ALU.add), [B_fb[ib]], [B_ss8[i8]])
                if early:
                    TS("dve", ss8[:, i8, :], ss8[:, i8, :], 64.0 * EPS, None, ALU.add, None, [B_ss8[i8]], [B_ss8[i8]])
                    ACT(ss8[:, i8, :], ss8[:, i8, :], AF.Ln, [B_ss8[i8]], [B_ss8[i8]])
                    ACT(ss8[:, i8, :], ss8[:, i8, :], AF.Exp, [B_ss8[i8]], [B_ss8[i8]], scale=-0.5)
                else:
                    TS("dve", ss8[:, i8, :], ss8[:, i8, :], 64.0 * EPS, None, ALU.add, None, [B_ss8[i8]], [B_ss8[i8]])
                    TT("pool", ss8[:, i8, :], ss8[:, i8, :], nhalf[:], ALU.pow, [B_ss8[i8], C_nhalf], [B_ss8[i8]])
                TT("dve", fb[:, ib, :].rearrange("p (h d) -> p h d", d=64),
                   fa[:, ia, :].rearrange("p (h d) -> p h d", d=64), bc_last(ss8[:, i8, :], 64), ALU.mult,
                   [B_fa[ia], B_ss8[i8]], [B_fb[ib]])
                TT("dve" if early else "pool", bfs[:, ic, :], fb[:, ib, :], gbc[:], ALU.mult, [B_fb[ib], gbuf], [B_bfs[ic]])
                transpose4(bfs[:, ic, :], B_bfs[ic], dst_ap, dst_bufs, eng_i)

            def silu2(bk, dst_ap, dst_bufs, eng_i):
                ia = rot("fa", 3)
                ic = rot("bfs", 3)
                ACT(fa[:, ia, :], psb[bk][:], AF.Tanh, [B_ps[bk]], [B_fa[ia]], scale=0.5)
                STT("dve", bfs[:, ic, :], fa[:, ia, :], 1.0, psb[bk][:], ALU.add, ALU.mult, [B_fa[ia], B_ps[bk]], [B_bfs[ic]])
                transpose4(bfs[:, ic, :], B_bfs[ic], dst_ap, dst_bufs, eng_i)

            def rslot_of(gt):
                return (gt // 2) % 3

            xt_of = {}

            def frontend_load(gb):
                for lt in range(2):
                    gt = gb * 2 + lt
                    xi = rot("xt", 2)
                    xt_of[gt] = xi
                    DMA("sp", xt[:, xi, :], x_d.ap()[gt * 128:(gt + 1) * 128, :], S_xt[xi], [], [B_xt[xi]])

            def fe_prep(gb, lt):
                gt = gb * 2 + lt
                xi = xt_of[gt]
                col = gt % 4
                MSET("dve", ssx[:, col:col + 1], 0.0, [], [B_ss])
                ACT(xnb[:], xt[:, xi, :], AF.Square, [B_xt[xi], B_ss], [B_xnb, B_ss], accum=ssx[:, col:col + 1])
                TS("dve", ssx[:, col:col + 1], ssx[:, col:col + 1], 1.0 / D, EPS, ALU.mult, ALU.add, [B_ss], [B_ss])
                if gb < 2:
                    ACT(ssx[:, col:col + 1], ssx[:, col:col + 1], AF.Ln, [B_ss], [B_ss])
                    ACT(ssx[:, col:col + 1], ssx[:, col:col + 1], AF.Exp, [B_ss], [B_ss], scale=-0.5)
                else:
                    TT("pool", ssx[:, col:col + 1], ssx[:, col:col + 1], nhalf[:, 0:1], ALU.pow, [B_ss, C_nhalf], [B_ss])
                ACT(xnb[:], xt[:, xi, :], AF.Copy, [B_xt[xi], B_ss], [B_xnb], scale=ssx[:, col:col + 1])

            def fe_round(gb, lt, q4):
                hs = gb % 2
                gt = gb * 2 + lt
                sq = seq_of_tile(gt)
                th = rot("trp", 2)
                for c in range(4):
                    kc = q4 * 4 + c
                    TR(trps[th][:, c * 128:(c + 1) * 128], xnb[:, kc * 128:(kc + 1) * 128], ident[:],
                       [B_xnb, C_ident], [B_trp[th]])
                for c in range(4):
                    kc = q4 * 4 + c
                    src = trps[th][:, c * 128:(c + 1) * 128]
                    dst = hT[:, hs, kc, lt * 128:(lt + 1) * 128]
                    if c % 2 == 0:
                        ACT(dst, src, AF.Identity, [B_trp[th], C_gs, C_sh], [B_hT[hs][lt]],
                            scale=gsT[:, sq, kc:kc + 1], bias=shT[:, sq, kc:kc + 1])
                    else:
                        TS("dve", dst, src, gsT[:, sq, kc:kc + 1], shT[:, sq, kc:kc + 1], ALU.mult, ALU.add,
                           [B_trp[th], C_gs, C_sh], [B_hT[hs][lt]])

            def frontend_tile(gb, lt):
                fe_prep(gb, lt)
                for q4 in range(4):
                    fe_round(gb, lt, q4)

            def ahead_proj(gb):
                hs = gb % 2
                rs = gb % 3
                for kg in range(2):
                    sl = load_w(G_K + kg)
                    for lt in range(2):
                        bk = proj(hs, lt, sl)
                        dst = kT[:, rs, 4 * kg:4 * kg + 4, lt * 128:(lt + 1) * 128]
                        headnorm(bk, gkbc, C_gk, dst, [B_kT[rs][lt]], lt, early=(gb < 2))
                for (gbase, ring, Bring) in ((G_V, vv, B_vv), (G_U, uu, B_uu)):
                    for g2 in range(2):
                        sl = load_w(gbase + g2)
                        for lt in range(2):
                            bk = proj(hs, lt, sl)
                            dst = ring[:, rs, lt, g2 * 512:(g2 + 1) * 512]
                            if lt == 0:
                                ACT(dst, psb[bk][:], AF.Copy, [B_ps[bk]], [Bring[rs][lt]])
                            else:
                                CP("dve", dst, psb[bk][:], [B_ps[bk]], [Bring[rs][lt]])

            S4 = [0, 1, 2, 3, 4, 5]

            def att_qk(lb, sq, hp, lt, ri):
                qb, c4 = hp // 4, hp % 4
                gt = lb * 2 + lt
                j = gt - SEQ_T0[sq]
                J = SEQ_J[sq]
                if j == 0:
                    olist, i0, interior = [3, 2, 1, 0], 1, False
                elif j == 1:
                    olist, i0, interior = [2, 1, 0, -1], 3, False
                elif j == J - 2:
                    olist, i0, interior = [1, 0, -1, -2], 5, False
                elif j == J - 1:
                    olist, i0, interior = [0, -1, -2, -3], 7, False
                else:
                    olist, i0, interior = [1, 0, -1, -2], 5, True
                odb = OD_BANK[rot("OD", 2)]
                OD = psb[odb]
                sbis = [S4[rot("S", 6)], S4[rot("S", 6)]]
                for a, o in enumerate(olist):
                    gk = gt + o
                    krs, klt = rslot_of(gk), gk % 2
                    for hh in range(2):
                        p0 = 64 * hh
                        MM(psb[sbis[hh]][:, a * 128:(a + 1) * 128], kT[p0:p0 + 64, krs, hp, klt * 128:(klt + 1) * 128],
                           qT[p0:p0 + 64, qb, c4, lt * 128:(lt + 1) * 128], True, True,
                           [B_kT[krs][klt], B_qT[qb]], [B_ps[sbis[hh]]], tp=(p0, 0))
                if interior:
                    gk2 = gt + 2
                    krs2, klt2 = rslot_of(gk2), gk2 % 2
                    for hh in range(2):
                        p0 = 64 * hh
                        MM(psb[sbis[hh]][0:64, 448:512], kT[p0:p0 + 64, krs2, hp, klt2 * 128:klt2 * 128 + 64],
                           qT[p0:p0 + 64, qb, c4, lt * 128 + 64:(lt + 1) * 128], True, True,
                           [B_kT[krs2][klt2], B_qT[qb]], [B_ps[sbis[hh]]], tp=(p0, 0))
                pts = []
                for hh in range(2):
                    h = 2 * hp + hh
                    p0 = 64 * hh
                    sbi = sbis[hh]
                    S = psb[sbi]
                    ei = rot("eb", 2)
                    ACT(eb[:, ei, :], S[:], AF.Exp, [B_ps[sbi]], [B_eb[ei]])
                    pi = rot("pt", 6)
                    TT("dve", pt[:, pi, :].rearrange("p (i c) -> p i c", c=64),
                       eb[:, ei, :].rearrange("p (i c) -> p i c", c=64),
                       rt[:, ri, hh, i0:i0 + 8, ::-1], ALU.mult, [B_eb[ei], B_rt[ri]], [B_pt[pi]])
                    if interior:
                        TT("dve", ptx[:, pi, :], eb[0:64, ei, 448:512], rt[0:64, ri, hh, 4, ::-1], ALU.mult,
                           [B_eb[ei], B_rt[ri]], [B_ptx[pi]])
                        MSET("dve", pt[0:64, pi, 448:512], 0.0, [], [B_pt[pi]])
                    pts.append((pi, hh, h, p0))
                return (gt, hp, lt, qb, c4, olist, interior, odb, pts)

            def att_pv(st_):
                (gt, hp, lt, qb, c4, olist, interior, odb, pts) = st_
                OD = psb[odb]
                n = len(olist)
                for a, o in enumerate(olist):
                    gk = gt + o
                    krs, klt = rslot_of(gk), gk % 2
                    last = (a == n - 1) and not interior
                    for (pi, hh, h, p0) in pts:
                        MM(OD[p0:p0 + 64, 0:128], vv[:, krs, klt, h * 64:(h + 1) * 64], pt[:, pi, a * 128:(a + 1) * 128],
                           a == 0, last, [B_vv[krs][klt], B_pt[pi]], [B_ps[odb]], tp=(0, p0))
                if interior:
                    gk = gt + 2
                    krs, klt = rslot_of(gk), gk % 2
                    for (pi, hh, h, p0) in pts:
                        MM(OD[p0:p0 + 64, 64:128], vv[0:64, krs, klt, h * 64:(h + 1) * 64], ptx[:, pi, :],
                           False, True, [B_vv[krs][klt], B_ptx[pi]], [B_ps[odb]], tp=(0, p0))
                for a, o in enumerate(olist):
                    last = (a == n - 1) and not interior
                    for (pi, hh, h, p0) in pts:
                        MM(OD[p0:p0 + 64, 128:256], ones[:, 0:64], pt[:, pi, a * 128:(a + 1) * 128],
                           a == 0, last, [C_ones, B_pt[pi]], [B_ps[odb]], tp=(0, p0))
                if interior:
                    for (pi, hh, h, p0) in pts:
                        MM(OD[p0:p0 + 64, 192:256], ones[0:64, 0:64], ptx[:, pi, :],
                           False, True, [C_ones, B_ptx[pi]], [B_ps[odb]], tp=(0, p0))
                di = rot("rd", 2)
                P.op("dve", lambda e: e.reciprocal(out=rd[:, di, :], in_=OD[:, 128:256]), [B_ps[odb]], [B_rd[di]])
                TT("dve", wgt[:, di, :], rd[:, di, :], sazT[:, qb, c4, lt * 128:(lt + 1) * 128], ALU.mult,
                   [B_rd[di], B_saz[qb]], [B_wgt[di]])
                TT("dve", attn_y[:, hp, lt * 128:(lt + 1) * 128], OD[:, 0:128], wgt[:, di, :], ALU.mult,
                   [B_ps[odb], B_wgt[di]], [B_ay[hp][lt]])

            def lagged(lb):
                hs = lb % 2
                sq = seq_of_tile(lb * 2)
                nb = lb + 2
                rt_next = []

                def rt_load(hp_):
                    ri_ = rot("rt", 2)
                    DMA("pool", rt[:, ri_].rearrange("p a i c -> p (a i c)"), rtab_d.ap()[hp_], S_rt[ri_],
                        [B_rtab_h[2 * hp_], B_rtab_h[2 * hp_ + 1]], [B_rt[ri_]])
                    rt_next.append(ri_)

                rt_load(0)
                for qg in range(2):
                    qb = qg
                    sl = load_w(G_Q + qg)
                    for lt in range(2):
                        bk = proj(hs, lt, sl)
                        headnorm(bk, gqbc, C_gq, qT[:, qb, :, lt * 128:(lt + 1) * 128], [B_qT[qb]], lt, early=(lb < 1))
                    sl = load_w(G_AZ + qg)
                    for lt in range(2):
                        bk = proj(hs, lt, sl)
                        silu2(bk, sazT[:, qb, :, lt * 128:(lt + 1) * 128], [B_saz[qb]], lt + 1)
                flush(True)
                pend = []
                for hp in range(8):
                    ri = rt_next.pop(0)
                    if hp + 1 < 8:
                        rt_load(hp + 1)
                    for lt in range(2):
                        pend.append(att_qk(lb, sq, hp, lt, ri))
                        if len(pend) > 2:
                            att_pv(pend.pop(0))
                while pend:
                    att_pv(pend.pop(0))
                for pg in range(2):
                    sl = load_w(G_PZ + pg)
                    for lt in range(2):
                        bk = proj(hs, lt, sl)
                        silu2(bk, spzT[:, 4 * pg:4 * pg + 4, lt * 128:(lt + 1) * 128], [B_spz], lt)
                flush(True)
                for ch in range(8):
                    g = ch // 2
                    bk = PJ[rot("pj", 2)]
                    for lt in range(2):
                        gt = lb * 2 + lt
                        j = gt - SEQ_T0[sq]
                        J = SEQ_J[sq]
                        srcs = []
                        if j > 0:
                            srcs.append((gt - 1, g * 5 + 0))
                        srcs.append((gt, g * 5 + (3 if j == 0 else (4 if j == J - 1 else 1))))
                        if j < J - 1:
                            srcs.append((gt + 1, g * 5 + 2))
                        for si, (sg, bi) in enumerate(srcs):
                            srs, slt = rslot_of(sg), sg % 2
                            MM(psb[bk][:, lt * 128:(lt + 1) * 128], uu[:, srs, slt, ch * 128:(ch + 1) * 128], band[:, bi, :],
                               si == 0, si == len(srcs) - 1, [B_uu[srs][slt], C_band], [B_ps[bk]])
                    if ch % 2 == 0:
                        ACT(dT[:, ch, :], psb[bk][:, 0:256], AF.Copy, [B_ps[bk]], [B_dT])
                    else:
                        CP("dve", dT[:, ch, :], psb[bk][:, 0:256], [B_ps[bk]], [B_dT])
                for oc in range(8):
                    g = oc // 2
                    bk = PJ[rot("pj", 2)]
                    for ic in range(2):
                        MM(psb[bk][:, 0:256], pw_sb[:, g, ic, (oc % 2) * 128:(oc % 2 + 1) * 128], dT[:, 2 * g + ic, :],
                           ic == 0, ic == 1, [B_dT, C_pw], [B_ps[bk]])
                    STT("dve", pool_y[:, oc, :], psb[bk][:, 0:256], pscT[:, oc:oc + 1], spzT[:, oc, :], ALU.mult, ALU.mult,
                        [B_ps[bk], C_psc, B_spz], [B_py])
                if nb < NBLK:
                    frontend_load(nb)
                ayb = [[B_ay[c][lt] for c in range(8)] for lt in range(2)]
                for mg in range(4):
                    slp = load_w(G_GP + mg)
                    tg = []
                    for lt in range(2):
                        bgp = proj(hs, lt, slp)
                        ia = rot("fa", 3)
                        ACT(fa[:, ia, :], psb[bgp][:], AF.Tanh, [B_ps[bgp]], [B_fa[ia]], scale=0.5)
                        tg.append(ia)
                    slu = load_w(G_UP + mg)
                    t1 = []
                    for lt in range(2):
                        bpu = proj(hs, lt, slu, nk=8, k0=0,
                                   lhs_fn=lambda kc: pool_y[:, kc, lt * 128:(lt + 1) * 128], lhs_bufs=[B_py])
                        ib = rot("fb", 2)
                        STT("dve", fb[:, ib, :], fa[:, tg[lt], :], 1.0, psb[bpu][:], ALU.add, ALU.mult,
                            [B_fa[tg[lt]], B_ps[bpu]], [B_fb[ib]])
                        t1.append(ib)
                    sla = load_w(G_GA + mg)
                    tg2 = []
                    for lt in range(2):
                        bga = proj(hs, lt, sla)
                        ia2 = rot("fa", 3)
                        ACT(fa[:, ia2, :], psb[bga][:], AF.Tanh, [B_ps[bga]], [B_fa[ia2]], scale=0.5)
                        tg2.append(ia2)
                    for lt in range(2):
                        bau = proj(hs, lt, slu, nk=8, k0=8,
                                   lhs_fn=lambda kc: attn_y[:, kc, lt * 128:(lt + 1) * 128], lhs_bufs=ayb[lt])
                        ia2 = tg2[lt]
                        STT("dve", fa[:, ia2, :], fa[:, ia2, :], 1.0, psb[bau][:], ALU.add, ALU.mult,
                            [B_fa[ia2], B_ps[bau]], [B_fa[ia2]])
                        ic = rot("bfs", 3)
                        TT("pool", bfs[:, ic, :], fb[:, t1[lt], :], fa[:, ia2, :], ALU.add, [B_fb[t1[lt]], B_fa[ia2]], [B_bfs[ic]])
                        transpose4(bfs[:, ic, :], B_bfs[ic], mT[:, 4 * mg:4 * mg + 4, lt * 128:(lt + 1) * 128], [B_mT[lt]], lt)
                flush(True)
                fe_items = []
                if nb < NBLK:
                    fe_prep(nb, 0)
                    fe_items = [[lambda: fe_round(nb, 0, 0)], [lambda: fe_round(nb, 0, 1)], [lambda: fe_round(nb, 0, 2)],
                                [lambda: fe_round(nb, 0, 3), lambda: fe_prep(nb, 1)], [],
                                [lambda: fe_round(nb, 1, 0)], [lambda: fe_round(nb, 1, 1)],
                                [lambda: fe_round(nb, 1, 2), lambda: fe_round(nb, 1, 3)]]
                for ngp in range(4):
                    sl = load_w(G_WO + ngp)
                    gi = rot("gq", 2)
                    DMA("sp", gq[:, gi, :], bass.AP(ada_dd, sq * 3 * D + 2 * D + ngp * 512, [[0, 128], [1, 512]]),
                        S_gq[gi], [B_adad], [B_gq[gi]])
                    for lt in range(2):
                        gt = lb * 2 + lt
                        xi = rot("xr", 2)
                        DMA("sp", xr[:, xi, :], x_d.ap()[gt * 128:(gt + 1) * 128, ngp * 512:(ngp + 1) * 512], S_xr[xi],
                            [], [B_xr[xi]])
                        bk = proj(hs, lt, sl, lhs_fn=lambda kc: mT[:, kc, lt * 128:(lt + 1) * 128], lhs_bufs=[B_mT[lt]])
                        ia = rot("fa", 3)
                        STT("dve", fa[:, ia, :], psb[bk][:], 0.25, gq[:, gi, :], ALU.mult, ALU.mult,
                            [B_ps[bk], B_gq[gi]], [B_fa[ia]])
                        TT("pool", xr[:, xi, :], fa[:, ia, :], xr[:, xi, :], ALU.add, [B_fa[ia], B_xr[xi]], [B_xr[xi]])
                        DMA("pool", y_d.ap()[gt * 128:(gt + 1) * 128, ngp * 512:(ngp + 1) * 512], xr[:, xi, :], S_yo[xi],
                            [B_xr[xi]], [B_yo[xi]])
                        if fe_items:
                            for f_ in fe_items.pop(0):
                                f_()

            if KLEVEL < 6:
                raise _Stop
            frontend_load(0)
            frontend_tile(0, 0)
            frontend_tile(0, 1)
            for s in range(NBLK + 1):
                if s < NBLK:
                    if KLEVEL < 10 + 2 * s:
                        raise _Stop
                    ahead_proj(s)
                if s == 0:
                    frontend_load(1)
                    frontend_tile(1, 0)
                    frontend_tile(1, 1)
                    build_rtable()
                flush(True)
                if s >= 1:
                    if KLEVEL < 10 + 2 * s + 1:
                        raise _Stop
                    lagged(s - 1)
        try:
            body()
        except _Stop:
            pass
        P.finish("sp")
        P.finish("pool")
        P.emit()
    return nc


_CACHE = {}


def kernel(x_prompt, x_sample, c_prompt, c_sample, w_ada, b_ada, norm_g, w_in, pool_w, pool_scale,
           q_norm_g, k_norm_g, rpb, w_pool_up, w_attn_up, w_o):
    f32 = np.float32
    xp = np.asarray(x_prompt, f32)
    xs = np.asarray(x_sample, f32)
    cp = np.asarray(c_prompt, f32)
    cs = np.asarray(c_sample, f32)
    if "nc" not in _CACHE:
        _CACHE["nc"] = build_program()
        _CACHE["consts"] = host_consts()
    nc = _CACHE["nc"]
    consts = _CACHE["consts"]
    shared = {
        "w_ada": np.ascontiguousarray(np.asarray(w_ada, f32)[0]),
        "b_ada": np.ascontiguousarray(np.asarray(b_ada, f32)[0].reshape(1, 3 * D)),
        "norm_g": np.ascontiguousarray(np.asarray(norm_g, f32)[0].reshape(16, 128)),
        "w_in": np.ascontiguousarray(np.asarray(w_in, f32)[0]),
        "pool_w": np.ascontiguousarray(np.asarray(pool_w, f32)[0]),
        "pool_scale": np.ascontiguousarray(np.asarray(pool_scale, f32)[0].reshape(8, 128)),
        "q_norm_g": np.ascontiguousarray(np.asarray(q_norm_g, f32)[0].reshape(1, 64)),
        "k_norm_g": np.ascontiguousarray(np.asarray(k_norm_g, f32)[0].reshape(1, 64)),
        "rpb": np.ascontiguousarray(np.asarray(rpb, f32)[0]),
        "w_pool_up": np.ascontiguousarray(np.asarray(w_pool_up, f32)[0]),
        "w_attn_up": np.ascontiguousarray(np.asarray(w_attn_up, f32)[0]),
        "w_o": np.ascontiguousarray(np.asarray(w_o, f32)[0]),
    }
    shared.update(consts)
    in_maps = []
    for c in range(8):
        xc = np.concatenate([xp[2 * c], xp[2 * c + 1], xs[c]], axis=0)
        cc = np.stack([cp[2 * c], cp[2 * c + 1], cs[c]], axis=0)
        m = dict(shared)
        m["x"] = np.ascontiguousarray(xc)
        m["c3"] = np.ascontiguousarray(cc)
        in_maps.append(m)
    res = run_bass_kernel_spmd(nc, in_maps, core_ids=list(range(8)))
    y_prompt = np.empty((16, 2048, D), f32)
    y_sample = np.empty((8, 4096, D), f32)
    for c in range(8):
        y = np.asarray(res.results[c]["y"], f32)
        y_prompt[2 * c] = y[0:2048]
        y_prompt[2 * c + 1] = y[2048:4096]
        y_sample[c] = y[4096:8192]
    return (y_prompt, y_sample)
```
